# Optimizing a Trainium2 kernel written in Bass

```python
import math
import jax
import jax.numpy as jnp
from jax import lax
import numpy as np

D_MODEL = 1024
BATCH = 8
SEQ = 8192
DEPTH = 2

ROPE_THETA = 10000.0
NORM_EPS = 1e-6
Q_BLOCK = 128
NEG_INF = -1e30
N_BRANCHES = 3

DIFF_HEADS = 4
DIFF_HEAD_DIM = 64
DIFF_V_DIM = 2 * DIFF_HEAD_DIM
DIFF_WIDTH = DIFF_HEADS * DIFF_V_DIM

DIL_PATTERNS = ((128, 1), (512, 4), (2048, 16))
DIL_HEADS = 4
DIL_HEAD_DIM = 64
DIL_WIDTH = DIL_HEADS * DIL_HEAD_DIM

MLA_HEADS = 8
MLA_NOPE_DIM = 64
MLA_ROPE_DIM = 32
MLA_V_DIM = 64
MLA_Q_LORA = 768
MLA_KV_LORA = 256
MLA_WIDTH = MLA_HEADS * MLA_V_DIM

MLP_HIDDEN = 4 * D_MODEL

IN_COLS = (2 * DIFF_HEADS * 2 * DIFF_HEAD_DIM + DIFF_WIDTH
           + len(DIL_PATTERNS) * 3 * DIL_HEADS * DIL_HEAD_DIM
           + MLA_Q_LORA + MLA_KV_LORA + MLA_ROPE_DIM
           + N_BRANCHES * D_MODEL)

kernel_name = "hybrid_gated_diff_dilated_mla_encoder"


def _in_layout():
    segs = [("diff_q", DIFF_HEADS * 2 * DIFF_HEAD_DIM),
            ("diff_k", DIFF_HEADS * 2 * DIFF_HEAD_DIM),
            ("diff_v", DIFF_WIDTH)]
    for g in range(len(DIL_PATTERNS)):
        for t in ("q", "k", "v"):
            segs.append(("dil%d_%s" % (g, t), DIL_HEADS * DIL_HEAD_DIM))
    segs += [("mla_cq", MLA_Q_LORA), ("mla_ckv", MLA_KV_LORA), ("mla_kr", MLA_ROPE_DIM)]
    segs += [("gate%d" % i, D_MODEL) for i in range(N_BRANCHES)]
    layout, off = {}, 0
    for name, w in segs:
        layout[name] = (off, off + w)
        off += w
    return layout


def _rmsnorm(x, g):
    x32 = x.astype(jnp.float32)
    y = x32 * lax.rsqrt(jnp.mean(x32 * x32, axis=-1, keepdims=True) + NORM_EPS)
    return (y * g.astype(jnp.float32)).astype(x.dtype)


def _rope(x, pos):
    half = x.shape[-1] // 2
    inv = ROPE_THETA ** (-jnp.arange(half, dtype=jnp.float32) / half)
    ang = pos[:, None] * inv[None, :]
    cos = jnp.cos(ang)[None, :, None, :].astype(x.dtype)
    sin = jnp.sin(ang)[None, :, None, :].astype(x.dtype)
    x1, x2 = x[..., :half], x[..., half:]
    return jnp.concatenate([x1 * cos - x2 * sin, x2 * cos + x1 * sin], axis=-1)


def _to_qblocks(t):
    b, s = t.shape[:2]
    return t.reshape((b, s // Q_BLOCK, Q_BLOCK) + t.shape[2:]).swapaxes(0, 1)


def _from_qblocks(t):
    nb, b, qb = t.shape[:3]
    return t.swapaxes(0, 1).reshape((b, nb * qb) + t.shape[3:])


def _dense_attention(q, k, v, scale):
    def one(qb):
        s = jnp.einsum('bqhd,bkhd->bhqk', qb, k).astype(jnp.float32) * scale
        p = jax.nn.softmax(s, axis=-1).astype(v.dtype)
        return jnp.einsum('bhqk,bkhd->bqhd', p, v)
    return _from_qblocks(lax.map(one, _to_qblocks(q)))


def _differential_attention(q1, q2, k1, k2, v, lam):
    scale = DIFF_HEAD_DIM ** -0.5
    def one(qs):
        qb1, qb2 = qs
        p1 = jax.nn.softmax(jnp.einsum('bqhd,bkhd->bhqk', qb1, k1).astype(jnp.float32) * scale, axis=-1)
        p2 = jax.nn.softmax(jnp.einsum('bqhd,bkhd->bhqk', qb2, k2).astype(jnp.float32) * scale, axis=-1)
        a = (p1 - lam * p2).astype(v.dtype)
        return jnp.einsum('bhqk,bkhd->bqhd', a, v)
    return _from_qblocks(lax.map(one, (_to_qblocks(q1), _to_qblocks(q2))))


def _dilated_group(q, k, v, window, dilation):
    b, s_len, h, dh = q.shape
    span = window // (2 * dilation)
    blk = span
    n = -(-s_len // (dilation * blk)) * blk
    lp = n * dilation
    nb = n // blk
    pad = lp - s_len

    def blocks(t):
        t = jnp.pad(t, ((0, 0), (0, pad), (0, 0), (0, 0)))
        return t.reshape(b, nb, blk, dilation, h, dh)

    def windows(t):
        tp = jnp.pad(t, [(0, 0), (1, 1)] + [(0, 0)] * (t.ndim - 2))
        return jnp.concatenate([tp[:, :-2], tp[:, 1:-1], tp[:, 2:]], axis=2)

    qb = blocks(q)
    kw = windows(blocks(k))
    vw = windows(blocks(v))
    valid = (jnp.arange(lp) < s_len).reshape(1, nb, blk, dilation)
    valid_w = windows(valid)

    sc = jnp.einsum('bnqchd,bnkchd->bnchqk', qb, kw).astype(jnp.float32) * (dh ** -0.5)
    rel = jnp.arange(3 * blk)[None, :] - blk - jnp.arange(blk)[:, None]
    band = jnp.abs(rel) <= span
    mask = band[None, None, None, None] & valid_w.transpose(0, 1, 3, 2)[:, :, :, None, None, :]
    sc = jnp.where(mask, sc, NEG_INF)
    m = jnp.max(sc, axis=-1, keepdims=True)
    p = jnp.exp(sc - m)
    l = jnp.sum(p, axis=-1, keepdims=True)
    o = jnp.einsum('bnchqk,bnkchd->bnqchd', (p / l).astype(v.dtype), vw)
    lse = (m + jnp.log(l))[..., 0]
    o = o.reshape(b, lp, h, dh)[:, :s_len]
    lse = lse.transpose(0, 1, 4, 2, 3).reshape(b, lp, h)[:, :s_len]
    return o, lse


def _mixer(h, layer_idx, w_in, b_gate, diff_lambda, g_diff, g_cq, g_ckv, w_uq, w_ukv,
           w_o_diff, w_o_dil, w_o_mla, w_out):
    b, s_len, _ = h.shape
    pos = jnp.arange(s_len, dtype=jnp.float32)
    layout = _in_layout()

    def proj(name):
        a, e = layout[name]
        return h @ w_in[:, a:e]

    q = _rope(proj("diff_q").reshape(b, s_len, 2 * DIFF_HEADS, DIFF_HEAD_DIM), pos)
    k = _rope(proj("diff_k").reshape(b, s_len, 2 * DIFF_HEADS, DIFF_HEAD_DIM), pos)
    q = q.reshape(b, s_len, DIFF_HEADS, 2, DIFF_HEAD_DIM)
    k = k.reshape(b, s_len, DIFF_HEADS, 2, DIFF_HEAD_DIM)
    v = proj("diff_v").reshape(b, s_len, DIFF_HEADS, DIFF_V_DIM)
    lam_init = 0.8 - 0.6 * math.exp(-0.3 * layer_idx)
    lp32 = diff_lambda.astype(jnp.float32)
    lam = jnp.exp(jnp.sum(lp32[0] * lp32[1])) - jnp.exp(jnp.sum(lp32[2] * lp32[3])) + lam_init
    o_a = _differential_attention(q[..., 0, :], q[..., 1, :], k[..., 0, :], k[..., 1, :], v, lam)
    o_a = _rmsnorm(o_a, g_diff) * (1.0 - lam_init)
    y_a = o_a.reshape(b, s_len, DIFF_WIDTH) @ w_o_diff

    outs, lses = [], []
    for g, (window, dilation) in enumerate(DIL_PATTERNS):
        qg = _rope(proj("dil%d_q" % g).reshape(b, s_len, DIL_HEADS, DIL_HEAD_DIM), pos)
        kg = _rope(proj("dil%d_k" % g).reshape(b, s_len, DIL_HEADS, DIL_HEAD_DIM), pos)
        vg = proj("dil%d_v" % g).reshape(b, s_len, DIL_HEADS, DIL_HEAD_DIM)
        o_g, lse_g = _dilated_group(qg, kg, vg, window, dilation)
        outs.append(o_g)
        lses.append(lse_g)
    alpha = jax.nn.softmax(jnp.stack(lses, axis=0), axis=0)
    o_b = jnp.sum(alpha[..., None].astype(outs[0].dtype) * jnp.stack(outs, axis=0), axis=0)
    y_b = o_b.reshape(b, s_len, DIL_WIDTH) @ w_o_dil

    c_q = _rmsnorm(proj("mla_cq"), g_cq)
    c_kv = _rmsnorm(proj("mla_ckv"), g_ckv)
    k_rope = _rope(proj("mla_kr").reshape(b, s_len, 1, MLA_ROPE_DIM), pos)
    qh = (c_q @ w_uq).reshape(b, s_len, MLA_HEADS, MLA_NOPE_DIM + MLA_ROPE_DIM)
    q_c = jnp.concatenate([qh[..., :MLA_NOPE_DIM], _rope(qh[..., MLA_NOPE_DIM:], pos)], axis=-1)
    kv = (c_kv @ w_ukv).reshape(b, s_len, MLA_HEADS, MLA_NOPE_DIM + MLA_V_DIM)
    k_c = jnp.concatenate([kv[..., :MLA_NOPE_DIM],
                           jnp.broadcast_to(k_rope, (b, s_len, MLA_HEADS, MLA_ROPE_DIM))], axis=-1)
    v_c = kv[..., MLA_NOPE_DIM:]
    o_c = _dense_attention(q_c, k_c, v_c, (MLA_NOPE_DIM + MLA_ROPE_DIM) ** -0.5)
    y_c = o_c.reshape(b, s_len, MLA_WIDTH) @ w_o_mla

    merged = (jax.nn.sigmoid(proj("gate0") + b_gate[0]) * y_a
              + jax.nn.sigmoid(proj("gate1") + b_gate[1]) * y_b
              + jax.nn.sigmoid(proj("gate2") + b_gate[2]) * y_c)
    return merged @ w_out


def setup_inputs(seed: int = 0) -> dict:
    key = jax.random.key(seed)
    ks = jax.random.split(key, 20)

    def nrm(k, shape, fan_in):
        return jax.random.normal(k, shape, jnp.float32) * (fan_in ** -0.5)

    def gain(k, shape):
        return 1.0 + 0.05 * jax.random.normal(k, shape, jnp.float32)

    return {
        "x": jax.random.normal(ks[0], (BATCH, SEQ, D_MODEL), jnp.float32),
        "w_in": nrm(ks[1], (DEPTH, D_MODEL, IN_COLS), D_MODEL),
        "b_gate": 0.01 * jax.random.normal(ks[2], (DEPTH, N_BRANCHES, D_MODEL), jnp.float32),
        "g_mix": gain(ks[3], (DEPTH, D_MODEL)),
        "diff_lambda": 0.1 * jax.random.normal(ks[4], (DEPTH, 4, DIFF_HEAD_DIM), jnp.float32),
        "g_diff": gain(ks[5], (DEPTH, DIFF_V_DIM)),
        "g_cq": gain(ks[6], (DEPTH, MLA_Q_LORA)),
        "g_ckv": gain(ks[7], (DEPTH, MLA_KV_LORA)),
        "w_uq": nrm(ks[8], (DEPTH, MLA_Q_LORA, MLA_HEADS * (MLA_NOPE_DIM + MLA_ROPE_DIM)), MLA_Q_LORA),
        "w_ukv": nrm(ks[9], (DEPTH, MLA_KV_LORA, MLA_HEADS * (MLA_NOPE_DIM + MLA_V_DIM)), MLA_KV_LORA),
        "w_o_diff": nrm(ks[10], (DEPTH, DIFF_WIDTH, D_MODEL), DIFF_WIDTH),
        "w_o_dil": nrm(ks[11], (DEPTH, DIL_WIDTH, D_MODEL), DIL_WIDTH),
        "w_o_mla": nrm(ks[12], (DEPTH, MLA_WIDTH, D_MODEL), MLA_WIDTH),
        "w_out": nrm(ks[13], (DEPTH, D_MODEL, D_MODEL), D_MODEL),
        "g_mlp": gain(ks[14], (DEPTH, D_MODEL)),
        "w_up": nrm(ks[15], (DEPTH, D_MODEL, MLP_HIDDEN), D_MODEL),
        "w_down": nrm(ks[16], (DEPTH, MLP_HIDDEN, D_MODEL), MLP_HIDDEN),
        "g_final": gain(ks[17], (D_MODEL,)),
    }


def reference(x, w_in, b_gate, g_mix, diff_lambda, g_diff, g_cq, g_ckv, w_uq, w_ukv,
              w_o_diff, w_o_dil, w_o_mla, w_out, g_mlp, w_up, w_down, g_final):
    for l in range(DEPTH):
        h = _rmsnorm(x, g_mix[l])
        x = x + _mixer(h, l, w_in[l], b_gate[l], diff_lambda[l], g_diff[l], g_cq[l], g_ckv[l],
                       w_uq[l], w_ukv[l], w_o_diff[l], w_o_dil[l], w_o_mla[l], w_out[l])
        h = _rmsnorm(x, g_mlp[l])
        x = x + jnp.square(jax.nn.relu(h @ w_up[l])) @ w_down[l]
    return _rmsnorm(x, g_final)
```

```python
import numpy as np
from contextlib import ExitStack
import concourse.bass as bass
import concourse.mybir as mybir
from concourse.bass_utils import run_bass_kernel_spmd

F32 = mybir.dt.float32
BF16 = mybir.dt.bfloat16
ALU = mybir.AluOpType
AF = mybir.ActivationFunctionType

S = 8192
D = 1024
TB = 512
NTB = S // TB
DEPTH = 2
EPS = 1e-6
IN_COLS = 7968
C_DQ, C_DK, C_DV = 0, 1024, 2048
C_DIL = 2560
C_CQ, C_CKV, C_KR, C_GATE = 4864, 5632, 5888, 5920
ENGS = ('pe', 'act', 'dve', 'pool', 'sp')


class Buf:
    __slots__ = ('name', 'lw', 'rd', 'dkey', 't')

    def __init__(self, name, t=None):
        self.name = name
        self.lw = None
        self.rd = {}
        self.dkey = None
        self.t = t

    def __getitem__(self, idx):
        return self.t[idx]


class KB:
    def __init__(self, nc, es, n_dsem=40):
        self.nc = nc
        self.e = dict(pe=nc.tensor, act=nc.scalar, dve=nc.vector, pool=nc.gpsimd, sp=nc.sync)
        self.h = {}
        self.cnt = {}
        for k in ENGS:
            self.h[('e', k)] = es.enter_context(nc.semaphore('sem_' + k))
            self.cnt[('e', k)] = 0
        self.seen = {k: {} for k in ENGS}
        self.bar = es.enter_context(nc.semaphore('bar'))
        self.barn = 0
        self.dpool = []
        for i in range(n_dsem):
            key = ('d', i)
            self.h[key] = es.enter_context(nc.semaphore('dsem%d' % i))
            self.cnt[key] = 0
            self.dpool.append(key)
        self.dfree = list(self.dpool)
        self.bufs = []
        self.pes = None
        self.uid = 0

    def phase_begin(self):
        self.pes = ExitStack()
        self.bufs = []
        self.dfree = list(self.dpool)

    def phase_end(self):
        self.barrier()
        self.pes.close()
        self.pes = None

    def sb(self, name, shape, dt):
        self.uid += 1
        name = "%s_u%d" % (name, self.uid)
        t = self.pes.enter_context(self.nc.sbuf_tensor(name, list(shape), dt))
        b = Buf(name, t)
        self.bufs.append(b)
        return b

    def ps(self, name, shape=(128, 512), dt=F32):
        self.uid += 1
        name = "%s_u%d" % (name, self.uid)
        t = self.pes.enter_context(self.nc.psum_tensor(name, list(shape), dt))
        b = Buf(name, t)
        self.bufs.append(b)
        return b

    def _deps(self, rd, wr):
        deps = {}

        def add(k, v):
            if deps.get(k, 0) < v:
                deps[k] = v
        for b in rd:
            if b.lw is not None:
                add(*b.lw)
        for b in wr:
            if b.lw is not None:
                add(*b.lw)
            for k, v in b.rd.items():
                add(k, v)
        return deps

    def _need(self, eng, deps):
        e = self.e[eng]
        seen = self.seen[eng]
        for k, v in deps.items():
            if seen.get(k, 0) >= v:
                continue
            e.wait_ge(self.h[k], v)
            seen[k] = v

    def op(self, eng, fn, rd=(), wr=(), sig=True):
        deps = self._deps(rd, wr)
        key = ('e', eng)
        if eng == 'pe':
            deps.pop(key, None)
        self._need(eng, deps)
        inst = fn(self.e[eng])
        if sig:
            self.cnt[key] += 1
            inst.then_inc(self.h[key], 1)
            mark = self.cnt[key]
        else:
            mark = self.cnt[key] + 1
        for b in wr:
            b.lw = (key, mark)
            b.rd = {}
        for b in rd:
            if b in wr:
                continue
            if b.rd.get(key, 0) < mark:
                b.rd[key] = mark
        return inst

    def dma(self, q, out, in_, sem, rd=(), wr=(), slow=False):
        deps = self._deps(rd, wr)
        self._need(q, deps)
        if sem.dkey is None:
            sem.dkey = self.dfree.pop()
        key = sem.dkey
        if slow:
            inst = self.e[q].dma_start(out=out, in_=in_, allow_slow_non_contiguous=True)
        else:
            inst = self.e[q].dma_start(out=out, in_=in_)
        self.cnt[key] += 16
        inst.then_inc(self.h[key], 16)
        mark = self.cnt[key]
        for b in wr:
            b.lw = (key, mark)
            b.rd = {}
        for b in rd:
            if b in wr:
                continue
            b.rd[key] = mark
        return inst

    def barrier(self):
        sp = self.e['sp']
        seen = self.seen['sp']
        for k, v in self.cnt.items():
            if v > seen.get(k, 0):
                sp.wait_ge(self.h[k], v)
                seen[k] = v
        self.barn += 1
        sp.sem_inc(self.bar, 1)
        for k in ENGS:
            if k != 'sp':
                self.e[k].wait_ge(self.bar, self.barn)
                for kk, v in self.cnt.items():
                    self.seen[k][kk] = v
        for b in self.bufs:
            b.lw = None
            b.rd = {}
            b.dkey = None


def mm(k, out_buf, out_ap, lhsT_ap, rhs_ap, start, stop, rd):
    k.op('pe', lambda e: e.matmul(out_ap, lhsT_ap, rhs_ap, start=start, stop=stop),
         rd=rd, wr=[out_buf], sig=stop)


def norm_block(k, xb, KC, g_buf, gcol0, ones_buf, sq, msps, rstd, tmp, out_fn, eps_buf):
    k.op('act', lambda e: e.activation(out=sq[:, 0:KC, :], in_=xb[:, 0:KC, :], func=AF.Square),
         rd=[xb], wr=[sq])
    for c in range(KC):
        mm(k, msps, msps[:, :], ones_buf[:, :], sq[:, c, :], c == 0, c == KC - 1, [ones_buf, sq])
    k.op('act', lambda e: e.activation(out=tmp[:, :], in_=msps[:, :], func=AF.Sqrt, bias=eps_buf[:, 0:1], scale=1.0),
         rd=[msps, eps_buf], wr=[tmp])
    k.op('dve', lambda e: e.reciprocal(out=rstd[:, :], in_=tmp[:, :]), rd=[tmp], wr=[rstd])
    for c in range(KC):
        ob, oap = out_fn(c)
        k.op('dve', lambda e, c=c, oap=oap: e.scalar_tensor_tensor(
            out=oap, in0=xb[:, c, :], scalar=g_buf[:, gcol0 + c:gcol0 + c + 1], in1=rstd[:, :],
            op0=ALU.mult, op1=ALU.mult), rd=[xb, g_buf, rstd], wr=[ob])


C_DQ, C_DK, C_DV, C_DIL, C_CQ, C_CKV, C_KR, C_GATE = 0, 512, 1024, 1536, 3840, 4608, 4864, 4896
NRC = 20
LAM_INIT = [0.8 - 0.6 * float(np.exp(-0.3 * l)) for l in range(DEPTH)]


def build(nc, upto=None, dump=(), layers=(0, 1)):
    dt_in = lambda name, shape: nc.dram_tensor(name, list(shape), F32, kind="ExternalInput").ap()
    x = dt_in("x", (S, D))
    w_in = dt_in("w_in", (DEPTH, D, IN_COLS))
    b_gate = dt_in("b_gate", (DEPTH, 3, D))
    g_mix = dt_in("g_mix", (DEPTH, D))
    diff_lambda = dt_in("diff_lambda", (DEPTH, 4, 64))
    g_diff = dt_in("g_diff", (DEPTH, 128))
    g_cq = dt_in("g_cq", (DEPTH, 768))
    g_ckv = dt_in("g_ckv", (DEPTH, 256))
    w_uq = dt_in("w_uq", (DEPTH, 768, 768))
    w_ukv = dt_in("w_ukv", (DEPTH, 256, 1024))
    w_o_diff = dt_in("w_o_diff", (DEPTH, 512, D))
    w_o_dil = dt_in("w_o_dil", (DEPTH, 256, D))
    w_o_mla = dt_in("w_o_mla", (DEPTH, 512, D))
    w_out = dt_in("w_out", (DEPTH, D, D))
    g_mlp = dt_in("g_mlp", (DEPTH, D))
    w_up = dt_in("w_up", (DEPTH, D, 4 * D))
    w_down = dt_in("w_down", (DEPTH, 4 * D, D))
    g_final = dt_in("g_final", (D,))
    rc64 = dt_in("rc64", (128, S))
    rs64 = dt_in("rs64", (128, S))
    rcq = dt_in("rcq", (96, S))
    rsq = dt_in("rsq", (96, S))
    out = nc.dram_tensor("out", [S, D], F32, kind="ExternalOutput").ap()

    def scr(name, shape, dt=BF16):
        kind = "ExternalOutput" if name in dump else "Internal"
        return nc.dram_tensor(name, list(shape), dt, kind=kind).ap()

    xT = scr("xT", (8, 128, S), F32)
    hT = scr("hT", (8, 128, S))
    RP = scr("RP", (NRC, 128, S))
    KR = scr("KR", (32, S))
    VA = scr("VA", (S, 512))
    VB = scr("VB", (3, S, 256))
    CQ = scr("CQ", (6, 128, S))
    CKV = scr("CKV", (2, 128, S))
    G = scr("G", (24, 128, S))
    QC = scr("QC", (8, 96, S))
    KN = scr("KN", (4, 128, S))
    VC = scr("VC", (S, 512))
    OA = scr("OA", (4, 128, S))
    OB = scr("OB", (2, 128, S))
    OC = scr("OC", (4, 128, S))
    H2 = scr("H2", (8, 128, S))
    U = scr("U", (32, 128, S))
    xTv = xT.rearrange("c p s -> p c s")
    hTv = hT.rearrange("c p s -> p c s")

    es = ExitStack()
    k = KB(nc, es)
    stop = [False]

    def done(name):
        if upto == name:
            stop[0] = True
        return stop[0]

    def consts():
        onesf = k.sb("onesf", (128, 128), F32)
        epsb = k.sb("epsb", (128, 1), F32)
        k.op('pool', lambda e: e.memset(onesf[:, :], 1.0), wr=[onesf])
        k.op('pool', lambda e: e.memset(epsb[:, :], EPS), wr=[epsb])
        return onesf, epsb

    def ones_tile(name, val, dt=F32):
        t = k.sb(name, (128, 128), dt)
        k.op('pool', lambda e: e.memset(t[:, :], val), wr=[t])
        return t

    def gvec(name, src1d, n):
        t = k.sb(name, (128, n), F32)
        k.dma('sp', t[:, :], src1d.rearrange("(c p) -> p c", p=128), t, wr=[t], slow=True)
        return t

    def wload(W, c0, src, n):
        sv = src.rearrange("(kc p) n -> p kc n", p=128)
        for a in range(0, n, 2048):
            m = min(2048, n - a)
            k.dma('pool', W[:, :, c0 + a:c0 + a + m], sv[:, :, a:a + m], W, wr=[W])

    def ph_first():
        k.phase_begin()
        onesf, epsb = consts()
        ident = k.sb("ident", (128, 128), F32)
        onesD = ones_tile("onesD", 1.0 / D)
        gt = gvec("gt", g_mix[0], 8)
        xin = [k.sb("xin%d" % i, (128, 4, D), F32) for i in range(2)]
        xTb = [k.sb("xTb%d" % i, (128, 8, TB), F32) for i in range(2)]
        hTb = [k.sb("hTb%d" % i, (128, 8, TB), BF16) for i in range(2)]
        sq = k.sb("sq", (128, 8, TB), F32)
        rstd = k.sb("rstd", (128, TB), F32)
        tmp = k.sb("tmp", (128, TB), F32)
        tp = [k.ps("tp%d" % i) for i in range(4)]
        msps = k.ps("msps")
        k.op('pool', lambda e: e.affine_select(out=ident[:, :], in_=onesf[:, :], pattern=[[-1, 128]],
                                               compare_op=ALU.is_equal, fill=0.0, base=0, channel_multiplier=1),
             rd=[onesf], wr=[ident])
        xv = x.rearrange("(b j p) d -> b p j d", p=128, j=4)
        k.dma('sp', xin[0][:, :, :], xv[0], xin[0], wr=[xin[0]])
        for blk in range(NTB):
            xi = xin[blk % 2]
            xo = xTb[blk % 2]
            ho = hTb[blk % 2]
            if blk + 1 < NTB:
                nx = xin[(blk + 1) % 2]
                k.dma('sp', nx[:, :, :], xv[blk + 1], nx, wr=[nx])
            for fc in range(8):
                p = tp[fc % 4]
                for j in range(4):
                    k.op('pe', lambda e: e.transpose(p[:, j * 128:(j + 1) * 128], xi[:, j, fc * 128:(fc + 1) * 128],
                                                     ident[:, :]), rd=[xi, ident], wr=[p], sig=(j == 3))
                if fc % 2 == 0:
                    k.op('act', lambda e: e.copy(out=xo[:, fc, :], in_=p[:, :]), rd=[p], wr=[xo])
                else:
                    k.op('dve', lambda e: e.tensor_copy(out=xo[:, fc, :], in_=p[:, :]), rd=[p], wr=[xo])
            norm_block(k, xo, 8, gt, 0, onesD, sq, msps, rstd, tmp, lambda c: (ho, ho[:, c, :]), epsb)
            sl = slice(blk * TB, (blk + 1) * TB)
            k.dma('sp', xTv[:, :, sl], xo[:, :, :], xo, rd=[xo])
            k.dma('sp', hTv[:, :, sl], ho[:, :, :], ho, rd=[ho])
        k.phase_end()

    def ph_rope(l):
        k.phase_begin()
        NW = 2592
        W = k.sb("W", (128, 8, NW), BF16)
        Wsw = k.sb("Wsw", (128, 8, NW), BF16)
        wload(W, 0, w_in[l][:, 0:1024], 1024)
        for g in range(3):
            wload(W, 1024 + g * 512, w_in[l][:, C_DIL + g * 768:C_DIL + g * 768 + 512], 512)
        wload(W, 2560, w_in[l][:, C_KR:C_KR + 32], 32)
        for kc in range(8):
            src = W[:, kc, 0:2560].rearrange("p (h t d) -> p h t d", t=2, d=32)
            dst = Wsw[:, kc, 0:2560].rearrange("p (h t d) -> p h t d", t=2, d=32)
            eng = 'dve' if kc % 2 == 0 else 'pool'
            k.op(eng, lambda e: e.tensor_copy(out=dst[:, :, 0, :], in_=src[:, :, 1, :]), rd=[W], wr=[Wsw])
            k.op(eng, lambda e: e.tensor_copy(out=dst[:, :, 1, :], in_=src[:, :, 0, :]), rd=[W], wr=[Wsw])
        k.op('dve', lambda e: e.tensor_copy(out=Wsw[:, :, 2560:2576], in_=W[:, :, 2576:2592]), rd=[W], wr=[Wsw])
        k.op('dve', lambda e: e.tensor_copy(out=Wsw[:, :, 2576:2592], in_=W[:, :, 2560:2576]), rd=[W], wr=[Wsw])
        hb = [k.sb("hb%d" % i, (128, 8, TB), BF16) for i in range(2)]
        ct = [k.sb("ct%d" % i, (128, TB), F32) for i in range(2)]
        st = [k.sb("st%d" % i, (128, TB), F32) for i in range(2)]
        ck = [k.sb("ck%d" % i, (32, TB), F32) for i in range(2)]
        sk = [k.sb("sk%d" % i, (32, TB), F32) for i in range(2)]
        psA = [k.ps("pa%d" % i) for i in range(3)]
        psB = [k.ps("pb%d" % i) for i in range(3)]
        t1 = [k.sb("t1_%d" % i, (128, TB), F32) for i in range(2)]
        t2 = [k.sb("t2_%d" % i, (128, TB), F32) for i in range(2)]
        og = [k.sb("og%d" % i, (128, TB), BF16) for i in range(4)]

        def load_blk(b):
            sl = slice(b * TB, (b + 1) * TB)
            i = b % 2
            k.dma('sp', hb[i][:, :, :], hTv[:, :, sl], hb[i], wr=[hb[i]])
            k.dma('sp', ct[i][:, :], rc64[:, sl], ct[i], wr=[ct[i]])
            k.dma('sp', st[i][:, :], rs64[:, sl], st[i], wr=[st[i]])
            k.dma('sp', ck[i][:, :], rcq[64:96, sl], ck[i], wr=[ck[i]])
            k.dma('sp', sk[i][:, :], rsq[64:96, sl], sk[i], wr=[sk[i]])
        load_blk(0)
        n = 0
        for b in range(NTB):
            if b + 1 < NTB:
                load_blk(b + 1)
            i = b % 2
            sl = slice(b * TB, (b + 1) * TB)
            for ci in range(NRC + 1):
                M = 128 if ci < NRC else 32
                c0 = ci * 128
                pa, pb = psA[n % 3], psB[n % 3]
                for kc in range(8):
                    mm(k, pa, pa[0:M, :], W[:, kc, c0:c0 + M], hb[i][:, kc, :], kc == 0, kc == 7, [W, hb[i]])
                for kc in range(8):
                    mm(k, pb, pb[0:M, :], Wsw[:, kc, c0:c0 + M], hb[i][:, kc, :], kc == 0, kc == 7, [Wsw, hb[i]])
                a1, a2, o = t1[n % 2], t2[n % 2], og[n % 4]
                cc, ss = (ct[i], st[i]) if ci < NRC else (ck[i], sk[i])
                k.op('dve', lambda e: e.tensor_tensor(out=a1[0:M, :], in0=pa[0:M, :], in1=cc[0:M, :], op=ALU.mult),
                     rd=[pa, cc], wr=[a1])
                k.op('dve', lambda e: e.tensor_tensor(out=a2[0:M, :], in0=pb[0:M, :], in1=ss[0:M, :], op=ALU.mult),
                     rd=[pb, ss], wr=[a2])
                k.op('pool', lambda e: e.tensor_tensor(out=o[0:M, :], in0=a1[0:M, :], in1=a2[0:M, :], op=ALU.add),
                     rd=[a1, a2], wr=[o])
                dst = RP[ci][:, sl] if ci < NRC else KR[:, sl]
                k.dma('sp', dst, o[0:M, :], o, rd=[o])
                n += 1
        k.phase_end()

    def ph_vcg(l):
        k.phase_begin()
        onesf, epsb = consts()
        Wv = k.sb("Wv", (128, 8, 1280), BF16)
        Wc = k.sb("Wc", (128, 8, 1024), BF16)
        Wg = k.sb("Wg", (128, 8, 3072), BF16)
        wload(Wv, 0, w_in[l][:, C_DV:C_DV + 512], 512)
        for g in range(3):
            wload(Wv, 512 + g * 256, w_in[l][:, C_DIL + g * 768 + 512:C_DIL + g * 768 + 768], 256)
        wload(Wc, 0, w_in[l][:, C_CQ:C_CQ + 1024], 1024)
        wload(Wg, 0, w_in[l][:, C_GATE:C_GATE + 3072], 3072)
        bg = k.sb("bg", (128, 24), F32)
        for i in range(3):
            k.dma('sp', bg[:, i * 8:(i + 1) * 8], b_gate[l][i].rearrange("(c p) -> p c", p=128), bg, wr=[bg], slow=True)
        gcq = gvec("gcq", g_cq[l], 6)
        gckv = gvec("gckv", g_ckv[l], 2)
        o768 = ones_tile("o768", 1.0 / 768)
        o256 = ones_tile("o256", 1.0 / 256)
        hb = [k.sb("hb%d" % i, (128, 8, TB), BF16) for i in range(2)]
        vst = k.sb("vst", (128, 4, 1280), BF16)
        cqf = k.sb("cqf", (128, 6, TB), F32)
        sq = k.sb("sq", (128, 6, TB), F32)
        cqo = k.sb("cqo", (128, 6, TB), BF16)
        ckvf = k.sb("ckvf", (128, 2, TB), F32)
        ckvo = k.sb("ckvo", (128, 2, TB), BF16)
        gst = [k.sb("gst%d" % i, (128, 8, TB), BF16) for i in range(2)]
        rstd = k.sb("rstd", (128, TB), F32)
        tmp = k.sb("tmp", (128, TB), F32)
        pv = [k.ps("pv%d" % i) for i in range(2)]
        pc = [k.ps("pc%d" % i) for i in range(3)]
        msps = k.ps("msps")
        CQv = CQ.rearrange("c p s -> p c s")
        CKVv = CKV.rearrange("c p s -> p c s")
        Gv = G.rearrange("c p s -> p c s")

        def load_blk(b):
            sl = slice(b * TB, (b + 1) * TB)
            k.dma('sp', hb[b % 2][:, :, :], hTv[:, :, sl], hb[b % 2], wr=[hb[b % 2]])
        load_blk(0)
        nv = 0
        npc = 0
        for b in range(NTB):
            if b + 1 < NTB:
                load_blk(b + 1)
            h = hb[b % 2]
            sl = slice(b * TB, (b + 1) * TB)
            for j in range(4):
                for (c0, n) in ((0, 512), (512, 512), (1024, 256)):
                    p = pv[nv % 2]
                    nv += 1
                    for kc in range(8):
                        mm(k, p, p[:, 0:n], h[:, kc, j * 128:(j + 1) * 128], Wv[:, kc, c0:c0 + n], kc == 0, kc == 7, [h, Wv])
                    k.op('act', lambda e: e.copy(out=vst[:, j, c0:c0 + n], in_=p[:, 0:n]), rd=[p], wr=[vst])
            k.dma('sp', VA[sl, :].rearrange("(j p) c -> p j c", p=128), vst[:, :, 0:512], vst, rd=[vst])
            for g in range(3):
                k.dma('sp', VB[g][sl, :].rearrange("(j p) c -> p j c", p=128), vst[:, :, 512 + g * 256:768 + g * 256],
                      vst, rd=[vst])
            for c in range(8):
                p = pc[npc % 3]
                npc += 1
                for kc in range(8):
                    mm(k, p, p[:, :], Wc[:, kc, c * 128:(c + 1) * 128], h[:, kc, :], kc == 0, kc == 7, [Wc, h])
                if c < 6:
                    k.op('dve', lambda e: e.tensor_copy(out=cqf[:, c, :], in_=p[:, :]), rd=[p], wr=[cqf])
                else:
                    k.op('dve', lambda e: e.tensor_copy(out=ckvf[:, c - 6, :], in_=p[:, :]), rd=[p], wr=[ckvf])
            norm_block(k, cqf, 6, gcq, 0, o768, sq, msps, rstd, tmp, lambda c: (cqo, cqo[:, c, :]), epsb)
            k.dma('sp', CQv[:, :, sl], cqo[:, :, :], cqo, rd=[cqo])
            norm_block(k, ckvf, 2, gckv, 0, o256, sq, msps, rstd, tmp, lambda c: (ckvo, ckvo[:, c, :]), epsb)
            k.dma('sp', CKVv[:, :, sl], ckvo[:, :, :], ckvo, rd=[ckvo])
            for gi in range(24):
                p = pc[npc % 3]
                npc += 1
                gs = gst[(gi // 8) % 2]
                for kc in range(8):
                    mm(k, p, p[:, :], Wg[:, kc, gi * 128:(gi + 1) * 128], h[:, kc, :], kc == 0, kc == 7, [Wg, h])
                k.op('act', lambda e: e.activation(out=gs[:, gi % 8, :], in_=p[:, :], func=AF.Sigmoid,
                                                   bias=bg[:, gi:gi + 1], scale=1.0), rd=[p, bg], wr=[gs])
                if gi % 8 == 7:
                    k.dma('sp', Gv[:, gi - 7:gi + 1, sl], gs[:, :, :], gs, rd=[gs])
        k.phase_end()

    def ph_mla_up(l):
        k.phase_begin()
        Wq = k.sb("Wq", (128, 6, 768), BF16)
        Wqs = k.sb("Wqs", (128, 6, 768), BF16)
        Wkn = k.sb("Wkn", (128, 2, 512), BF16)
        Wvv = k.sb("Wvv", (128, 2, 512), BF16)
        wload(Wq, 0, w_uq[l], 768)
        ukv = w_ukv[l].rearrange("(kc p) (h t d) -> p kc h t d", p=128, t=2, d=64)
        for kc in range(2):
            k.dma('pool', Wkn[:, kc, :].rearrange("p (h d) -> p h d", d=64), ukv[:, kc, :, 0, :], Wkn, wr=[Wkn])
            k.dma('pool', Wvv[:, kc, :].rearrange("p (h d) -> p h d", d=64), ukv[:, kc, :, 1, :], Wvv, wr=[Wvv])
        for kc in range(6):
            src = Wq[:, kc, :].rearrange("p (h e) -> p h e", e=96)
            dst = Wqs[:, kc, :].rearrange("p (h e) -> p h e", e=96)
            eng = 'dve' if kc % 2 == 0 else 'pool'
            k.op(eng, lambda e: e.tensor_copy(out=dst[:, :, 0:64], in_=src[:, :, 0:64]), rd=[Wq], wr=[Wqs])
            k.op(eng, lambda e: e.tensor_copy(out=dst[:, :, 64:80], in_=src[:, :, 80:96]), rd=[Wq], wr=[Wqs])
            k.op(eng, lambda e: e.tensor_copy(out=dst[:, :, 80:96], in_=src[:, :, 64:80]), rd=[Wq], wr=[Wqs])
        cqb = [k.sb("cqb%d" % i, (128, 6, TB), BF16) for i in range(2)]
        ckb = [k.sb("ckb%d" % i, (128, 2, TB), BF16) for i in range(2)]
        ct = [k.sb("ct%d" % i, (96, TB), F32) for i in range(2)]
        st = [k.sb("st%d" % i, (96, TB), F32) for i in range(2)]
        psA = [k.ps("pa%d" % i) for i in range(3)]
        psB = [k.ps("pb%d" % i) for i in range(3)]
        t1 = [k.sb("t1_%d" % i, (96, TB), F32) for i in range(2)]
        t2 = [k.sb("t2_%d" % i, (96, TB), F32) for i in range(2)]
        og = [k.sb("og%d" % i, (128, TB), BF16) for i in range(4)]
        vst = k.sb("vst", (128, 4, 512), BF16)
        CQv = CQ.rearrange("c p s -> p c s")
        CKVv = CKV.rearrange("c p s -> p c s")

        def load_blk(b):
            sl = slice(b * TB, (b + 1) * TB)
            i = b % 2
            k.dma('sp', cqb[i][:, :, :], CQv[:, :, sl], cqb[i], wr=[cqb[i]])
            k.dma('sp', ckb[i][:, :, :], CKVv[:, :, sl], ckb[i], wr=[ckb[i]])
            k.dma('sp', ct[i][:, :], rcq[:, sl], ct[i], wr=[ct[i]])
            k.dma('sp', st[i][:, :], rsq[:, sl], st[i], wr=[st[i]])
        load_blk(0)
        n = 0
        no = 0
        for b in range(NTB):
            if b + 1 < NTB:
                load_blk(b + 1)
            i = b % 2
            sl = slice(b * TB, (b + 1) * TB)
            for h in range(8):
                pa, pb = psA[n % 3], psB[n % 3]
                for kc in range(6):
                    mm(k, pa, pa[0:96, :], Wq[:, kc, h * 96:(h + 1) * 96], cqb[i][:, kc, :], kc == 0, kc == 5, [Wq, cqb[i]])
                for kc in range(6):
                    mm(k, pb, pb[0:96, :], Wqs[:, kc, h * 96:(h + 1) * 96], cqb[i][:, kc, :], kc == 0, kc == 5, [Wqs, cqb[i]])
                a1, a2, o = t1[n % 2], t2[n % 2], og[no % 4]
                k.op('dve', lambda e: e.tensor_tensor(out=a1[:, :], in0=pa[0:96, :], in1=ct[i][:, :], op=ALU.mult),
                     rd=[pa, ct[i]], wr=[a1])
                k.op('dve', lambda e: e.tensor_tensor(out=a2[:, :], in0=pb[0:96, :], in1=st[i][:, :], op=ALU.mult),
                     rd=[pb, st[i]], wr=[a2])
                k.op('pool', lambda e: e.tensor_tensor(out=o[0:96, :], in0=a1[:, :], in1=a2[:, :], op=ALU.add),
                     rd=[a1, a2], wr=[o])
                k.dma('sp', QC[h][:, sl], o[0:96, :], o, rd=[o])
                n += 1
                no += 1
            for j in range(4):
                pa = psA[n % 3]
                n += 1
                o = og[no % 4]
                no += 1
                for kc in range(2):
                    mm(k, pa, pa[:, :], Wkn[:, kc, j * 128:(j + 1) * 128], ckb[i][:, kc, :], kc == 0, kc == 1, [Wkn, ckb[i]])
                k.op('act', lambda e: e.copy(out=o[:, :], in_=pa[:, :]), rd=[pa], wr=[o])
                k.dma('sp', KN[j][:, sl], o[:, :], o, rd=[o])
            for jt in range(4):
                pb = psB[n % 3]
                n += 1
                for kc in range(2):
                    mm(k, pb, pb[:, :], ckb[i][:, kc, jt * 128:(jt + 1) * 128], Wvv[:, kc, :], kc == 0, kc == 1, [Wvv, ckb[i]])
                k.op('act', lambda e: e.copy(out=vst[:, jt, :], in_=pb[:, :]), rd=[pb], wr=[vst])
            k.dma('sp', VC[sl, :].rearrange("(j p) c -> p j c", p=128), vst[:, :, :], vst, rd=[vst])
        k.phase_end()

    def ph_diff(l):
        k.phase_begin()
        onesf, epsb = consts()
        onesb = ones_tile("onesb", 1.0, BF16)
        o128 = ones_tile("o128", 1.0 / 128)
        gd = k.sb("gd", (128, 1), F32)
        k.dma('sp', gd[:, :], g_diff[l].rearrange("(p o) -> p o", o=1), gd, wr=[gd], slow=True)
        gdl = k.sb("gdl", (128, 1), F32)
        k.op('dve', lambda e: e.tensor_scalar(out=gdl[:, :], in0=gd[:, :], scalar1=1.0 - LAM_INIT[l], scalar2=None,
                                              op0=ALU.mult), rd=[gd], wr=[gdl])
        dl = k.sb("dl", (1, 256), F32)
        k.dma('sp', dl[:, :], diff_lambda[l].rearrange("(o a) d -> o (a d)", o=1), dl, wr=[dl])
        prod = k.sb("prod", (1, 128), F32)
        sums = k.sb("sums", (1, 2), F32)
        ex = k.sb("ex", (1, 2), F32)
        nl = k.sb("nl", (1, 2), F32)
        nlam = k.sb("nlam", (128, 2), F32)
        k.op('dve', lambda e: e.tensor_tensor(out=prod[0:1, 0:64], in0=dl[0:1, 0:64], in1=dl[0:1, 64:128], op=ALU.mult),
             rd=[dl], wr=[prod])
        k.op('dve', lambda e: e.tensor_tensor(out=prod[0:1, 64:128], in0=dl[0:1, 128:192], in1=dl[0:1, 192:256], op=ALU.mult),
             rd=[dl, prod], wr=[prod])
        k.op('dve', lambda e: e.tensor_reduce(out=sums[0:1, 0:2], in_=prod[0:1, :].rearrange("p (a d) -> p a d", a=2),
                                              axis=mybir.AxisListType.X, op=ALU.add), rd=[prod], wr=[sums])
        k.op('act', lambda e: e.activation(out=ex[0:1, :], in_=sums[0:1, :], func=AF.Exp), rd=[sums], wr=[ex])
        k.op('dve', lambda e: e.tensor_tensor(out=nl[0:1, 0:1], in0=ex[0:1, 1:2], in1=ex[0:1, 0:1], op=ALU.subtract),
             rd=[ex], wr=[nl])
        k.op('dve', lambda e: e.tensor_scalar(out=nl[0:1, 0:1], in0=nl[0:1, 0:1], scalar1=-LAM_INIT[l], scalar2=None,
                                              op0=ALU.add), rd=[nl], wr=[nl])
        k.op('dve', lambda e: e.tensor_copy(out=nl[0:1, 1:2], in_=nl[0:1, 0:1]), rd=[nl], wr=[nl])
        SA = k.ps("SA", (128, 1024))
        SB_ = k.ps("SB", (128, 1024))
        SC = k.ps("SC", (128, 1024))
        Sq = [SA, SB_]
        acc1, acc2 = k.ps("acc1"), k.ps("acc2")
        mm(k, SC, SC[:, 0:2], onesf[0:1, :], nl[0:1, 0:2], True, True, [onesf, nl])
        k.op('dve', lambda e: e.tensor_copy(out=nlam[:, :], in_=SC[:, 0:2]), rd=[SC], wr=[nlam])

        KT = [k.sb("KT%d" % i, (128, S), BF16) for i in range(2)]
        QT = [k.sb("QT%d" % i, (128, S), BF16) for i in range(2)]
        V = [k.sb("V%d" % i, (128, 64, 128), BF16) for i in range(2)]
        NP = 12
        Pb = [k.sb("P%d" % i, (128, 1024), BF16) for i in range(NP)]
        ps1 = [k.sb("ps1_%d" % i, (128, TB), F32) for i in range(2)]
        ps2 = [k.sb("ps2_%d" % i, (128, TB), F32) for i in range(2)]
        a1s = k.sb("a1s", (128, TB), F32)
        a2s = k.sb("a2s", (128, TB), F32)
        r1 = k.sb("r1", (128, TB), F32)
        r2 = k.sb("r2", (128, TB), F32)
        o1 = k.sb("o1", (128, TB), F32)
        o2 = k.sb("o2", (128, TB), F32)
        od = k.sb("od", (128, TB), F32)
        sqd = k.sb("sqd", (128, TB), F32)
        tmp = k.sb("tmp", (128, TB), F32)
        rstd = k.sb("rstd", (128, TB), F32)
        ost = [k.sb("ost%d" % i, (128, TB), BF16) for i in range(2)]

        def load_head(h):
            i = h % 2
            k.dma('sp', KT[i][:, :], RP[4 + h], KT[i], wr=[KT[i]])
            k.dma('sp', QT[i][:, :], RP[h], QT[i], wr=[QT[i]])
            k.dma('sp', V[i][:, :, :], VA[:, h * 128:(h + 1) * 128].rearrange("(kc p) d -> p kc d", p=128), V[i], wr=[V[i]])
        load_head(0)
        sc = [0]
        pc = [0]
        pending = []

        def run_pending(kc):
            keep = []
            for (t, fn) in pending:
                if kc is None or t <= kc:
                    fn()
                else:
                    keep.append((t, fn))
            pending[:] = keep
        nq = 0
        for h in range(4):
            if h + 1 < 4:
                load_head(h + 1)
            i = h % 2
            for qb in range(NTB):
                qsl = slice(qb * TB, (qb + 1) * TB)
                pa1, pa2 = ps1[nq % 2], ps2[nq % 2]

                def qk(kc):
                    ksl = slice(kc * 128, (kc + 1) * 128)
                    s = Sq[sc[0] % 2]
                    sc[0] += 1
                    mm(k, s, s[:, 0:512], KT[i][0:64, ksl], QT[i][0:64, qsl], True, True, [KT[i], QT[i]])
                    mm(k, s, s[:, 512:1024], KT[i][64:128, ksl], QT[i][64:128, qsl], True, True, [KT[i], QT[i]])
                    return s

                def pv(kc, s):
                    p = Pb[pc[0] % NP]
                    pc[0] += 1
                    k.op('act', lambda e: e.activation(out=p[:, :], in_=s[:, :], func=AF.Exp, scale=0.125), rd=[s], wr=[p])
                    st_, sp_ = kc == 0, kc == 63
                    mm(k, acc1, acc1[:, :], V[i][:, kc, :], p[:, 0:512], st_, sp_, [V[i], p])
                    mm(k, acc2, acc2[:, :], V[i][:, kc, :], p[:, 512:1024], st_, sp_, [V[i], p])
                    if kc == 0:
                        k.op('dve', lambda e: e.tensor_copy(out=pa1[:, :], in_=p[:, 0:512]), rd=[p], wr=[pa1])
                        k.op('pool', lambda e: e.tensor_copy(out=pa2[:, :], in_=p[:, 512:1024]), rd=[p], wr=[pa2])
                    else:
                        k.op('dve', lambda e: e.tensor_tensor(out=pa1[:, :], in0=pa1[:, :], in1=p[:, 0:512], op=ALU.add),
                             rd=[p, pa1], wr=[pa1])
                        k.op('pool', lambda e: e.tensor_tensor(out=pa2[:, :], in0=pa2[:, :], in1=p[:, 512:1024], op=ALU.add),
                             rd=[p, pa2], wr=[pa2])
                cur = qk(0)
                for kc in range(64):
                    nxt = qk(kc + 1) if kc + 1 < 64 else None
                    pv(kc, cur)
                    cur = nxt
                    run_pending(kc)
                k.op('dve', lambda e: e.tensor_copy(out=a1s[:, :], in_=acc1[:, :]), rd=[acc1], wr=[a1s])
                k.op('dve', lambda e: e.tensor_copy(out=a2s[:, :], in_=acc2[:, :]), rd=[acc2], wr=[a2s])
                o = ost[nq % 2]
                nq += 1

                def ep1(pa1=pa1, pa2=pa2):
                    mm(k, SC, SC[:, 0:512], onesf[:, :], pa1[:, :], True, True, [onesf, pa1])
                    mm(k, SC, SC[:, 512:1024], onesf[:, :], pa2[:, :], True, True, [onesf, pa2])
                    k.op('dve', lambda e: e.reciprocal(out=r1[:, :], in_=SC[:, 0:512]), rd=[SC], wr=[r1])
                    k.op('dve', lambda e: e.tensor_tensor(out=o1[:, :], in0=a1s[:, :], in1=r1[:, :], op=ALU.mult), rd=[a1s, r1], wr=[o1])
                    k.op('dve', lambda e: e.reciprocal(out=r2[:, :], in_=SC[:, 512:1024]), rd=[SC], wr=[r2])
                    k.op('dve', lambda e: e.tensor_tensor(out=o2[:, :], in0=a2s[:, :], in1=r2[:, :], op=ALU.mult), rd=[a2s, r2], wr=[o2])
                    k.op('dve', lambda e: e.scalar_tensor_tensor(out=od[:, :], in0=o2[:, :], scalar=nlam[:, 0:1], in1=o1[:, :],
                                                                 op0=ALU.mult, op1=ALU.add), rd=[o2, o1, nlam], wr=[od])
                    k.op('pool', lambda e: e.tensor_tensor(out=sqd[:, :], in0=od[:, :], in1=od[:, :], op=ALU.mult), rd=[od], wr=[sqd])

                def ep2(o=o, h=h, qsl=qsl):
                    mm(k, SC, SC[:, 0:512], o128[:, :], sqd[:, :], True, True, [o128, sqd])
                    k.op('act', lambda e: e.activation(out=tmp[:, :], in_=SC[:, 0:512], func=AF.Ln, bias=epsb[:, 0:1], scale=1.0),
                         rd=[SC, epsb], wr=[tmp])
                    k.op('act', lambda e: e.activation(out=rstd[:, :], in_=tmp[:, :], func=AF.Exp, scale=-0.5), rd=[tmp], wr=[rstd])
                    k.op('dve', lambda e: e.scalar_tensor_tensor(out=o[:, :], in0=od[:, :], scalar=gdl[:, 0:1], in1=rstd[:, :],
                                                                 op0=ALU.mult, op1=ALU.mult), rd=[od, gdl, rstd], wr=[o])
                    k.dma('sp', OA[h][:, qsl], o[:, :], o, rd=[o])
                pending.append((3, ep1))
                pending.append((30, ep2))
        run_pending(None)
        k.phase_end()

    def ph_mla(l):
        k.phase_begin()
        KT = [k.sb("KT%d" % i, (128, S), BF16) for i in range(2)]
        QT = [k.sb("QT%d" % i, (128, S), BF16) for i in range(2)]
        V = [k.sb("V%d" % i, (128, 64, 128), BF16) for i in range(2)]
        for i in range(2):
            k.op('pool', lambda e: e.memset(V[i][:, :, 64:128], 1.0), wr=[V[i]])
        Pb = [k.sb("P%d" % i, (128, 1024), BF16) for i in range(3)]
        Sps = [k.ps("S%d" % i, (128, 1024)) for i in range(3)]
        acc = [k.ps("acc%d" % i) for i in range(2)]
        rr = k.sb("rr", (64, TB), F32)
        ost = [k.sb("ost%d" % i, (64, TB), BF16) for i in range(2)]
        sc_ = 96 ** -0.5

        def load_head(h):
            i = h % 2
            b0 = 64 * (h % 2)
            k.dma('sp', KT[i][0:64, :], KN[h // 2][b0:b0 + 64, :], KT[i], wr=[KT[i]])
            k.dma('sp', KT[i][64:96, :], KR[:, :], KT[i], wr=[KT[i]])
            k.dma('sp', QT[i][0:96, :], QC[h], QT[i], wr=[QT[i]])
            k.dma('sp', V[i][:, :, 0:64], VC[:, h * 64:(h + 1) * 64].rearrange("(kc p) d -> p kc d", p=128), V[i], wr=[V[i]])
        load_head(0)
        sc = [0]
        pc = [0]
        nq = 0
        for h in range(8):
            if h + 1 < 8:
                load_head(h + 1)
            i = h % 2
            b0 = 64 * (h % 2)
            for qb in range(NTB):
                qsl = slice(qb * TB, (qb + 1) * TB)
                a = acc[nq % 2]

                def qk(j):
                    s1 = Sps[sc[0] % 3]
                    sc[0] += 1
                    for t in range(2):
                        kc = 2 * j + t
                        mm(k, s1, s1[:, t * 512:(t + 1) * 512], KT[i][0:96, kc * 128:(kc + 1) * 128], QT[i][0:96, qsl],
                           True, True, [KT[i], QT[i]])
                    return s1

                def pv(j, s1):
                    p1 = Pb[pc[0] % 3]
                    pc[0] += 1
                    k.op('act', lambda e: e.activation(out=p1[:, :], in_=s1[:, :], func=AF.Exp, scale=sc_), rd=[s1], wr=[p1])
                    for t in range(2):
                        kc = 2 * j + t
                        mm(k, a, a[:, :], V[i][:, kc, :], p1[:, t * 512:(t + 1) * 512], kc == 0, kc == 63, [V[i], p1])
                cur = qk(0)
                for j in range(32):
                    nxt = qk(j + 1) if j + 1 < 32 else None
                    pv(j, cur)
                    cur = nxt
                o = ost[nq % 2]
                nq += 1
                k.op('dve', lambda e: e.reciprocal(out=rr[:, :], in_=a[64:128, :]), rd=[a], wr=[rr])
                k.op('dve', lambda e: e.tensor_tensor(out=o[:, :], in0=a[0:64, :], in1=rr[:, :], op=ALU.mult), rd=[a, rr], wr=[o])
                k.dma('sp', OC[h // 2][b0:b0 + 64, qsl], o[:, :], o, rd=[o])
        k.phase_end()

    def ph_dil(l):
        k.phase_begin()
        PADM = 1024
        onesb = ones_tile("onesb", 1.0, BF16)
        M0 = k.sb("M0", (128, 2, 128), BF16)
        M = k.sb("M", (128, 2, 128), BF16)
        Mf = k.sb("Mf", (128, 2, 128), BF16)
        Ml = k.sb("Ml", (128, 2, 128), BF16)
        for j in range(2):
            k.op('pool', lambda e: e.affine_select(out=M0[:, j, :], in_=onesb[:, :], pattern=[[-1, 128]], compare_op=ALU.is_ge,
                                                   fill=0.0, base=128 * j, channel_multiplier=1), rd=[onesb], wr=[M0])
            k.op('pool', lambda e: e.affine_select(out=M[:, j, :], in_=M0[:, j, :], pattern=[[1, 128]], compare_op=ALU.is_ge,
                                                   fill=0.0, base=128 - 128 * j, channel_multiplier=-1), rd=[M0], wr=[M])
        k.op('pool', lambda e: e.affine_select(out=Mf[:, 0, :], in_=M[:, 0, :], pattern=[[0, 128]], compare_op=ALU.is_ge,
                                               fill=0.0, base=-64, channel_multiplier=1), rd=[M], wr=[Mf])
        k.op('pool', lambda e: e.tensor_copy(out=Mf[:, 1, :], in_=M[:, 1, :]), rd=[M], wr=[Mf])
        k.op('pool', lambda e: e.affine_select(out=Ml[:, 1, :], in_=M[:, 1, :], pattern=[[0, 128]], compare_op=ALU.is_ge,
                                               fill=0.0, base=63, channel_multiplier=-1), rd=[M], wr=[Ml])
        k.op('pool', lambda e: e.tensor_copy(out=Ml[:, 0, :], in_=M[:, 0, :]), rd=[M], wr=[Ml])
        QT = [k.sb("QT%d" % i, (64, S), BF16) for i in range(2)]
        KT = [k.sb("KT%d" % i, (64, S + 2 * PADM), BF16) for i in range(2)]
        Vg = [k.sb("Vg%d" % i, (128, 80, 128), BF16) for i in range(2)]
        for i in range(2):
            k.op('pool', lambda e: e.memset(KT[i][:, 0:PADM], 0.0), wr=[KT[i]])
            k.op('pool', lambda e: e.memset(KT[i][:, PADM + S:PADM + S + PADM], 0.0), wr=[KT[i]])
            k.op('pool', lambda e: e.memset(Vg[i][:, :, 0:64], 0.0), wr=[Vg[i]])
            k.op('pool', lambda e: e.memset(Vg[i][:, :, 64:128], 1.0), wr=[Vg[i]])
        tot = k.sb("tot", (128, S), F32)
        rr = k.sb("rr", (64, 2048), F32)
        ost = [k.sb("ost%d" % i, (64, 2048), BF16) for i in range(2)]
        Pb = [k.sb("P%d" % i, (128, 256), BF16) for i in range(3)]
        Pm = [k.sb("Pm%d" % i, (128, 256), BF16) for i in range(3)]
        Sps = [k.ps("S%d" % i) for i in range(3)]
        acc = [k.ps("acc%d" % i) for i in range(3)]
        DIL = (1, 4, 16)
        jobs = [(h, g) for h in range(4) for g in range(3)]

        def load_job(ji):
            h, g = jobs[ji]
            i = ji % 2
            d = DIL[g]
            n = S // d
            nc_ = n // 128 + 1
            b0 = 64 * (h % 2)
            k.dma('sp', QT[i][:, :], RP[8 + 4 * g + h // 2][b0:b0 + 64, :], QT[i], wr=[QT[i]])
            k.dma('sp', KT[i][:, PADM:PADM + S], RP[8 + 4 * g + 2 + h // 2][b0:b0 + 64, :], KT[i], wr=[KT[i]])
            src = VB[g][:, h * 64:(h + 1) * 64].rearrange("(t c) e -> c t e", c=d)
            for c in range(d):
                base = c * nc_
                k.dma('sp', Vg[i][:, base + 1:base + n // 128, 0:64],
                      src[c][64:n - 64, :].rearrange("(m p) e -> p m e", p=128), Vg[i], wr=[Vg[i]])
                k.dma('sp', Vg[i][64:128, base, 0:64], src[c][0:64, :], Vg[i], wr=[Vg[i]])
                k.dma('sp', Vg[i][0:64, base + n // 128, 0:64], src[c][n - 64:n, :], Vg[i], wr=[Vg[i]])
        load_job(0)
        un = 0
        no = 0
        for ji, (h, g) in enumerate(jobs):
            if ji + 1 < len(jobs):
                load_job(ji + 1)
            i = ji % 2
            d = DIL[g]
            n = S // d
            nu = n // 128
            nc_ = nu + 1
            for c in range(d):
                for u in range(nu):
                    tq0 = 128 * u
                    s = Sps[un % 3]
                    pb_, pm_, a = Pb[un % 3], Pm[un % 3], acc[un % 3]
                    q0 = tq0 * d + c
                    qcols = slice(q0, q0 + 127 * d + 1, d)
                    for j in range(2):
                        ks = PADM + (tq0 - 64 + 128 * j) * d + c
                        kcols = slice(ks, ks + 127 * d + 1, d)
                        mm(k, s, s[:, j * 128:(j + 1) * 128], KT[i][:, kcols], QT[i][:, qcols], True, True, [KT[i], QT[i]])
                    k.op('act', lambda e: e.activation(out=pb_[:, :], in_=s[:, 0:256], func=AF.Exp, scale=0.125), rd=[s], wr=[pb_])
                    mk = Mf if u == 0 else (Ml if u == nu - 1 else M)
                    eng = 'dve' if un % 2 == 0 else 'pool'
                    k.op(eng, lambda e: e.tensor_tensor(out=pm_[:, :], in0=pb_[:, :], in1=mk[:, :, :].rearrange("p a b -> p (a b)"),
                                                        op=ALU.mult), rd=[pb_, mk], wr=[pm_])
                    for j in range(2):
                        mm(k, a, a[:, 0:128], Vg[i][:, c * nc_ + u + j, :], pm_[:, j * 128:(j + 1) * 128], j == 0, j == 1, [Vg[i], pm_])
                    dst = tot[:, qcols]
                    if g == 0:
                        k.op('act', lambda e: e.copy(out=dst, in_=a[:, 0:128]), rd=[a], wr=[tot])
                    else:
                        k.op('dve', lambda e: e.tensor_tensor(out=dst, in0=a[:, 0:128], in1=dst, op=ALU.add), rd=[a, tot], wr=[tot])
                    un += 1
            if g == 2:
                b0 = 64 * (h % 2)
                for r in range(4):
                    rs = slice(r * 2048, (r + 1) * 2048)
                    o = ost[no % 2]
                    no += 1
                    k.op('dve', lambda e: e.reciprocal(out=rr[:, :], in_=tot[64:128, rs]), rd=[tot], wr=[rr])
                    k.op('dve', lambda e: e.tensor_tensor(out=o[:, :], in0=tot[0:64, rs], in1=rr[:, :], op=ALU.mult),
                         rd=[tot, rr], wr=[o])
                    k.dma('sp', OB[h // 2][b0:b0 + 64, rs], o[:, :], o, rd=[o])
        k.phase_end()

    def ph_out(l):
        k.phase_begin()
        onesf, epsb = consts()
        onesD = ones_tile("onesD", 1.0 / D)
        Woa = k.sb("Woa", (128, 4, D), BF16)
        Wob = k.sb("Wob", (128, 2, D), BF16)
        Woc = k.sb("Woc", (128, 4, D), BF16)
        Wo = k.sb("Wo", (128, 8, D), BF16)
        wload(Woa, 0, w_o_diff[l], D)
        wload(Wob, 0, w_o_dil[l], D)
        wload(Woc, 0, w_o_mla[l], D)
        wload(Wo, 0, w_out[l], D)
        gm = gvec("gm", g_mlp[l], 8)
        oab = [k.sb("oab%d" % i, (128, 4, TB), BF16) for i in range(2)]
        obb = [k.sb("obb%d" % i, (128, 2, TB), BF16) for i in range(2)]
        ocb = [k.sb("ocb%d" % i, (128, 4, TB), BF16) for i in range(2)]
        gb = [k.sb("gb%d" % i, (128, 24, TB), BF16) for i in range(2)]
        xb = [k.sb("xb%d" % i, (128, 8, TB), F32) for i in range(2)]
        mg = k.sb("mg", (128, 8, TB), BF16)
        m1 = [k.sb("m1_%d" % i, (128, TB), F32) for i in range(2)]
        m2 = [k.sb("m2_%d" % i, (128, TB), F32) for i in range(2)]
        m3 = [k.sb("m3_%d" % i, (128, TB), F32) for i in range(2)]
        sq = k.sb("sq", (128, 8, TB), F32)
        rstd = k.sb("rstd", (128, TB), F32)
        tmp = k.sb("tmp", (128, TB), F32)
        h2b = [k.sb("h2b%d" % i, (128, 8, TB), BF16) for i in range(2)]
        pa = [k.ps("pa%d" % i) for i in range(2)]
        pb = [k.ps("pb%d" % i) for i in range(2)]
        pc = [k.ps("pc%d" % i) for i in range(2)]
        msps = k.ps("msps")
        OAv = OA.rearrange("c p s -> p c s")
        OBv = OB.rearrange("c p s -> p c s")
        OCv = OC.rearrange("c p s -> p c s")
        Gv = G.rearrange("c p s -> p c s")
        H2v = H2.rearrange("c p s -> p c s")

        def load_blk(b):
            sl = slice(b * TB, (b + 1) * TB)
            i = b % 2
            k.dma('sp', oab[i][:, :, :], OAv[:, :, sl], oab[i], wr=[oab[i]])
            k.dma('sp', obb[i][:, :, :], OBv[:, :, sl], obb[i], wr=[obb[i]])
            k.dma('sp', ocb[i][:, :, :], OCv[:, :, sl], ocb[i], wr=[ocb[i]])
            k.dma('sp', gb[i][:, :, :], Gv[:, :, sl], gb[i], wr=[gb[i]])
            k.dma('sp', xb[i][:, :, :], xTv[:, :, sl], xb[i], wr=[xb[i]])
        load_blk(0)
        n = 0
        for b in range(NTB):
            if b + 1 < NTB:
                load_blk(b + 1)
            i = b % 2
            sl = slice(b * TB, (b + 1) * TB)
            for oc in range(8):
                cs = slice(oc * 128, (oc + 1) * 128)
                a, bb, c = pa[n % 2], pb[n % 2], pc[n % 2]
                for kc in range(4):
                    mm(k, a, a[:, :], Woa[:, kc, cs], oab[i][:, kc, :], kc == 0, kc == 3, [Woa, oab[i]])
                for kc in range(2):
                    mm(k, bb, bb[:, :], Wob[:, kc, cs], obb[i][:, kc, :], kc == 0, kc == 1, [Wob, obb[i]])
                for kc in range(4):
                    mm(k, c, c[:, :], Woc[:, kc, cs], ocb[i][:, kc, :], kc == 0, kc == 3, [Woc, ocb[i]])
                x1, x2, x3 = m1[n % 2], m2[n % 2], m3[n % 2]
                k.op('dve', lambda e: e.tensor_tensor(out=x1[:, :], in0=a[:, :], in1=gb[i][:, oc, :], op=ALU.mult), rd=[a, gb[i]], wr=[x1])
                k.op('dve', lambda e: e.tensor_tensor(out=x2[:, :], in0=bb[:, :], in1=gb[i][:, 8 + oc, :], op=ALU.mult), rd=[bb, gb[i]], wr=[x2])
                k.op('dve', lambda e: e.tensor_tensor(out=x3[:, :], in0=c[:, :], in1=gb[i][:, 16 + oc, :], op=ALU.mult), rd=[c, gb[i]], wr=[x3])
                k.op('pool', lambda e: e.tensor_tensor(out=x1[:, :], in0=x1[:, :], in1=x2[:, :], op=ALU.add), rd=[x1, x2], wr=[x1])
                k.op('pool', lambda e: e.tensor_tensor(out=mg[:, oc, :], in0=x1[:, :], in1=x3[:, :], op=ALU.add), rd=[x1, x3], wr=[mg])
                n += 1
            for oc in range(8):
                cs = slice(oc * 128, (oc + 1) * 128)
                a = pa[n % 2]
                n += 1
                for kc in range(8):
                    mm(k, a, a[:, :], Wo[:, kc, cs], mg[:, kc, :], kc == 0, kc == 7, [Wo, mg])
                k.op('dve', lambda e: e.tensor_tensor(out=xb[i][:, oc, :], in0=a[:, :], in1=xb[i][:, oc, :], op=ALU.add),
                     rd=[a, xb[i]], wr=[xb[i]])
            ho = h2b[i]
            norm_block(k, xb[i], 8, gm, 0, onesD, sq, msps, rstd, tmp, lambda c: (ho, ho[:, c, :]), epsb)
            k.dma('sp', xTv[:, :, sl], xb[i][:, :, :], xb[i], rd=[xb[i]])
            k.dma('sp', H2v[:, :, sl], ho[:, :, :], ho, rd=[ho])
        k.phase_end()

    def ph_up(l):
        k.phase_begin()
        Wu = k.sb("Wu", (128, 8, 4 * D), BF16)
        wload(Wu, 0, w_up[l], 4 * D)
        hb = [k.sb("hb%d" % i, (128, 8, TB), BF16) for i in range(2)]
        ust = [k.sb("ust%d" % i, (128, 8, TB), BF16) for i in range(2)]
        sqv = [k.sb("sqv%d" % i, (128, TB), F32) for i in range(2)]
        ps = [k.ps("ps%d" % i) for i in range(4)]
        H2v = H2.rearrange("c p s -> p c s")
        Uv = U.rearrange("c p s -> p c s")

        def load_blk(b):
            sl = slice(b * TB, (b + 1) * TB)
            k.dma('sp', hb[b % 2][:, :, :], H2v[:, :, sl], hb[b % 2], wr=[hb[b % 2]])
        load_blk(0)
        n = 0
        for b in range(NTB):
            if b + 1 < NTB:
                load_blk(b + 1)
            h = hb[b % 2]
            sl = slice(b * TB, (b + 1) * TB)
            for uc in range(32):
                p = ps[n % 4]
                sv = sqv[n % 2]
                n += 1
                us = ust[(uc // 8) % 2]
                for kc in range(8):
                    mm(k, p, p[:, :], Wu[:, kc, uc * 128:(uc + 1) * 128], h[:, kc, :], kc == 0, kc == 7, [Wu, h])
                k.op('act', lambda e: e.activation(out=sv[:, :], in_=p[:, :], func=AF.Square), rd=[p], wr=[sv])
                k.op('dve', lambda e: e.scalar_tensor_tensor(out=us[:, uc % 8, :], in0=p[:, :], scalar=0.0, in1=sv[:, :],
                                                             op0=ALU.is_gt, op1=ALU.mult), rd=[p, sv], wr=[us])
                if uc % 8 == 7:
                    k.dma('sp', Uv[:, uc - 7:uc + 1, sl], us[:, :, :], us, rd=[us])
        k.phase_end()

    def ph_down(l):
        last = (l == DEPTH - 1)
        k.phase_begin()
        onesf, epsb = consts()
        onesD = ones_tile("onesD", 1.0 / D)
        Wd = k.sb("Wd", (128, 32, D), BF16)
        wload(Wd, 0, w_down[l], D)
        gn = gvec("gn", g_final if last else g_mix[l + 1], 8)
        ub = [k.sb("ub%d" % i, (128, 32, TB), BF16) for i in range(2)]
        xb = [k.sb("xb%d" % i, (128, 8, TB), F32) for i in range(2)]
        sq = k.sb("sq", (128, 8, TB), F32)
        rstd = k.sb("rstd", (128, TB), F32)
        tmp = k.sb("tmp", (128, TB), F32)
        ps = [k.ps("ps%d" % i) for i in range(3)]
        msps = k.ps("msps")
        Uv = U.rearrange("c p s -> p c s")
        if last:
            ident = k.sb("ident", (128, 128), F32)
            k.op('pool', lambda e: e.affine_select(out=ident[:, :], in_=onesf[:, :], pattern=[[-1, 128]],
                                                   compare_op=ALU.is_equal, fill=0.0, base=0, channel_multiplier=1),
                 rd=[onesf], wr=[ident])
            yb = k.sb("yb", (128, 8, TB), F32)
            yo = [k.sb("yo%d" % i, (128, D), F32) for i in range(2)]
            tp = [k.ps("tp%d" % i) for i in range(2)]
            outv = out.rearrange("(b j p) d -> b j p d", p=128, j=4)
        else:
            hTo = [k.sb("hTo%d" % i, (128, 8, TB), BF16) for i in range(2)]

        def load_blk(b):
            sl = slice(b * TB, (b + 1) * TB)
            i = b % 2
            k.dma('sp', ub[i][:, :, :], Uv[:, :, sl], ub[i], wr=[ub[i]])
            k.dma('sp', xb[i][:, :, :], xTv[:, :, sl], xb[i], wr=[xb[i]])
        load_blk(0)
        n = 0
        ny = 0
        for b in range(NTB):
            if b + 1 < NTB:
                load_blk(b + 1)
            i = b % 2
            sl = slice(b * TB, (b + 1) * TB)
            for oc in range(8):
                p = ps[n % 3]
                n += 1
                for kc in range(32):
                    mm(k, p, p[:, :], Wd[:, kc, oc * 128:(oc + 1) * 128], ub[i][:, kc, :], kc == 0, kc == 31, [Wd, ub[i]])
                k.op('dve', lambda e: e.tensor_tensor(out=xb[i][:, oc, :], in0=p[:, :], in1=xb[i][:, oc, :], op=ALU.add),
                     rd=[p, xb[i]], wr=[xb[i]])
            if not last:
                ho = hTo[i]
                norm_block(k, xb[i], 8, gn, 0, onesD, sq, msps, rstd, tmp, lambda c: (ho, ho[:, c, :]), epsb)
                k.dma('sp', xTv[:, :, sl], xb[i][:, :, :], xb[i], rd=[xb[i]])
                k.dma('sp', hTv[:, :, sl], ho[:, :, :], ho, rd=[ho])
            else:
                norm_block(k, xb[i], 8, gn, 0, onesD, sq, msps, rstd, tmp, lambda c: (yb, yb[:, c, :]), epsb)
                for j in range(4):
                    y = yo[ny % 2]
                    ny += 1
                    for half in range(2):
                        t = tp[half]
                        for q in range(4):
                            fc = half * 4 + q
                            k.op('pe', lambda e: e.transpose(t[:, q * 128:(q + 1) * 128], yb[:, fc, j * 128:(j + 1) * 128],
                                                             ident[:, :]), rd=[yb, ident], wr=[t], sig=(q == 3))
                        if half == 0:
                            k.op('act', lambda e: e.copy(out=y[:, 0:512], in_=t[:, :]), rd=[t], wr=[y])
                        else:
                            k.op('dve', lambda e: e.tensor_copy(out=y[:, 512:1024], in_=t[:, :]), rd=[t], wr=[y])
                    k.dma('sp', outv[b][j], y[:, :], y, rd=[y])
        k.phase_end()

    def run_all():
        ph_first()
        if done('first'):
            return
        for l in layers:
            ph_rope(l)
            if done('rope%d' % l):
                return
            ph_vcg(l)
            if done('vcg%d' % l):
                return
            ph_mla_up(l)
            if done('mlaup%d' % l):
                return
            ph_diff(l)
            if done('diff%d' % l):
                return
            ph_mla(l)
            if done('mla%d' % l):
                return
            ph_dil(l)
            if done('dil%d' % l):
                return
            ph_out(l)
            if done('out%d' % l):
                return
            ph_up(l)
            if done('up%d' % l):
                return
            ph_down(l)
            if done('down%d' % l):
                return
    run_all()
    es.close()
    return nc


def host_tables():
    pos = np.arange(S, dtype=np.float32)

    def tab(half):
        inv = (np.float32(10000.0) ** (-np.arange(half, dtype=np.float32) / np.float32(half))).astype(np.float32)
        ang = (pos[None, :] * inv[:, None]).astype(np.float32)
        return np.cos(ang).astype(np.float32), np.sin(ang).astype(np.float32)
    c32, s32 = tab(32)
    c16, s16 = tab(16)
    rc64 = np.concatenate([c32, c32, c32, c32], 0)
    rs64 = np.concatenate([-s32, s32, -s32, s32], 0)
    rcq = np.concatenate([np.ones((64, S), np.float32), c16, c16], 0)
    rsq = np.concatenate([np.zeros((64, S), np.float32), -s16, s16], 0)
    return dict(rc64=np.ascontiguousarray(rc64), rs64=np.ascontiguousarray(rs64),
                rcq=np.ascontiguousarray(rcq), rsq=np.ascontiguousarray(rsq))


def kernel(**inputs):
    nc = bass.Bass("TRN2", target_bir_lowering=False)
    build(nc)
    shared = {k: np.ascontiguousarray(np.asarray(v, dtype=np.float32)) for k, v in inputs.items() if k != 'x'}
    shared.update(host_tables())
    x = np.asarray(inputs['x'], dtype=np.float32)
    in_maps = [dict(shared, x=np.ascontiguousarray(x[c])) for c in range(8)]
    res = run_bass_kernel_spmd(nc, in_maps, core_ids=list(range(8)))
    return np.stack([np.asarray(r['out'], dtype=np.float32) for r in res.results], 0)
```

```python
import numpy as np
from contextlib import ExitStack
import concourse.bass as bass
import concourse.mybir as mybir
from concourse.bass_utils import run_bass_kernel_spmd

F32 = mybir.dt.float32
BF16 = mybir.dt.bfloat16
ALU = mybir.AluOpType
AF = mybir.ActivationFunctionType

S = 8192
D = 1024
TB = 512
NTB = S // TB
DEPTH = 2
EPS = 1e-6
IN_COLS = 7968
C_DQ, C_DK, C_DV = 0, 1024, 2048
C_DIL = 2560
C_CQ, C_CKV, C_KR, C_GATE = 4864, 5632, 5888, 5920
ENGS = ('pe', 'act', 'dve', 'pool', 'sp')


class Buf:
    __slots__ = ('name', 'lw', 'rd', 'dkey', 't')

    def __init__(self, name, t=None):
        self.name = name
        self.lw = None
        self.rd = {}
        self.dkey = None
        self.t = t

    def __getitem__(self, idx):
        return self.t[idx]


class KB:
    def __init__(self, nc, es, n_dsem=40):
        self.nc = nc
        self.e = dict(pe=nc.tensor, act=nc.scalar, dve=nc.vector, pool=nc.gpsimd, sp=nc.sync)
        self.h = {}
        self.cnt = {}
        for k in ENGS:
            self.h[('e', k)] = es.enter_context(nc.semaphore('sem_' + k))
            self.cnt[('e', k)] = 0
        self.seen = {k: {} for k in ENGS}
        self.bar = es.enter_context(nc.semaphore('bar'))
        self.barn = 0
        self.dpool = []
        for i in range(n_dsem):
            key = ('d', i)
            self.h[key] = es.enter_context(nc.semaphore('dsem%d' % i))
            self.cnt[key] = 0
            self.dpool.append(key)
        self.dfree = list(self.dpool)
        self.bufs = []
        self.pes = None
        self.uid = 0

    def phase_begin(self):
        self.pes = ExitStack()
        self.bufs = []
        self.dfree = list(self.dpool)

    def phase_end(self):
        self.barrier()
        self.pes.close()
        self.pes = None

    def sb(self, name, shape, dt):
        self.uid += 1
        name = "%s_u%d" % (name, self.uid)
        t = self.pes.enter_context(self.nc.sbuf_tensor(name, list(shape), dt))
        b = Buf(name, t)
        self.bufs.append(b)
        return b

    def ps(self, name, shape=(128, 512), dt=F32):
        self.uid += 1
        name = "%s_u%d" % (name, self.uid)
        t = self.pes.enter_context(self.nc.psum_tensor(name, list(shape), dt))
        b = Buf(name, t)
        self.bufs.append(b)
        return b

    def _deps(self, rd, wr):
        deps = {}

        def add(k, v):
            if deps.get(k, 0) < v:
                deps[k] = v
        for b in rd:
            if b.lw is not None:
                add(*b.lw)
        for b in wr:
            if b.lw is not None:
                add(*b.lw)
            for k, v in b.rd.items():
                add(k, v)
        return deps

    def _need(self, eng, deps):
        e = self.e[eng]
        seen = self.seen[eng]
        for k, v in deps.items():
            if seen.get(k, 0) >= v:
                continue
            e.wait_ge(self.h[k], v)
            seen[k] = v

    def op(self, eng, fn, rd=(), wr=(), sig=True):
        deps = self._deps(rd, wr)
        key = ('e', eng)
        if eng == 'pe':
            deps.pop(key, None)
        self._need(eng, deps)
        inst = fn(self.e[eng])
        if sig:
            self.cnt[key] += 1
            inst.then_inc(self.h[key], 1)
            mark = self.cnt[key]
        else:
            mark = self.cnt[key] + 1
        for b in wr:
            b.lw = (key, mark)
            b.rd = {}
        for b in rd:
            if b in wr:
                continue
            if b.rd.get(key, 0) < mark:
                b.rd[key] = mark
        return inst

    def dma(self, q, out, in_, sem, rd=(), wr=(), slow=False):
        deps = self._deps(rd, wr)
        self._need(q, deps)
        if sem.dkey is None:
            sem.dkey = self.dfree.pop()
        key = sem.dkey
        if slow:
            inst = self.e[q].dma_start(out=out, in_=in_, allow_slow_non_contiguous=True)
        else:
            inst = self.e[q].dma_start(out=out, in_=in_)
        self.cnt[key] += 16
        inst.then_inc(self.h[key], 16)
        mark = self.cnt[key]
        for b in wr:
            b.lw = (key, mark)
            b.rd = {}
        for b in rd:
            if b in wr:
                continue
            b.rd[key] = mark
        return inst

    def barrier(self):
        sp = self.e['sp']
        seen = self.seen['sp']
        for k, v in self.cnt.items():
            if v > seen.get(k, 0):
                sp.wait_ge(self.h[k], v)
                seen[k] = v
        self.barn += 1
        sp.sem_inc(self.bar, 1)
        for k in ENGS:
            if k != 'sp':
                self.e[k].wait_ge(self.bar, self.barn)
                for kk, v in self.cnt.items():
                    self.seen[k][kk] = v
        for b in self.bufs:
            b.lw = None
            b.rd = {}
            b.dkey = None


def mm(k, out_buf, out_ap, lhsT_ap, rhs_ap, start, stop, rd):
    k.op('pe', lambda e: e.matmul(out_ap, lhsT_ap, rhs_ap, start=start, stop=stop),
         rd=rd, wr=[out_buf], sig=stop)


def norm_block(k, xb, KC, g_buf, gcol0, ones_buf, sq, msps, rstd, tmp, out_fn, eps_buf):
    k.op('act', lambda e: e.activation(out=sq[:, 0:KC, :], in_=xb[:, 0:KC, :], func=AF.Square),
         rd=[xb], wr=[sq])
    for c in range(KC):
        mm(k, msps, msps[:, :], ones_buf[:, :], sq[:, c, :], c == 0, c == KC - 1, [ones_buf, sq])
    k.op('act', lambda e: e.activation(out=tmp[:, :], in_=msps[:, :], func=AF.Sqrt, bias=eps_buf[:, 0:1], scale=1.0),
         rd=[msps, eps_buf], wr=[tmp])
    k.op('dve', lambda e: e.reciprocal(out=rstd[:, :], in_=tmp[:, :]), rd=[tmp], wr=[rstd])
    for c in range(KC):
        ob, oap = out_fn(c)
        k.op('dve', lambda e, c=c, oap=oap: e.scalar_tensor_tensor(
            out=oap, in0=xb[:, c, :], scalar=g_buf[:, gcol0 + c:gcol0 + c + 1], in1=rstd[:, :],
            op0=ALU.mult, op1=ALU.mult), rd=[xb, g_buf, rstd], wr=[ob])


C_DQ, C_DK, C_DV, C_DIL, C_CQ, C_CKV, C_KR, C_GATE = 0, 512, 1024, 1536, 3840, 4608, 4864, 4896
NRC = 20
LAM_INIT = [0.8 - 0.6 * float(np.exp(-0.3 * l)) for l in range(DEPTH)]


def build(nc, upto=None, dump=(), layers=(0, 1)):
    dt_in = lambda name, shape: nc.dram_tensor(name, list(shape), F32, kind="ExternalInput").ap()
    x = dt_in("x", (S, D))
    w_in = dt_in("w_in", (DEPTH, D, IN_COLS))
    b_gate = dt_in("b_gate", (DEPTH, 3, D))
    g_mix = dt_in("g_mix", (DEPTH, D))
    diff_lambda = dt_in("diff_lambda", (DEPTH, 4, 64))
    g_diff = dt_in("g_diff", (DEPTH, 128))
    g_cq = dt_in("g_cq", (DEPTH, 768))
    g_ckv = dt_in("g_ckv", (DEPTH, 256))
    w_uq = dt_in("w_uq", (DEPTH, 768, 768))
    w_ukv = dt_in("w_ukv", (DEPTH, 256, 1024))
    w_o_diff = dt_in("w_o_diff", (DEPTH, 512, D))
    w_o_dil = dt_in("w_o_dil", (DEPTH, 256, D))
    w_o_mla = dt_in("w_o_mla", (DEPTH, 512, D))
    w_out = dt_in("w_out", (DEPTH, D, D))
    g_mlp = dt_in("g_mlp", (DEPTH, D))
    w_up = dt_in("w_up", (DEPTH, D, 4 * D))
    w_down = dt_in("w_down", (DEPTH, 4 * D, D))
    g_final = dt_in("g_final", (D,))
    rc64 = dt_in("rc64", (128, S))
    rs64 = dt_in("rs64", (128, S))
    rcq = dt_in("rcq", (96, S))
    rsq = dt_in("rsq", (96, S))
    out = nc.dram_tensor("out", [S, D], F32, kind="ExternalOutput").ap()

    def scr(name, shape, dt=BF16):
        kind = "ExternalOutput" if name in dump else "Internal"
        return nc.dram_tensor(name, list(shape), dt, kind=kind).ap()

    xT = scr("xT", (8, 128, S), F32)
    hT = scr("hT", (8, 128, S))
    RP = scr("RP", (NRC, 128, S))
    KR = scr("KR", (32, S))
    VA = scr("VA", (S, 512))
    VB = scr("VB", (3, S, 256))
    CQ = scr("CQ", (6, 128, S))
    CKV = scr("CKV", (2, 128, S))
    G = scr("G", (24, 128, S))
    QC = scr("QC", (8, 96, S))
    KN = scr("KN", (4, 128, S))
    VC = scr("VC", (S, 512))
    OA = scr("OA", (4, 128, S))
    OB = scr("OB", (2, 128, S))
    OC = scr("OC", (4, 128, S))
    H2 = scr("H2", (8, 128, S))
    U = scr("U", (32, 128, S))
    xTv = xT.rearrange("c p s -> p c s")
    hTv = hT.rearrange("c p s -> p c s")

    es = ExitStack()
    k = KB(nc, es)
    stop = [False]

    def done(name):
        if upto == name:
            stop[0] = True
        return stop[0]

    def consts():
        onesf = k.sb("onesf", (128, 128), F32)
        epsb = k.sb("epsb", (128, 1), F32)
        k.op('pool', lambda e: e.memset(onesf[:, :], 1.0), wr=[onesf])
        k.op('pool', lambda e: e.memset(epsb[:, :], EPS), wr=[epsb])
        return onesf, epsb

    def ones_tile(name, val, dt=F32):
        t = k.sb(name, (128, 128), dt)
        k.op('pool', lambda e: e.memset(t[:, :], val), wr=[t])
        return t

    def gvec(name, src1d, n):
        t = k.sb(name, (128, n), F32)
        k.dma('sp', t[:, :], src1d.rearrange("(c p) -> p c", p=128), t, wr=[t], slow=True)
        return t

    def wload(W, c0, src, n):
        sv = src.rearrange("(kc p) n -> p kc n", p=128)
        for a in range(0, n, 2048):
            m = min(2048, n - a)
            k.dma('pool', W[:, :, c0 + a:c0 + a + m], sv[:, :, a:a + m], W, wr=[W])

    def ph_first():
        k.phase_begin()
        onesf, epsb = consts()
        ident = k.sb("ident", (128, 128), F32)
        onesD = ones_tile("onesD", 1.0 / D)
        gt = gvec("gt", g_mix[0], 8)
        xin = [k.sb("xin%d" % i, (128, 4, D), F32) for i in range(2)]
        xTb = [k.sb("xTb%d" % i, (128, 8, TB), F32) for i in range(2)]
        hTb = [k.sb("hTb%d" % i, (128, 8, TB), BF16) for i in range(2)]
        sq = k.sb("sq", (128, 8, TB), F32)
        rstd = k.sb("rstd", (128, TB), F32)
        tmp = k.sb("tmp", (128, TB), F32)
        tp = [k.ps("tp%d" % i) for i in range(4)]
        msps = k.ps("msps")
        k.op('pool', lambda e: e.affine_select(out=ident[:, :], in_=onesf[:, :], pattern=[[-1, 128]],
                                               compare_op=ALU.is_equal, fill=0.0, base=0, channel_multiplier=1),
             rd=[onesf], wr=[ident])
        xv = x.rearrange("(b j p) d -> b p j d", p=128, j=4)
        k.dma('sp', xin[0][:, :, :], xv[0], xin[0], wr=[xin[0]])
        for blk in range(NTB):
            xi = xin[blk % 2]
            xo = xTb[blk % 2]
            ho = hTb[blk % 2]
            if blk + 1 < NTB:
                nx = xin[(blk + 1) % 2]
                k.dma('sp', nx[:, :, :], xv[blk + 1], nx, wr=[nx])
            for fc in range(8):
                p = tp[fc % 4]
                for j in range(4):
                    k.op('pe', lambda e: e.transpose(p[:, j * 128:(j + 1) * 128], xi[:, j, fc * 128:(fc + 1) * 128],
                                                     ident[:, :]), rd=[xi, ident], wr=[p], sig=(j == 3))
                if fc % 2 == 0:
                    k.op('act', lambda e: e.copy(out=xo[:, fc, :], in_=p[:, :]), rd=[p], wr=[xo])
                else:
                    k.op('dve', lambda e: e.tensor_copy(out=xo[:, fc, :], in_=p[:, :]), rd=[p], wr=[xo])
            norm_block(k, xo, 8, gt, 0, onesD, sq, msps, rstd, tmp, lambda c: (ho, ho[:, c, :]), epsb)
            sl = slice(blk * TB, (blk + 1) * TB)
            k.dma('sp', xTv[:, :, sl], xo[:, :, :], xo, rd=[xo])
            k.dma('sp', hTv[:, :, sl], ho[:, :, :], ho, rd=[ho])
        k.phase_end()

    def ph_rope(l):
        k.phase_begin()
        NW = 2592
        W = k.sb("W", (128, 8, NW), BF16)
        Wsw = k.sb("Wsw", (128, 8, NW), BF16)
        wload(W, 0, w_in[l][:, 0:1024], 1024)
        for g in range(3):
            wload(W, 1024 + g * 512, w_in[l][:, C_DIL + g * 768:C_DIL + g * 768 + 512], 512)
        wload(W, 2560, w_in[l][:, C_KR:C_KR + 32], 32)
        for kc in range(8):
            src = W[:, kc, 0:2560].rearrange("p (h t d) -> p h t d", t=2, d=32)
            dst = Wsw[:, kc, 0:2560].rearrange("p (h t d) -> p h t d", t=2, d=32)
            eng = 'dve' if kc % 2 == 0 else 'pool'
            k.op(eng, lambda e: e.tensor_copy(out=dst[:, :, 0, :], in_=src[:, :, 1, :]), rd=[W], wr=[Wsw])
            k.op(eng, lambda e: e.tensor_copy(out=dst[:, :, 1, :], in_=src[:, :, 0, :]), rd=[W], wr=[Wsw])
        k.op('dve', lambda e: e.tensor_copy(out=Wsw[:, :, 2560:2576], in_=W[:, :, 2576:2592]), rd=[W], wr=[Wsw])
        k.op('dve', lambda e: e.tensor_copy(out=Wsw[:, :, 2576:2592], in_=W[:, :, 2560:2576]), rd=[W], wr=[Wsw])
        hb = [k.sb("hb%d" % i, (128, 8, TB), BF16) for i in range(2)]
        ct = [k.sb("ct%d" % i, (128, TB), F32) for i in range(2)]
        st = [k.sb("st%d" % i, (128, TB), F32) for i in range(2)]
        ck = [k.sb("ck%d" % i, (32, TB), F32) for i in range(2)]
        sk = [k.sb("sk%d" % i, (32, TB), F32) for i in range(2)]
        psA = [k.ps("pa%d" % i) for i in range(3)]
        psB = [k.ps("pb%d" % i) for i in range(3)]
        t1 = [k.sb("t1_%d" % i, (128, TB), F32) for i in range(2)]
        t2 = [k.sb("t2_%d" % i, (128, TB), F32) for i in range(2)]
        og = [k.sb("og%d" % i, (128, TB), BF16) for i in range(4)]

        def load_blk(b):
            sl = slice(b * TB, (b + 1) * TB)
            i = b % 2
            k.dma('sp', hb[i][:, :, :], hTv[:, :, sl], hb[i], wr=[hb[i]])
            k.dma('sp', ct[i][:, :], rc64[:, sl], ct[i], wr=[ct[i]])
            k.dma('sp', st[i][:, :], rs64[:, sl], st[i], wr=[st[i]])
            k.dma('sp', ck[i][:, :], rcq[64:96, sl], ck[i], wr=[ck[i]])
            k.dma('sp', sk[i][:, :], rsq[64:96, sl], sk[i], wr=[sk[i]])
        load_blk(0)
        n = 0
        for b in range(NTB):
            if b + 1 < NTB:
                load_blk(b + 1)
            i = b % 2
            sl = slice(b * TB, (b + 1) * TB)
            for ci in range(NRC + 1):
                M = 128 if ci < NRC else 32
                c0 = ci * 128
                pa, pb = psA[n % 3], psB[n % 3]
                for kc in range(8):
                    mm(k, pa, pa[0:M, :], W[:, kc, c0:c0 + M], hb[i][:, kc, :], kc == 0, kc == 7, [W, hb[i]])
                for kc in range(8):
                    mm(k, pb, pb[0:M, :], Wsw[:, kc, c0:c0 + M], hb[i][:, kc, :], kc == 0, kc == 7, [Wsw, hb[i]])
                a1, a2, o = t1[n % 2], t2[n % 2], og[n % 4]
                cc, ss = (ct[i], st[i]) if ci < NRC else (ck[i], sk[i])
                k.op('dve', lambda e: e.tensor_tensor(out=a1[0:M, :], in0=pa[0:M, :], in1=cc[0:M, :], op=ALU.mult),
                     rd=[pa, cc], wr=[a1])
                k.op('dve', lambda e: e.tensor_tensor(out=a2[0:M, :], in0=pb[0:M, :], in1=ss[0:M, :], op=ALU.mult),
                     rd=[pb, ss], wr=[a2])
                k.op('pool', lambda e: e.tensor_tensor(out=o[0:M, :], in0=a1[0:M, :], in1=a2[0:M, :], op=ALU.add),
                     rd=[a1, a2], wr=[o])
                dst = RP[ci][:, sl] if ci < NRC else KR[:, sl]
                k.dma('sp', dst, o[0:M, :], o, rd=[o])
                n += 1
        k.phase_end()

    def ph_vcg(l):
        k.phase_begin()
        onesf, epsb = consts()
        Wv = k.sb("Wv", (128, 8, 1280), BF16)
        Wc = k.sb("Wc", (128, 8, 1024), BF16)
        Wg = k.sb("Wg", (128, 8, 3072), BF16)
        wload(Wv, 0, w_in[l][:, C_DV:C_DV + 512], 512)
        for g in range(3):
            wload(Wv, 512 + g * 256, w_in[l][:, C_DIL + g * 768 + 512:C_DIL + g * 768 + 768], 256)
        wload(Wc, 0, w_in[l][:, C_CQ:C_CQ + 1024], 1024)
        wload(Wg, 0, w_in[l][:, C_GATE:C_GATE + 3072], 3072)
        bg = k.sb("bg", (128, 24), F32)
        for i in range(3):
            k.dma('sp', bg[:, i * 8:(i + 1) * 8], b_gate[l][i].rearrange("(c p) -> p c", p=128), bg, wr=[bg], slow=True)
        gcq = gvec("gcq", g_cq[l], 6)
        gckv = gvec("gckv", g_ckv[l], 2)
        o768 = ones_tile("o768", 1.0 / 768)
        o256 = ones_tile("o256", 1.0 / 256)
        hb = [k.sb("hb%d" % i, (128, 8, TB), BF16) for i in range(2)]
        vst = k.sb("vst", (128, 4, 1280), BF16)
        cqf = k.sb("cqf", (128, 6, TB), F32)
        sq = k.sb("sq", (128, 6, TB), F32)
        cqo = k.sb("cqo", (128, 6, TB), BF16)
        ckvf = k.sb("ckvf", (128, 2, TB), F32)
        ckvo = k.sb("ckvo", (128, 2, TB), BF16)
        gst = [k.sb("gst%d" % i, (128, 8, TB), BF16) for i in range(2)]
        rstd = k.sb("rstd", (128, TB), F32)
        tmp = k.sb("tmp", (128, TB), F32)
        pv = [k.ps("pv%d" % i) for i in range(2)]
        pc = [k.ps("pc%d" % i) for i in range(3)]
        msps = k.ps("msps")
        CQv = CQ.rearrange("c p s -> p c s")
        CKVv = CKV.rearrange("c p s -> p c s")
        Gv = G.rearrange("c p s -> p c s")

        def load_blk(b):
            sl = slice(b * TB, (b + 1) * TB)
            k.dma('sp', hb[b % 2][:, :, :], hTv[:, :, sl], hb[b % 2], wr=[hb[b % 2]])
        load_blk(0)
        nv = 0
        npc = 0
        for b in range(NTB):
            if b + 1 < NTB:
                load_blk(b + 1)
            h = hb[b % 2]
            sl = slice(b * TB, (b + 1) * TB)
            for j in range(4):
                for (c0, n) in ((0, 512), (512, 512), (1024, 256)):
                    p = pv[nv % 2]
                    nv += 1
                    for kc in range(8):
                        mm(k, p, p[:, 0:n], h[:, kc, j * 128:(j + 1) * 128], Wv[:, kc, c0:c0 + n], kc == 0, kc == 7, [h, Wv])
                    k.op('act', lambda e: e.copy(out=vst[:, j, c0:c0 + n], in_=p[:, 0:n]), rd=[p], wr=[vst])
            k.dma('sp', VA[sl, :].rearrange("(j p) c -> p j c", p=128), vst[:, :, 0:512], vst, rd=[vst])
            for g in range(3):
                k.dma('sp', VB[g][sl, :].rearrange("(j p) c -> p j c", p=128), vst[:, :, 512 + g * 256:768 + g * 256],
                      vst, rd=[vst])
            for c in range(8):
                p = pc[npc % 3]
                npc += 1
                for kc in range(8):
                    mm(k, p, p[:, :], Wc[:, kc, c * 128:(c + 1) * 128], h[:, kc, :], kc == 0, kc == 7, [Wc, h])
                if c < 6:
                    k.op('dve', lambda e: e.tensor_copy(out=cqf[:, c, :], in_=p[:, :]), rd=[p], wr=[cqf])
                else:
                    k.op('dve', lambda e: e.tensor_copy(out=ckvf[:, c - 6, :], in_=p[:, :]), rd=[p], wr=[ckvf])
            norm_block(k, cqf, 6, gcq, 0, o768, sq, msps, rstd, tmp, lambda c: (cqo, cqo[:, c, :]), epsb)
            k.dma('sp', CQv[:, :, sl], cqo[:, :, :], cqo, rd=[cqo])
            norm_block(k, ckvf, 2, gckv, 0, o256, sq, msps, rstd, tmp, lambda c: (ckvo, ckvo[:, c, :]), epsb)
            k.dma('sp', CKVv[:, :, sl], ckvo[:, :, :], ckvo, rd=[ckvo])
            for gi in range(24):
                p = pc[npc % 3]
                npc += 1
                gs = gst[(gi // 8) % 2]
                for kc in range(8):
                    mm(k, p, p[:, :], Wg[:, kc, gi * 128:(gi + 1) * 128], h[:, kc, :], kc == 0, kc == 7, [Wg, h])
                k.op('act', lambda e: e.activation(out=gs[:, gi % 8, :], in_=p[:, :], func=AF.Sigmoid,
                                                   bias=bg[:, gi:gi + 1], scale=1.0), rd=[p, bg], wr=[gs])
                if gi % 8 == 7:
                    k.dma('sp', Gv[:, gi - 7:gi + 1, sl], gs[:, :, :], gs, rd=[gs])
        k.phase_end()

    def ph_mla_up(l):
        k.phase_begin()
        Wq = k.sb("Wq", (128, 6, 768), BF16)
        Wqs = k.sb("Wqs", (128, 6, 768), BF16)
        Wkn = k.sb("Wkn", (128, 2, 512), BF16)
        Wvv = k.sb("Wvv", (128, 2, 512), BF16)
        wload(Wq, 0, w_uq[l], 768)
        ukv = w_ukv[l].rearrange("(kc p) (h t d) -> p kc h t d", p=128, t=2, d=64)
        for kc in range(2):
            k.dma('pool', Wkn[:, kc, :].rearrange("p (h d) -> p h d", d=64), ukv[:, kc, :, 0, :], Wkn, wr=[Wkn])
            k.dma('pool', Wvv[:, kc, :].rearrange("p (h d) -> p h d", d=64), ukv[:, kc, :, 1, :], Wvv, wr=[Wvv])
        for kc in range(6):
            src = Wq[:, kc, :].rearrange("p (h e) -> p h e", e=96)
            dst = Wqs[:, kc, :].rearrange("p (h e) -> p h e", e=96)
            eng = 'dve' if kc % 2 == 0 else 'pool'
            k.op(eng, lambda e: e.tensor_copy(out=dst[:, :, 0:64], in_=src[:, :, 0:64]), rd=[Wq], wr=[Wqs])
            k.op(eng, lambda e: e.tensor_copy(out=dst[:, :, 64:80], in_=src[:, :, 80:96]), rd=[Wq], wr=[Wqs])
            k.op(eng, lambda e: e.tensor_copy(out=dst[:, :, 80:96], in_=src[:, :, 64:80]), rd=[Wq], wr=[Wqs])
        cqb = [k.sb("cqb%d" % i, (128, 6, TB), BF16) for i in range(2)]
        ckb = [k.sb("ckb%d" % i, (128, 2, TB), BF16) for i in range(2)]
        ct = [k.sb("ct%d" % i, (96, TB), F32) for i in range(2)]
        st = [k.sb("st%d" % i, (96, TB), F32) for i in range(2)]
        psA = [k.ps("pa%d" % i) for i in range(3)]
        psB = [k.ps("pb%d" % i) for i in range(3)]
        t1 = [k.sb("t1_%d" % i, (96, TB), F32) for i in range(2)]
        t2 = [k.sb("t2_%d" % i, (96, TB), F32) for i in range(2)]
        og = [k.sb("og%d" % i, (128, TB), BF16) for i in range(4)]
        vst = k.sb("vst", (128, 4, 512), BF16)
        CQv = CQ.rearrange("c p s -> p c s")
        CKVv = CKV.rearrange("c p s -> p c s")

        def load_blk(b):
            sl = slice(b * TB, (b + 1) * TB)
            i = b % 2
            k.dma('sp', cqb[i][:, :, :], CQv[:, :, sl], cqb[i], wr=[cqb[i]])
            k.dma('sp', ckb[i][:, :, :], CKVv[:, :, sl], ckb[i], wr=[ckb[i]])
            k.dma('sp', ct[i][:, :], rcq[:, sl], ct[i], wr=[ct[i]])
            k.dma('sp', st[i][:, :], rsq[:, sl], st[i], wr=[st[i]])
        load_blk(0)
        n = 0
        no = 0
        for b in range(NTB):
            if b + 1 < NTB:
                load_blk(b + 1)
            i = b % 2
            sl = slice(b * TB, (b + 1) * TB)
            for h in range(8):
                pa, pb = psA[n % 3], psB[n % 3]
                for kc in range(6):
                    mm(k, pa, pa[0:96, :], Wq[:, kc, h * 96:(h + 1) * 96], cqb[i][:, kc, :], kc == 0, kc == 5, [Wq, cqb[i]])
                for kc in range(6):
                    mm(k, pb, pb[0:96, :], Wqs[:, kc, h * 96:(h + 1) * 96], cqb[i][:, kc, :], kc == 0, kc == 5, [Wqs, cqb[i]])
                a1, a2, o = t1[n % 2], t2[n % 2], og[no % 4]
                k.op('dve', lambda e: e.tensor_tensor(out=a1[:, :], in0=pa[0:96, :], in1=ct[i][:, :], op=ALU.mult),
                     rd=[pa, ct[i]], wr=[a1])
                k.op('dve', lambda e: e.tensor_tensor(out=a2[:, :], in0=pb[0:96, :], in1=st[i][:, :], op=ALU.mult),
                     rd=[pb, st[i]], wr=[a2])
                k.op('pool', lambda e: e.tensor_tensor(out=o[0:96, :], in0=a1[:, :], in1=a2[:, :], op=ALU.add),
                     rd=[a1, a2], wr=[o])
                k.dma('sp', QC[h][:, sl], o[0:96, :], o, rd=[o])
                n += 1
                no += 1
            for j in range(4):
                pa = psA[n % 3]
                n += 1
                o = og[no % 4]
                no += 1
                for kc in range(2):
                    mm(k, pa, pa[:, :], Wkn[:, kc, j * 128:(j + 1) * 128], ckb[i][:, kc, :], kc == 0, kc == 1, [Wkn, ckb[i]])
                k.op('act', lambda e: e.copy(out=o[:, :], in_=pa[:, :]), rd=[pa], wr=[o])
                k.dma('sp', KN[j][:, sl], o[:, :], o, rd=[o])
            for jt in range(4):
                pb = psB[n % 3]
                n += 1
                for kc in range(2):
                    mm(k, pb, pb[:, :], ckb[i][:, kc, jt * 128:(jt + 1) * 128], Wvv[:, kc, :], kc == 0, kc == 1, [Wvv, ckb[i]])
                k.op('act', lambda e: e.copy(out=vst[:, jt, :], in_=pb[:, :]), rd=[pb], wr=[vst])
            k.dma('sp', VC[sl, :].rearrange("(j p) c -> p j c", p=128), vst[:, :, :], vst, rd=[vst])
        k.phase_end()

    def ph_diff(l):
        k.phase_begin()
        onesf, epsb = consts()
        onesb = ones_tile("onesb", 1.0, BF16)
        o128 = ones_tile("o128", 1.0 / 128)
        gd = k.sb("gd", (128, 1), F32)
        k.dma('sp', gd[:, :], g_diff[l].rearrange("(p o) -> p o", o=1), gd, wr=[gd], slow=True)
        gdl = k.sb("gdl", (128, 1), F32)
        k.op('dve', lambda e: e.tensor_scalar(out=gdl[:, :], in0=gd[:, :], scalar1=1.0 - LAM_INIT[l], scalar2=None,
                                              op0=ALU.mult), rd=[gd], wr=[gdl])
        dl = k.sb("dl", (1, 256), F32)
        k.dma('sp', dl[:, :], diff_lambda[l].rearrange("(o a) d -> o (a d)", o=1), dl, wr=[dl])
        prod = k.sb("prod", (1, 128), F32)
        sums = k.sb("sums", (1, 2), F32)
        ex = k.sb("ex", (1, 2), F32)
        nl = k.sb("nl", (1, 2), F32)
        nlam = k.sb("nlam", (128, 2), F32)
        k.op('dve', lambda e: e.tensor_tensor(out=prod[0:1, 0:64], in0=dl[0:1, 0:64], in1=dl[0:1, 64:128], op=ALU.mult),
             rd=[dl], wr=[prod])
        k.op('dve', lambda e: e.tensor_tensor(out=prod[0:1, 64:128], in0=dl[0:1, 128:192], in1=dl[0:1, 192:256], op=ALU.mult),
             rd=[dl, prod], wr=[prod])
        k.op('dve', lambda e: e.tensor_reduce(out=sums[0:1, 0:2], in_=prod[0:1, :].rearrange("p (a d) -> p a d", a=2),
                                              axis=mybir.AxisListType.X, op=ALU.add), rd=[prod], wr=[sums])
        k.op('act', lambda e: e.activation(out=ex[0:1, :], in_=sums[0:1, :], func=AF.Exp), rd=[sums], wr=[ex])
        k.op('dve', lambda e: e.tensor_tensor(out=nl[0:1, 0:1], in0=ex[0:1, 1:2], in1=ex[0:1, 0:1], op=ALU.subtract),
             rd=[ex], wr=[nl])
        k.op('dve', lambda e: e.tensor_scalar(out=nl[0:1, 0:1], in0=nl[0:1, 0:1], scalar1=-LAM_INIT[l], scalar2=None,
                                              op0=ALU.add), rd=[nl], wr=[nl])
        k.op('dve', lambda e: e.tensor_copy(out=nl[0:1, 1:2], in_=nl[0:1, 0:1]), rd=[nl], wr=[nl])
        SA = k.ps("SA", (128, 1024))
        SB_ = k.ps("SB", (128, 1024))
        SC = k.ps("SC", (128, 1024))
        Sq = [SA, SB_]
        acc1, acc2 = k.ps("acc1"), k.ps("acc2")
        mm(k, SC, SC[:, 0:2], onesf[0:1, :], nl[0:1, 0:2], True, True, [onesf, nl])
        k.op('dve', lambda e: e.tensor_copy(out=nlam[:, :], in_=SC[:, 0:2]), rd=[SC], wr=[nlam])

        KT = [k.sb("KT%d" % i, (128, S), BF16) for i in range(2)]
        QT = [k.sb("QT%d" % i, (128, S), BF16) for i in range(2)]
        V = [k.sb("V%d" % i, (128, 64, 128), BF16) for i in range(2)]
        NP = 12
        Pb = [k.sb("P%d" % i, (128, 1024), BF16) for i in range(NP)]
        ps1 = [k.sb("ps1_%d" % i, (128, TB), F32) for i in range(2)]
        ps2 = [k.sb("ps2_%d" % i, (128, TB), F32) for i in range(2)]
        a1s = k.sb("a1s", (128, TB), F32)
        a2s = k.sb("a2s", (128, TB), F32)
        r1 = k.sb("r1", (128, TB), F32)
        r2 = k.sb("r2", (128, TB), F32)
        o1 = k.sb("o1", (128, TB), F32)
        o2 = k.sb("o2", (128, TB), F32)
        od = k.sb("od", (128, TB), F32)
        sqd = k.sb("sqd", (128, TB), F32)
        tmp = k.sb("tmp", (128, TB), F32)
        rstd = k.sb("rstd", (128, TB), F32)
        ost = [k.sb("ost%d" % i, (128, TB), BF16) for i in range(2)]

        def load_head(h):
            i = h % 2
            k.dma('sp', KT[i][:, :], RP[4 + h], KT[i], wr=[KT[i]])
            k.dma('sp', QT[i][:, :], RP[h], QT[i], wr=[QT[i]])
            k.dma('sp', V[i][:, :, :], VA[:, h * 128:(h + 1) * 128].rearrange("(kc p) d -> p kc d", p=128), V[i], wr=[V[i]])
        load_head(0)
        sc = [0]
        pc = [0]
        pending = []

        def run_pending(kc):
            keep = []
            for (t, fn) in pending:
                if kc is None or t <= kc:
                    fn()
                else:
                    keep.append((t, fn))
            pending[:] = keep
        nq = 0
        for h in range(4):
            if h + 1 < 4:
                load_head(h + 1)
            i = h % 2
            for qb in range(NTB):
                qsl = slice(qb * TB, (qb + 1) * TB)
                pa1, pa2 = ps1[nq % 2], ps2[nq % 2]

                def qk(kc):
                    ksl = slice(kc * 128, (kc + 1) * 128)
                    s = Sq[sc[0] % 2]
                    sc[0] += 1
                    mm(k, s, s[:, 0:512], KT[i][0:64, ksl], QT[i][0:64, qsl], True, True, [KT[i], QT[i]])
                    mm(k, s, s[:, 512:1024], KT[i][64:128, ksl], QT[i][64:128, qsl], True, True, [KT[i], QT[i]])
                    return s

                def pv(kc, s):
                    p = Pb[pc[0] % NP]
                    pc[0] += 1
                    k.op('act', lambda e: e.activation(out=p[:, :], in_=s[:, :], func=AF.Exp, scale=0.125), rd=[s], wr=[p])
                    st_, sp_ = kc == 0, kc == 63
                    mm(k, acc1, acc1[:, :], V[i][:, kc, :], p[:, 0:512], st_, sp_, [V[i], p])
                    mm(k, acc2, acc2[:, :], V[i][:, kc, :], p[:, 512:1024], st_, sp_, [V[i], p])
                    if kc == 0:
                        k.op('dve', lambda e: e.tensor_copy(out=pa1[:, :], in_=p[:, 0:512]), rd=[p], wr=[pa1])
                        k.op('pool', lambda e: e.tensor_copy(out=pa2[:, :], in_=p[:, 512:1024]), rd=[p], wr=[pa2])
                    else:
                        k.op('dve', lambda e: e.tensor_tensor(out=pa1[:, :], in0=pa1[:, :], in1=p[:, 0:512], op=ALU.add),
                             rd=[p, pa1], wr=[pa1])
                        k.op('pool', lambda e: e.tensor_tensor(out=pa2[:, :], in0=pa2[:, :], in1=p[:, 512:1024], op=ALU.add),
                             rd=[p, pa2], wr=[pa2])
                cur = qk(0)
                for kc in range(64):
                    nxt = qk(kc + 1) if kc + 1 < 64 else None
                    pv(kc, cur)
                    cur = nxt
                    run_pending(kc)
                k.op('dve', lambda e: e.tensor_copy(out=a1s[:, :], in_=acc1[:, :]), rd=[acc1], wr=[a1s])
                k.op('dve', lambda e: e.tensor_copy(out=a2s[:, :], in_=acc2[:, :]), rd=[acc2], wr=[a2s])
                o = ost[nq % 2]
                nq += 1

                def ep1(pa1=pa1, pa2=pa2):
                    mm(k, SC, SC[:, 0:512], onesf[:, :], pa1[:, :], True, True, [onesf, pa1])
                    mm(k, SC, SC[:, 512:1024], onesf[:, :], pa2[:, :], True, True, [onesf, pa2])
                    k.op('dve', lambda e: e.reciprocal(out=r1[:, :], in_=SC[:, 0:512]), rd=[SC], wr=[r1])
                    k.op('dve', lambda e: e.tensor_tensor(out=o1[:, :], in0=a1s[:, :], in1=r1[:, :], op=ALU.mult), rd=[a1s, r1], wr=[o1])
                    k.op('dve', lambda e: e.reciprocal(out=r2[:, :], in_=SC[:, 512:1024]), rd=[SC], wr=[r2])
                    k.op('dve', lambda e: e.tensor_tensor(out=o2[:, :], in0=a2s[:, :], in1=r2[:, :], op=ALU.mult), rd=[a2s, r2], wr=[o2])
                    k.op('dve', lambda e: e.scalar_tensor_tensor(out=od[:, :], in0=o2[:, :], scalar=nlam[:, 0:1], in1=o1[:, :],
                                                                 op0=ALU.mult, op1=ALU.add), rd=[o2, o1, nlam], wr=[od])
                    k.op('pool', lambda e: e.tensor_tensor(out=sqd[:, :], in0=od[:, :], in1=od[:, :], op=ALU.mult), rd=[od], wr=[sqd])

                def ep2(o=o, h=h, qsl=qsl):
                    mm(k, SC, SC[:, 0:512], o128[:, :], sqd[:, :], True, True, [o128, sqd])
                    k.op('act', lambda e: e.activation(out=tmp[:, :], in_=SC[:, 0:512], func=AF.Ln, bias=epsb[:, 0:1], scale=1.0),
                         rd=[SC, epsb], wr=[tmp])
                    k.op('act', lambda e: e.activation(out=rstd[:, :], in_=tmp[:, :], func=AF.Exp, scale=-0.5), rd=[tmp], wr=[rstd])
                    k.op('dve', lambda e: e.scalar_tensor_tensor(out=o[:, :], in0=od[:, :], scalar=gdl[:, 0:1], in1=rstd[:, :],
                                                                 op0=ALU.mult, op1=ALU.mult), rd=[od, gdl, rstd], wr=[o])
                    k.dma('sp', OA[h][:, qsl], o[:, :], o, rd=[o])
                pending.append((3, ep1))
                pending.append((30, ep2))
        run_pending(None)
        k.phase_end()

    def ph_mla(l):
        k.phase_begin()
        KT = [k.sb("KT%d" % i, (128, S), BF16) for i in range(2)]
        QT = [k.sb("QT%d" % i, (128, S), BF16) for i in range(2)]
        V = [k.sb("V%d" % i, (128, 64, 128), BF16) for i in range(2)]
        for i in range(2):
            k.op('pool', lambda e: e.memset(V[i][:, :, 64:128], 1.0), wr=[V[i]])
        Pb = [k.sb("P%d" % i, (128, 1024), BF16) for i in range(4)]
        Sps = [k.ps("S%d" % i, (128, 1024)) for i in range(3)]
        acc = [k.ps("acc%d" % i) for i in range(2)]
        rr = k.sb("rr", (64, TB), F32)
        ost = [k.sb("ost%d" % i, (64, TB), BF16) for i in range(2)]
        sc_ = 96 ** -0.5

        def load_head(h):
            i = h % 2
            b0 = 64 * (h % 2)
            k.dma('sp', KT[i][0:64, :], KN[h // 2][b0:b0 + 64, :], KT[i], wr=[KT[i]])
            k.dma('sp', KT[i][64:96, :], KR[:, :], KT[i], wr=[KT[i]])
            k.dma('sp', QT[i][0:96, :], QC[h], QT[i], wr=[QT[i]])
            k.dma('sp', V[i][:, :, 0:64], VC[:, h * 64:(h + 1) * 64].rearrange("(kc p) d -> p kc d", p=128), V[i], wr=[V[i]])
        load_head(0)
        sc = [0]
        pc = [0]
        nq = 0
        for h in range(8):
            if h + 1 < 8:
                load_head(h + 1)
            i = h % 2
            b0 = 64 * (h % 2)
            for qb in range(NTB):
                qsl = slice(qb * TB, (qb + 1) * TB)
                a = acc[nq % 2]

                def qk(j):
                    s1 = Sps[sc[0] % 3]
                    sc[0] += 1
                    for t in range(2):
                        kc = 2 * j + t
                        mm(k, s1, s1[:, t * 512:(t + 1) * 512], KT[i][0:96, kc * 128:(kc + 1) * 128], QT[i][0:96, qsl],
                           True, True, [KT[i], QT[i]])
                    return s1

                def pv(j, s1):
                    p1 = Pb[pc[0] % 4]
                    pc[0] += 1
                    k.op('act', lambda e: e.activation(out=p1[:, :], in_=s1[:, :], func=AF.Exp, scale=sc_), rd=[s1], wr=[p1])
                    for t in range(2):
                        kc = 2 * j + t
                        mm(k, a, a[:, :], V[i][:, kc, :], p1[:, t * 512:(t + 1) * 512], kc == 0, kc == 63, [V[i], p1])
                ss = {0: qk(0), 1: qk(1)}
                for j in range(32):
                    if j + 2 < 32:
                        ss[j + 2] = qk(j + 2)
                    pv(j, ss.pop(j))
                o = ost[nq % 2]
                nq += 1
                k.op('dve', lambda e: e.reciprocal(out=rr[:, :], in_=a[64:128, :]), rd=[a], wr=[rr])
                k.op('dve', lambda e: e.tensor_tensor(out=o[:, :], in0=a[0:64, :], in1=rr[:, :], op=ALU.mult), rd=[a, rr], wr=[o])
                k.dma('sp', OC[h // 2][b0:b0 + 64, qsl], o[:, :], o, rd=[o])
        k.phase_end()

    def ph_dil(l):
        k.phase_begin()
        PADM = 1024
        onesb = ones_tile("onesb", 1.0, BF16)
        M0 = k.sb("M0", (128, 2, 128), BF16)
        M = k.sb("M", (128, 2, 128), BF16)
        Mf = k.sb("Mf", (128, 2, 128), BF16)
        Ml = k.sb("Ml", (128, 2, 128), BF16)
        for j in range(2):
            k.op('pool', lambda e: e.affine_select(out=M0[:, j, :], in_=onesb[:, :], pattern=[[-1, 128]], compare_op=ALU.is_ge,
                                                   fill=0.0, base=128 * j, channel_multiplier=1), rd=[onesb], wr=[M0])
            k.op('pool', lambda e: e.affine_select(out=M[:, j, :], in_=M0[:, j, :], pattern=[[1, 128]], compare_op=ALU.is_ge,
                                                   fill=0.0, base=128 - 128 * j, channel_multiplier=-1), rd=[M0], wr=[M])
        k.op('pool', lambda e: e.affine_select(out=Mf[:, 0, :], in_=M[:, 0, :], pattern=[[0, 128]], compare_op=ALU.is_ge,
                                               fill=0.0, base=-64, channel_multiplier=1), rd=[M], wr=[Mf])
        k.op('pool', lambda e: e.tensor_copy(out=Mf[:, 1, :], in_=M[:, 1, :]), rd=[M], wr=[Mf])
        k.op('pool', lambda e: e.affine_select(out=Ml[:, 1, :], in_=M[:, 1, :], pattern=[[0, 128]], compare_op=ALU.is_ge,
                                               fill=0.0, base=63, channel_multiplier=-1), rd=[M], wr=[Ml])
        k.op('pool', lambda e: e.tensor_copy(out=Ml[:, 0, :], in_=M[:, 0, :]), rd=[M], wr=[Ml])
        QT = [k.sb("QT%d" % i, (64, S), BF16) for i in range(2)]
        KT = [k.sb("KT%d" % i, (64, S + 2 * PADM), BF16) for i in range(2)]
        Vg = [k.sb("Vg%d" % i, (128, 80, 128), BF16) for i in range(2)]
        for i in range(2):
            k.op('pool', lambda e: e.memset(KT[i][:, 0:PADM], 0.0), wr=[KT[i]])
            k.op('pool', lambda e: e.memset(KT[i][:, PADM + S:PADM + S + PADM], 0.0), wr=[KT[i]])
            k.op('pool', lambda e: e.memset(Vg[i][:, :, 0:64], 0.0), wr=[Vg[i]])
            k.op('pool', lambda e: e.memset(Vg[i][:, :, 64:128], 1.0), wr=[Vg[i]])
        tot = k.sb("tot", (128, S), F32)
        rr = k.sb("rr", (64, 2048), F32)
        ost = [k.sb("ost%d" % i, (64, 2048), BF16) for i in range(2)]
        Pb = [k.sb("P%d" % i, (128, 1024), BF16) for i in range(3)]
        Pm = [k.sb("Pm%d" % i, (128, 1024), BF16) for i in range(3)]
        Sps = [k.ps("S%d" % i, (128, 1024)) for i in range(2)]
        acc = [k.ps("acc%d" % i) for i in range(3)]
        MB = {}
        for nm, parts in (('m', (M, M, M, M)), ('f', (Mf, M, M, M)), ('l', (M, M, M, Ml)), ('fl', (Mf, M, M, Ml))):
            t = k.sb("MB" + nm, (128, 4, 256), BF16)
            for uu, src in enumerate(parts):
                k.op('pool', lambda e: e.tensor_copy(out=t[:, uu, :], in_=src[:, :, :].rearrange("p a b -> p (a b)")), rd=[src], wr=[t])
            MB[nm] = t
        DIL = (1, 4, 16)
        jobs = [(h, g) for h in range(4) for g in range(3)]

        def load_job(ji):
            h, g = jobs[ji]
            i = ji % 2
            d = DIL[g]
            n = S // d
            nc_ = n // 128 + 1
            b0 = 64 * (h % 2)
            k.dma('sp', QT[i][:, :], RP[8 + 4 * g + h // 2][b0:b0 + 64, :], QT[i], wr=[QT[i]])
            k.dma('sp', KT[i][:, PADM:PADM + S], RP[8 + 4 * g + 2 + h // 2][b0:b0 + 64, :], KT[i], wr=[KT[i]])
            src = VB[g][:, h * 64:(h + 1) * 64].rearrange("(t c) e -> c t e", c=d)
            for c in range(d):
                base = c * nc_
                k.dma('sp', Vg[i][:, base + 1:base + n // 128, 0:64],
                      src[c][64:n - 64, :].rearrange("(m p) e -> p m e", p=128), Vg[i], wr=[Vg[i]])
                k.dma('sp', Vg[i][64:128, base, 0:64], src[c][0:64, :], Vg[i], wr=[Vg[i]])
                k.dma('sp', Vg[i][0:64, base + n // 128, 0:64], src[c][n - 64:n, :], Vg[i], wr=[Vg[i]])
        load_job(0)
        un = 0
        no = 0
        for ji, (h, g) in enumerate(jobs):
            if ji + 1 < len(jobs):
                load_job(ji + 1)
            i = ji % 2
            d = DIL[g]
            n = S // d
            nu = n // 128
            nc_ = nu + 1
            nb = nu // 4
            for c in range(d):
                for ub in range(nb):
                    sx = Sps[un % 2]
                    pb_, pm_, a = Pb[un % 3], Pm[un % 3], acc[un % 3]
                    for uu in range(4):
                        tq0 = 128 * (4 * ub + uu)
                        q0 = tq0 * d + c
                        qcols = slice(q0, q0 + 127 * d + 1, d)
                        for j in range(2):
                            ks = PADM + (tq0 - 64 + 128 * j) * d + c
                            kcols = slice(ks, ks + 127 * d + 1, d)
                            o_ = uu * 256 + j * 128
                            mm(k, sx, sx[:, o_:o_ + 128], KT[i][:, kcols], QT[i][:, qcols], True, True, [KT[i], QT[i]])
                    k.op('act', lambda e: e.activation(out=pb_[:, :], in_=sx[:, :], func=AF.Exp, scale=0.125), rd=[sx], wr=[pb_])
                    mk = MB['fl' if nb == 1 else ('f' if ub == 0 else ('l' if ub == nb - 1 else 'm'))]
                    eng = 'dve' if un % 2 == 0 else 'pool'
                    k.op(eng, lambda e: e.tensor_tensor(out=pm_[:, :], in0=pb_[:, :], in1=mk[:, :, :].rearrange("p a b -> p (a b)"),
                                                        op=ALU.mult), rd=[pb_, mk], wr=[pm_])
                    for uu in range(4):
                        u = 4 * ub + uu
                        for j in range(2):
                            o_ = uu * 256 + j * 128
                            mm(k, a, a[:, uu * 128:(uu + 1) * 128], Vg[i][:, c * nc_ + u + j, :], pm_[:, o_:o_ + 128],
                               j == 0, j == 1, [Vg[i], pm_])
                    q0 = 512 * ub * d + c
                    dst = tot[:, q0:q0 + 511 * d + 1:d]
                    if g == 0:
                        k.op('act', lambda e: e.copy(out=dst, in_=a[:, :]), rd=[a], wr=[tot])
                    else:
                        k.op('dve', lambda e: e.tensor_tensor(out=dst, in0=a[:, :], in1=dst, op=ALU.add), rd=[a, tot], wr=[tot])
                    un += 1
            if g == 2:
                b0 = 64 * (h % 2)
                for r in range(4):
                    rs = slice(r * 2048, (r + 1) * 2048)
                    o = ost[no % 2]
                    no += 1
                    k.op('dve', lambda e: e.reciprocal(out=rr[:, :], in_=tot[64:128, rs]), rd=[tot], wr=[rr])
                    k.op('dve', lambda e: e.tensor_tensor(out=o[:, :], in0=tot[0:64, rs], in1=rr[:, :], op=ALU.mult),
                         rd=[tot, rr], wr=[o])
                    k.dma('sp', OB[h // 2][b0:b0 + 64, rs], o[:, :], o, rd=[o])
        k.phase_end()

    def ph_out(l):
        k.phase_begin()
        onesf, epsb = consts()
        onesD = ones_tile("onesD", 1.0 / D)
        Woa = k.sb("Woa", (128, 4, D), BF16)
        Wob = k.sb("Wob", (128, 2, D), BF16)
        Woc = k.sb("Woc", (128, 4, D), BF16)
        Wo = k.sb("Wo", (128, 8, D), BF16)
        wload(Woa, 0, w_o_diff[l], D)
        wload(Wob, 0, w_o_dil[l], D)
        wload(Woc, 0, w_o_mla[l], D)
        wload(Wo, 0, w_out[l], D)
        gm = gvec("gm", g_mlp[l], 8)
        oab = [k.sb("oab%d" % i, (128, 4, TB), BF16) for i in range(2)]
        obb = [k.sb("obb%d" % i, (128, 2, TB), BF16) for i in range(2)]
        ocb = [k.sb("ocb%d" % i, (128, 4, TB), BF16) for i in range(2)]
        gb = [k.sb("gb%d" % i, (128, 24, TB), BF16) for i in range(2)]
        xb = [k.sb("xb%d" % i, (128, 8, TB), F32) for i in range(2)]
        mg = k.sb("mg", (128, 8, TB), BF16)
        m1 = [k.sb("m1_%d" % i, (128, TB), F32) for i in range(2)]
        m2 = [k.sb("m2_%d" % i, (128, TB), F32) for i in range(2)]
        m3 = [k.sb("m3_%d" % i, (128, TB), F32) for i in range(2)]
        sq = k.sb("sq", (128, 8, TB), F32)
        rstd = k.sb("rstd", (128, TB), F32)
        tmp = k.sb("tmp", (128, TB), F32)
        h2b = [k.sb("h2b%d" % i, (128, 8, TB), BF16) for i in range(2)]
        pa = [k.ps("pa%d" % i) for i in range(2)]
        pb = [k.ps("pb%d" % i) for i in range(2)]
        pc = [k.ps("pc%d" % i) for i in range(2)]
        msps = k.ps("msps")
        OAv = OA.rearrange("c p s -> p c s")
        OBv = OB.rearrange("c p s -> p c s")
        OCv = OC.rearrange("c p s -> p c s")
        Gv = G.rearrange("c p s -> p c s")
        H2v = H2.rearrange("c p s -> p c s")

        def load_blk(b):
            sl = slice(b * TB, (b + 1) * TB)
            i = b % 2
            k.dma('sp', oab[i][:, :, :], OAv[:, :, sl], oab[i], wr=[oab[i]])
            k.dma('sp', obb[i][:, :, :], OBv[:, :, sl], obb[i], wr=[obb[i]])
            k.dma('sp', ocb[i][:, :, :], OCv[:, :, sl], ocb[i], wr=[ocb[i]])
            k.dma('sp', gb[i][:, :, :], Gv[:, :, sl], gb[i], wr=[gb[i]])
            k.dma('sp', xb[i][:, :, :], xTv[:, :, sl], xb[i], wr=[xb[i]])
        load_blk(0)
        n = 0
        for b in range(NTB):
            if b + 1 < NTB:
                load_blk(b + 1)
            i = b % 2
            sl = slice(b * TB, (b + 1) * TB)
            for oc in range(8):
                cs = slice(oc * 128, (oc + 1) * 128)
                a, bb, c = pa[n % 2], pb[n % 2], pc[n % 2]
                for kc in range(4):
                    mm(k, a, a[:, :], Woa[:, kc, cs], oab[i][:, kc, :], kc == 0, kc == 3, [Woa, oab[i]])
                for kc in range(2):
                    mm(k, bb, bb[:, :], Wob[:, kc, cs], obb[i][:, kc, :], kc == 0, kc == 1, [Wob, obb[i]])
                for kc in range(4):
                    mm(k, c, c[:, :], Woc[:, kc, cs], ocb[i][:, kc, :], kc == 0, kc == 3, [Woc, ocb[i]])
                x1, x2, x3 = m1[n % 2], m2[n % 2], m3[n % 2]
                k.op('dve', lambda e: e.tensor_tensor(out=x1[:, :], in0=a[:, :], in1=gb[i][:, oc, :], op=ALU.mult), rd=[a, gb[i]], wr=[x1])
                k.op('dve', lambda e: e.tensor_tensor(out=x2[:, :], in0=bb[:, :], in1=gb[i][:, 8 + oc, :], op=ALU.mult), rd=[bb, gb[i]], wr=[x2])
                k.op('dve', lambda e: e.tensor_tensor(out=x3[:, :], in0=c[:, :], in1=gb[i][:, 16 + oc, :], op=ALU.mult), rd=[c, gb[i]], wr=[x3])
                k.op('pool', lambda e: e.tensor_tensor(out=x1[:, :], in0=x1[:, :], in1=x2[:, :], op=ALU.add), rd=[x1, x2], wr=[x1])
                k.op('pool', lambda e: e.tensor_tensor(out=mg[:, oc, :], in0=x1[:, :], in1=x3[:, :], op=ALU.add), rd=[x1, x3], wr=[mg])
                n += 1
            for oc in range(8):
                cs = slice(oc * 128, (oc + 1) * 128)
                a = pa[n % 2]
                n += 1
                for kc in range(8):
                    mm(k, a, a[:, :], Wo[:, kc, cs], mg[:, kc, :], kc == 0, kc == 7, [Wo, mg])
                k.op('dve', lambda e: e.tensor_tensor(out=xb[i][:, oc, :], in0=a[:, :], in1=xb[i][:, oc, :], op=ALU.add),
                     rd=[a, xb[i]], wr=[xb[i]])
            ho = h2b[i]
            norm_block(k, xb[i], 8, gm, 0, onesD, sq, msps, rstd, tmp, lambda c: (ho, ho[:, c, :]), epsb)
            k.dma('sp', xTv[:, :, sl], xb[i][:, :, :], xb[i], rd=[xb[i]])
            k.dma('sp', H2v[:, :, sl], ho[:, :, :], ho, rd=[ho])
        k.phase_end()

    def ph_up(l):
        k.phase_begin()
        Wu = k.sb("Wu", (128, 8, 4 * D), BF16)
        wload(Wu, 0, w_up[l], 4 * D)
        hb = [k.sb("hb%d" % i, (128, 8, TB), BF16) for i in range(2)]
        ust = [k.sb("ust%d" % i, (128, 8, TB), BF16) for i in range(2)]
        sqv = [k.sb("sqv%d" % i, (128, TB), F32) for i in range(2)]
        ps = [k.ps("ps%d" % i) for i in range(4)]
        H2v = H2.rearrange("c p s -> p c s")
        Uv = U.rearrange("c p s -> p c s")

        def load_blk(b):
            sl = slice(b * TB, (b + 1) * TB)
            k.dma('sp', hb[b % 2][:, :, :], H2v[:, :, sl], hb[b % 2], wr=[hb[b % 2]])
        load_blk(0)
        n = 0
        for b in range(NTB):
            if b + 1 < NTB:
                load_blk(b + 1)
            h = hb[b % 2]
            sl = slice(b * TB, (b + 1) * TB)
            for uc in range(32):
                p = ps[n % 4]
                sv = sqv[n % 2]
                n += 1
                us = ust[(uc // 8) % 2]
                for kc in range(8):
                    mm(k, p, p[:, :], Wu[:, kc, uc * 128:(uc + 1) * 128], h[:, kc, :], kc == 0, kc == 7, [Wu, h])
                k.op('act', lambda e: e.activation(out=sv[:, :], in_=p[:, :], func=AF.Square), rd=[p], wr=[sv])
                k.op('dve', lambda e: e.scalar_tensor_tensor(out=us[:, uc % 8, :], in0=p[:, :], scalar=0.0, in1=sv[:, :],
                                                             op0=ALU.is_gt, op1=ALU.mult), rd=[p, sv], wr=[us])
                if uc % 8 == 7:
                    k.dma('sp', Uv[:, uc - 7:uc + 1, sl], us[:, :, :], us, rd=[us])
        k.phase_end()

    def ph_down(l):
        last = (l == DEPTH - 1)
        k.phase_begin()
        onesf, epsb = consts()
        onesD = ones_tile("onesD", 1.0 / D)
        Wd = k.sb("Wd", (128, 32, D), BF16)
        wload(Wd, 0, w_down[l], D)
        gn = gvec("gn", g_final if last else g_mix[l + 1], 8)
        ub = [k.sb("ub%d" % i, (128, 32, TB), BF16) for i in range(2)]
        xb = [k.sb("xb%d" % i, (128, 8, TB), F32) for i in range(2)]
        sq = k.sb("sq", (128, 8, TB), F32)
        rstd = k.sb("rstd", (128, TB), F32)
        tmp = k.sb("tmp", (128, TB), F32)
        ps = [k.ps("ps%d" % i) for i in range(3)]
        msps = k.ps("msps")
        Uv = U.rearrange("c p s -> p c s")
        if last:
            ident = k.sb("ident", (128, 128), F32)
            k.op('pool', lambda e: e.affine_select(out=ident[:, :], in_=onesf[:, :], pattern=[[-1, 128]],
                                                   compare_op=ALU.is_equal, fill=0.0, base=0, channel_multiplier=1),
                 rd=[onesf], wr=[ident])
            yb = k.sb("yb", (128, 8, TB), F32)
            yo = [k.sb("yo%d" % i, (128, D), F32) for i in range(2)]
            tp = [k.ps("tp%d" % i) for i in range(2)]
            outv = out.rearrange("(b j p) d -> b j p d", p=128, j=4)
        else:
            hTo = [k.sb("hTo%d" % i, (128, 8, TB), BF16) for i in range(2)]

        def load_blk(b):
            sl = slice(b * TB, (b + 1) * TB)
            i = b % 2
            k.dma('sp', ub[i][:, :, :], Uv[:, :, sl], ub[i], wr=[ub[i]])
            k.dma('sp', xb[i][:, :, :], xTv[:, :, sl], xb[i], wr=[xb[i]])
        load_blk(0)
        n = 0
        ny = 0
        for b in range(NTB):
            if b + 1 < NTB:
                load_blk(b + 1)
            i = b % 2
            sl = slice(b * TB, (b + 1) * TB)
            for oc in range(8):
                p = ps[n % 3]
                n += 1
                for kc in range(32):
                    mm(k, p, p[:, :], Wd[:, kc, oc * 128:(oc + 1) * 128], ub[i][:, kc, :], kc == 0, kc == 31, [Wd, ub[i]])
                k.op('dve', lambda e: e.tensor_tensor(out=xb[i][:, oc, :], in0=p[:, :], in1=xb[i][:, oc, :], op=ALU.add),
                     rd=[p, xb[i]], wr=[xb[i]])
            if not last:
                ho = hTo[i]
                norm_block(k, xb[i], 8, gn, 0, onesD, sq, msps, rstd, tmp, lambda c: (ho, ho[:, c, :]), epsb)
                k.dma('sp', xTv[:, :, sl], xb[i][:, :, :], xb[i], rd=[xb[i]])
                k.dma('sp', hTv[:, :, sl], ho[:, :, :], ho, rd=[ho])
            else:
                norm_block(k, xb[i], 8, gn, 0, onesD, sq, msps, rstd, tmp, lambda c: (yb, yb[:, c, :]), epsb)
                for j in range(4):
                    y = yo[ny % 2]
                    ny += 1
                    for half in range(2):
                        t = tp[half]
                        for q in range(4):
                            fc = half * 4 + q
                            k.op('pe', lambda e: e.transpose(t[:, q * 128:(q + 1) * 128], yb[:, fc, j * 128:(j + 1) * 128],
                                                             ident[:, :]), rd=[yb, ident], wr=[t], sig=(q == 3))
                        if half == 0:
                            k.op('act', lambda e: e.copy(out=y[:, 0:512], in_=t[:, :]), rd=[t], wr=[y])
                        else:
                            k.op('dve', lambda e: e.tensor_copy(out=y[:, 512:1024], in_=t[:, :]), rd=[t], wr=[y])
                    k.dma('sp', outv[b][j], y[:, :], y, rd=[y])
        k.phase_end()

    def run_all():
        ph_first()
        if done('first'):
            return
        for l in layers:
            ph_rope(l)
            if done('rope%d' % l):
                return
            ph_vcg(l)
            if done('vcg%d' % l):
                return
            ph_mla_up(l)
            if done('mlaup%d' % l):
                return
            ph_diff(l)
            if done('diff%d' % l):
                return
            ph_mla(l)
            if done('mla%d' % l):
                return
            ph_dil(l)
            if done('dil%d' % l):
                return
            ph_out(l)
            if done('out%d' % l):
                return
            ph_up(l)
            if done('up%d' % l):
                return
            ph_down(l)
            if done('down%d' % l):
                return
    run_all()
    es.close()
    return nc


def host_tables():
    pos = np.arange(S, dtype=np.float32)

    def tab(half):
        inv = (np.float32(10000.0) ** (-np.arange(half, dtype=np.float32) / np.float32(half))).astype(np.float32)
        ang = (pos[None, :] * inv[:, None]).astype(np.float32)
        return np.cos(ang).astype(np.float32), np.sin(ang).astype(np.float32)
    c32, s32 = tab(32)
    c16, s16 = tab(16)
    rc64 = np.concatenate([c32, c32, c32, c32], 0)
    rs64 = np.concatenate([-s32, s32, -s32, s32], 0)
    rcq = np.concatenate([np.ones((64, S), np.float32), c16, c16], 0)
    rsq = np.concatenate([np.zeros((64, S), np.float32), -s16, s16], 0)
    return dict(rc64=np.ascontiguousarray(rc64), rs64=np.ascontiguousarray(rs64),
                rcq=np.ascontiguousarray(rcq), rsq=np.ascontiguousarray(rsq))


def kernel(**inputs):
    nc = bass.Bass("TRN2", target_bir_lowering=False)
    build(nc)
    shared = {k: np.ascontiguousarray(np.asarray(v, dtype=np.float32)) for k, v in inputs.items() if k != 'x'}
    shared.update(host_tables())
    x = np.asarray(inputs['x'], dtype=np.float32)
    in_maps = [dict(shared, x=np.ascontiguousarray(x[c])) for c in range(8)]
    res = run_bass_kernel_spmd(nc, in_maps, core_ids=list(range(8)))
    return np.stack([np.asarray(r['out'], dtype=np.float32) for r in res.results], 0)
```

```python
import numpy as np
from contextlib import ExitStack
import concourse.bass as bass
import concourse.mybir as mybir
from concourse.bass_utils import run_bass_kernel_spmd

F32 = mybir.dt.float32
BF16 = mybir.dt.bfloat16
ALU = mybir.AluOpType
AF = mybir.ActivationFunctionType

S = 8192
D = 1024
TB = 512
NTB = S // TB
DEPTH = 2
EPS = 1e-6
IN_COLS = 7968
C_DQ, C_DK, C_DV = 0, 1024, 2048
C_DIL = 2560
C_CQ, C_CKV, C_KR, C_GATE = 4864, 5632, 5888, 5920
ENGS = ('pe', 'act', 'dve', 'pool', 'sp')


class Buf:
    __slots__ = ('name', 'lw', 'rd', 'dkey', 't')

    def __init__(self, name, t=None):
        self.name = name
        self.lw = None
        self.rd = {}
        self.dkey = None
        self.t = t

    def __getitem__(self, idx):
        return self.t[idx]


class KB:
    def __init__(self, nc, es, n_dsem=40):
        self.nc = nc
        self.e = dict(pe=nc.tensor, act=nc.scalar, dve=nc.vector, pool=nc.gpsimd, sp=nc.sync)
        self.h = {}
        self.cnt = {}
        for k in ENGS:
            self.h[('e', k)] = es.enter_context(nc.semaphore('sem_' + k))
            self.cnt[('e', k)] = 0
        self.seen = {k: {} for k in ENGS}
        self.bar = es.enter_context(nc.semaphore('bar'))
        self.barn = 0
        self.dpool = []
        for i in range(n_dsem):
            key = ('d', i)
            self.h[key] = es.enter_context(nc.semaphore('dsem%d' % i))
            self.cnt[key] = 0
            self.dpool.append(key)
        self.dfree = list(self.dpool)
        self.bufs = []
        self.pes = None
        self.uid = 0

    def phase_begin(self):
        self.pes = ExitStack()
        self.bufs = []
        self.dfree = list(self.dpool)

    def phase_end(self):
        self.barrier()
        self.pes.close()
        self.pes = None

    def sb(self, name, shape, dt):
        self.uid += 1
        name = "%s_u%d" % (name, self.uid)
        t = self.pes.enter_context(self.nc.sbuf_tensor(name, list(shape), dt))
        b = Buf(name, t)
        self.bufs.append(b)
        return b

    def ps(self, name, shape=(128, 512), dt=F32):
        self.uid += 1
        name = "%s_u%d" % (name, self.uid)
        t = self.pes.enter_context(self.nc.psum_tensor(name, list(shape), dt))
        b = Buf(name, t)
        self.bufs.append(b)
        return b

    def _deps(self, rd, wr):
        deps = {}

        def add(k, v):
            if deps.get(k, 0) < v:
                deps[k] = v
        for b in rd:
            if b.lw is not None:
                add(*b.lw)
        for b in wr:
            if b.lw is not None:
                add(*b.lw)
            for k, v in b.rd.items():
                add(k, v)
        return deps

    def _need(self, eng, deps):
        e = self.e[eng]
        seen = self.seen[eng]
        for k, v in deps.items():
            if seen.get(k, 0) >= v:
                continue
            e.wait_ge(self.h[k], v)
            seen[k] = v

    def op(self, eng, fn, rd=(), wr=(), sig=True):
        deps = self._deps(rd, wr)
        key = ('e', eng)
        if eng == 'pe':
            deps.pop(key, None)
        self._need(eng, deps)
        inst = fn(self.e[eng])
        if sig:
            self.cnt[key] += 1
            inst.then_inc(self.h[key], 1)
            mark = self.cnt[key]
        else:
            mark = self.cnt[key] + 1
        for b in wr:
            b.lw = (key, mark)
            b.rd = {}
        for b in rd:
            if b in wr:
                continue
            if b.rd.get(key, 0) < mark:
                b.rd[key] = mark
        return inst

    def dma(self, q, out, in_, sem, rd=(), wr=(), slow=False):
        deps = self._deps(rd, wr)
        self._need(q, deps)
        if sem.dkey is None:
            sem.dkey = self.dfree.pop()
        key = sem.dkey
        if slow:
            inst = self.e[q].dma_start(out=out, in_=in_, allow_slow_non_contiguous=True)
        else:
            inst = self.e[q].dma_start(out=out, in_=in_)
        self.cnt[key] += 16
        inst.then_inc(self.h[key], 16)
        mark = self.cnt[key]
        for b in wr:
            b.lw = (key, mark)
            b.rd = {}
        for b in rd:
            if b in wr:
                continue
            b.rd[key] = mark
        return inst

    def barrier(self):
        sp = self.e['sp']
        seen = self.seen['sp']
        for k, v in self.cnt.items():
            if v > seen.get(k, 0):
                sp.wait_ge(self.h[k], v)
                seen[k] = v
        self.barn += 1
        sp.sem_inc(self.bar, 1)
        for k in ENGS:
            if k != 'sp':
                self.e[k].wait_ge(self.bar, self.barn)
                for kk, v in self.cnt.items():
                    self.seen[k][kk] = v
        for b in self.bufs:
            b.lw = None
            b.rd = {}
            b.dkey = None


def mm(k, out_buf, out_ap, lhsT_ap, rhs_ap, start, stop, rd):
    k.op('pe', lambda e: e.matmul(out_ap, lhsT_ap, rhs_ap, start=start, stop=stop),
         rd=rd, wr=[out_buf], sig=stop)


def norm_block(k, xb, KC, g_buf, gcol0, ones_buf, sq, msps, rstd, tmp, out_fn, eps_buf):
    k.op('act', lambda e: e.activation(out=sq[:, 0:KC, :], in_=xb[:, 0:KC, :], func=AF.Square),
         rd=[xb], wr=[sq])
    for c in range(KC):
        mm(k, msps, msps[:, :], ones_buf[:, :], sq[:, c, :], c == 0, c == KC - 1, [ones_buf, sq])
    k.op('act', lambda e: e.activation(out=tmp[:, :], in_=msps[:, :], func=AF.Sqrt, bias=eps_buf[:, 0:1], scale=1.0),
         rd=[msps, eps_buf], wr=[tmp])
    k.op('dve', lambda e: e.reciprocal(out=rstd[:, :], in_=tmp[:, :]), rd=[tmp], wr=[rstd])
    for c in range(KC):
        ob, oap = out_fn(c)
        k.op('dve', lambda e, c=c, oap=oap: e.scalar_tensor_tensor(
            out=oap, in0=xb[:, c, :], scalar=g_buf[:, gcol0 + c:gcol0 + c + 1], in1=rstd[:, :],
            op0=ALU.mult, op1=ALU.mult), rd=[xb, g_buf, rstd], wr=[ob])


C_DQ, C_DK, C_DV, C_DIL, C_CQ, C_CKV, C_KR, C_GATE = 0, 512, 1024, 1536, 3840, 4608, 4864, 4896
NRC = 20
LAM_INIT = [0.8 - 0.6 * float(np.exp(-0.3 * l)) for l in range(DEPTH)]


def build(nc, upto=None, dump=(), layers=(0, 1)):
    dt_in = lambda name, shape: nc.dram_tensor(name, list(shape), F32, kind="ExternalInput").ap()
    x = dt_in("x", (S, D))
    w_in = dt_in("w_in", (DEPTH, D, IN_COLS))
    b_gate = dt_in("b_gate", (DEPTH, 3, D))
    g_mix = dt_in("g_mix", (DEPTH, D))
    diff_lambda = dt_in("diff_lambda", (DEPTH, 4, 64))
    g_diff = dt_in("g_diff", (DEPTH, 128))
    g_cq = dt_in("g_cq", (DEPTH, 768))
    g_ckv = dt_in("g_ckv", (DEPTH, 256))
    w_uq = dt_in("w_uq", (DEPTH, 768, 768))
    w_ukv = dt_in("w_ukv", (DEPTH, 256, 1024))
    w_o_diff = dt_in("w_o_diff", (DEPTH, 512, D))
    w_o_dil = dt_in("w_o_dil", (DEPTH, 256, D))
    w_o_mla = dt_in("w_o_mla", (DEPTH, 512, D))
    w_out = dt_in("w_out", (DEPTH, D, D))
    g_mlp = dt_in("g_mlp", (DEPTH, D))
    w_up = dt_in("w_up", (DEPTH, D, 4 * D))
    w_down = dt_in("w_down", (DEPTH, 4 * D, D))
    g_final = dt_in("g_final", (D,))
    rc64 = dt_in("rc64", (128, S))
    rs64 = dt_in("rs64", (128, S))
    rcq = dt_in("rcq", (96, S))
    rsq = dt_in("rsq", (96, S))
    out = nc.dram_tensor("out", [S, D], F32, kind="ExternalOutput").ap()

    def scr(name, shape, dt=BF16):
        kind = "ExternalOutput" if name in dump else "Internal"
        return nc.dram_tensor(name, list(shape), dt, kind=kind).ap()

    xT = scr("xT", (8, 128, S), F32)
    hT = scr("hT", (8, 128, S))
    RP = scr("RP", (NRC, 128, S))
    KR = scr("KR", (32, S))
    VA = scr("VA", (S, 512))
    VB = scr("VB", (3, S, 256))
    CQ = scr("CQ", (6, 128, S))
    CKV = scr("CKV", (2, 128, S))
    G = scr("G", (24, 128, S))
    QC = scr("QC", (8, 96, S))
    KN = scr("KN", (4, 128, S))
    VC = scr("VC", (S, 512))
    OA = scr("OA", (4, 128, S))
    OB = scr("OB", (2, 128, S))
    OC = scr("OC", (4, 128, S))
    H2 = scr("H2", (8, 128, S))
    U = scr("U", (32, 128, S))
    xTv = xT.rearrange("c p s -> p c s")
    hTv = hT.rearrange("c p s -> p c s")

    es = ExitStack()
    k = KB(nc, es)
    stop = [False]

    def done(name):
        if upto == name:
            stop[0] = True
        return stop[0]

    def consts():
        onesf = k.sb("onesf", (128, 128), F32)
        epsb = k.sb("epsb", (128, 1), F32)
        k.op('pool', lambda e: e.memset(onesf[:, :], 1.0), wr=[onesf])
        k.op('pool', lambda e: e.memset(epsb[:, :], EPS), wr=[epsb])
        return onesf, epsb

    def ones_tile(name, val, dt=F32):
        t = k.sb(name, (128, 128), dt)
        k.op('pool', lambda e: e.memset(t[:, :], val), wr=[t])
        return t

    def gvec(name, src1d, n):
        t = k.sb(name, (128, n), F32)
        k.dma('sp', t[:, :], src1d.rearrange("(c p) -> p c", p=128), t, wr=[t], slow=True)
        return t

    def wload(W, c0, src, n):
        sv = src.rearrange("(kc p) n -> p kc n", p=128)
        for a in range(0, n, 2048):
            m = min(2048, n - a)
            k.dma('pool', W[:, :, c0 + a:c0 + a + m], sv[:, :, a:a + m], W, wr=[W])

    def ph_first():
        k.phase_begin()
        onesf, epsb = consts()
        ident = k.sb("ident", (128, 128), F32)
        onesD = ones_tile("onesD", 1.0 / D)
        gt = gvec("gt", g_mix[0], 8)
        xin = [k.sb("xin%d" % i, (128, 4, D), F32) for i in range(2)]
        xTb = [k.sb("xTb%d" % i, (128, 8, TB), F32) for i in range(2)]
        hTb = [k.sb("hTb%d" % i, (128, 8, TB), BF16) for i in range(2)]
        sq = k.sb("sq", (128, 8, TB), F32)
        rstd = k.sb("rstd", (128, TB), F32)
        tmp = k.sb("tmp", (128, TB), F32)
        tp = [k.ps("tp%d" % i) for i in range(4)]
        msps = k.ps("msps")
        k.op('pool', lambda e: e.affine_select(out=ident[:, :], in_=onesf[:, :], pattern=[[-1, 128]],
                                               compare_op=ALU.is_equal, fill=0.0, base=0, channel_multiplier=1),
             rd=[onesf], wr=[ident])
        xv = x.rearrange("(b j p) d -> b p j d", p=128, j=4)
        k.dma('sp', xin[0][:, :, :], xv[0], xin[0], wr=[xin[0]])
        for blk in range(NTB):
            xi = xin[blk % 2]
            xo = xTb[blk % 2]
            ho = hTb[blk % 2]
            if blk + 1 < NTB:
                nx = xin[(blk + 1) % 2]
                k.dma('sp', nx[:, :, :], xv[blk + 1], nx, wr=[nx])
            for fc in range(8):
                p = tp[fc % 4]
                for j in range(4):
                    k.op('pe', lambda e: e.transpose(p[:, j * 128:(j + 1) * 128], xi[:, j, fc * 128:(fc + 1) * 128],
                                                     ident[:, :]), rd=[xi, ident], wr=[p], sig=(j == 3))
                if fc % 2 == 0:
                    k.op('act', lambda e: e.copy(out=xo[:, fc, :], in_=p[:, :]), rd=[p], wr=[xo])
                else:
                    k.op('dve', lambda e: e.tensor_copy(out=xo[:, fc, :], in_=p[:, :]), rd=[p], wr=[xo])
            norm_block(k, xo, 8, gt, 0, onesD, sq, msps, rstd, tmp, lambda c: (ho, ho[:, c, :]), epsb)
            sl = slice(blk * TB, (blk + 1) * TB)
            k.dma('sp', xTv[:, :, sl], xo[:, :, :], xo, rd=[xo])
            k.dma('sp', hTv[:, :, sl], ho[:, :, :], ho, rd=[ho])
        k.phase_end()

    def ph_rope(l):
        k.phase_begin()
        NW = 2592
        W = k.sb("W", (128, 8, NW), BF16)
        Wsw = k.sb("Wsw", (128, 8, NW), BF16)
        wload(W, 0, w_in[l][:, 0:1024], 1024)
        for g in range(3):
            wload(W, 1024 + g * 512, w_in[l][:, C_DIL + g * 768:C_DIL + g * 768 + 512], 512)
        wload(W, 2560, w_in[l][:, C_KR:C_KR + 32], 32)
        for kc in range(8):
            src = W[:, kc, 0:2560].rearrange("p (h t d) -> p h t d", t=2, d=32)
            dst = Wsw[:, kc, 0:2560].rearrange("p (h t d) -> p h t d", t=2, d=32)
            eng = 'dve' if kc % 2 == 0 else 'pool'
            k.op(eng, lambda e: e.tensor_copy(out=dst[:, :, 0, :], in_=src[:, :, 1, :]), rd=[W], wr=[Wsw])
            k.op(eng, lambda e: e.tensor_copy(out=dst[:, :, 1, :], in_=src[:, :, 0, :]), rd=[W], wr=[Wsw])
        k.op('dve', lambda e: e.tensor_copy(out=Wsw[:, :, 2560:2576], in_=W[:, :, 2576:2592]), rd=[W], wr=[Wsw])
        k.op('dve', lambda e: e.tensor_copy(out=Wsw[:, :, 2576:2592], in_=W[:, :, 2560:2576]), rd=[W], wr=[Wsw])
        hb = [k.sb("hb%d" % i, (128, 8, TB), BF16) for i in range(2)]
        ct = [k.sb("ct%d" % i, (128, TB), F32) for i in range(2)]
        st = [k.sb("st%d" % i, (128, TB), F32) for i in range(2)]
        ck = [k.sb("ck%d" % i, (32, TB), F32) for i in range(2)]
        sk = [k.sb("sk%d" % i, (32, TB), F32) for i in range(2)]
        psA = [k.ps("pa%d" % i) for i in range(3)]
        psB = [k.ps("pb%d" % i) for i in range(3)]
        t1 = [k.sb("t1_%d" % i, (128, TB), F32) for i in range(2)]
        t2 = [k.sb("t2_%d" % i, (128, TB), F32) for i in range(2)]
        og = [k.sb("og%d" % i, (128, TB), BF16) for i in range(4)]

        def load_blk(b):
            sl = slice(b * TB, (b + 1) * TB)
            i = b % 2
            k.dma('sp', hb[i][:, :, :], hTv[:, :, sl], hb[i], wr=[hb[i]])
            k.dma('sp', ct[i][:, :], rc64[:, sl], ct[i], wr=[ct[i]])
            k.dma('sp', st[i][:, :], rs64[:, sl], st[i], wr=[st[i]])
            k.dma('sp', ck[i][:, :], rcq[64:96, sl], ck[i], wr=[ck[i]])
            k.dma('sp', sk[i][:, :], rsq[64:96, sl], sk[i], wr=[sk[i]])
        load_blk(0)
        n = 0
        for b in range(NTB):
            if b + 1 < NTB:
                load_blk(b + 1)
            i = b % 2
            sl = slice(b * TB, (b + 1) * TB)
            for ci in range(NRC + 1):
                M = 128 if ci < NRC else 32
                c0 = ci * 128
                pa, pb = psA[n % 3], psB[n % 3]
                for kc in range(8):
                    mm(k, pa, pa[0:M, :], W[:, kc, c0:c0 + M], hb[i][:, kc, :], kc == 0, kc == 7, [W, hb[i]])
                for kc in range(8):
                    mm(k, pb, pb[0:M, :], Wsw[:, kc, c0:c0 + M], hb[i][:, kc, :], kc == 0, kc == 7, [Wsw, hb[i]])
                a1, a2, o = t1[n % 2], t2[n % 2], og[n % 4]
                cc, ss = (ct[i], st[i]) if ci < NRC else (ck[i], sk[i])
                k.op('dve', lambda e: e.tensor_tensor(out=a1[0:M, :], in0=pa[0:M, :], in1=cc[0:M, :], op=ALU.mult),
                     rd=[pa, cc], wr=[a1])
                k.op('dve', lambda e: e.tensor_tensor(out=a2[0:M, :], in0=pb[0:M, :], in1=ss[0:M, :], op=ALU.mult),
                     rd=[pb, ss], wr=[a2])
                k.op('pool', lambda e: e.tensor_tensor(out=o[0:M, :], in0=a1[0:M, :], in1=a2[0:M, :], op=ALU.add),
                     rd=[a1, a2], wr=[o])
                dst = RP[ci][:, sl] if ci < NRC else KR[:, sl]
                k.dma('sp', dst, o[0:M, :], o, rd=[o])
                n += 1
        k.phase_end()

    def ph_vcg(l):
        k.phase_begin()
        onesf, epsb = consts()
        Wv = k.sb("Wv", (128, 8, 1280), BF16)
        Wc = k.sb("Wc", (128, 8, 1024), BF16)
        Wg = k.sb("Wg", (128, 8, 3072), BF16)
        wload(Wv, 0, w_in[l][:, C_DV:C_DV + 512], 512)
        for g in range(3):
            wload(Wv, 512 + g * 256, w_in[l][:, C_DIL + g * 768 + 512:C_DIL + g * 768 + 768], 256)
        wload(Wc, 0, w_in[l][:, C_CQ:C_CQ + 1024], 1024)
        wload(Wg, 0, w_in[l][:, C_GATE:C_GATE + 3072], 3072)
        bg = k.sb("bg", (128, 24), F32)
        for i in range(3):
            k.dma('sp', bg[:, i * 8:(i + 1) * 8], b_gate[l][i].rearrange("(c p) -> p c", p=128), bg, wr=[bg], slow=True)
        gcq = gvec("gcq", g_cq[l], 6)
        gckv = gvec("gckv", g_ckv[l], 2)
        o768 = ones_tile("o768", 1.0 / 768)
        o256 = ones_tile("o256", 1.0 / 256)
        hb = [k.sb("hb%d" % i, (128, 8, TB), BF16) for i in range(2)]
        vst = k.sb("vst", (128, 4, 1280), BF16)
        cqf = k.sb("cqf", (128, 6, TB), F32)
        sq = k.sb("sq", (128, 6, TB), F32)
        cqo = k.sb("cqo", (128, 6, TB), BF16)
        ckvf = k.sb("ckvf", (128, 2, TB), F32)
        ckvo = k.sb("ckvo", (128, 2, TB), BF16)
        gst = [k.sb("gst%d" % i, (128, 8, TB), BF16) for i in range(2)]
        rstd = k.sb("rstd", (128, TB), F32)
        tmp = k.sb("tmp", (128, TB), F32)
        pv = [k.ps("pv%d" % i) for i in range(2)]
        pc = [k.ps("pc%d" % i) for i in range(3)]
        msps = k.ps("msps")
        CQv = CQ.rearrange("c p s -> p c s")
        CKVv = CKV.rearrange("c p s -> p c s")
        Gv = G.rearrange("c p s -> p c s")

        def load_blk(b):
            sl = slice(b * TB, (b + 1) * TB)
            k.dma('sp', hb[b % 2][:, :, :], hTv[:, :, sl], hb[b % 2], wr=[hb[b % 2]])
        load_blk(0)
        nv = 0
        npc = 0
        for b in range(NTB):
            if b + 1 < NTB:
                load_blk(b + 1)
            h = hb[b % 2]
            sl = slice(b * TB, (b + 1) * TB)
            for j in range(4):
                for (c0, n) in ((0, 512), (512, 512), (1024, 256)):
                    p = pv[nv % 2]
                    nv += 1
                    for kc in range(8):
                        mm(k, p, p[:, 0:n], h[:, kc, j * 128:(j + 1) * 128], Wv[:, kc, c0:c0 + n], kc == 0, kc == 7, [h, Wv])
                    k.op('act', lambda e: e.copy(out=vst[:, j, c0:c0 + n], in_=p[:, 0:n]), rd=[p], wr=[vst])
            k.dma('sp', VA[sl, :].rearrange("(j p) c -> p j c", p=128), vst[:, :, 0:512], vst, rd=[vst])
            for g in range(3):
                k.dma('sp', VB[g][sl, :].rearrange("(j p) c -> p j c", p=128), vst[:, :, 512 + g * 256:768 + g * 256],
                      vst, rd=[vst])
            for c in range(8):
                p = pc[npc % 3]
                npc += 1
                for kc in range(8):
                    mm(k, p, p[:, :], Wc[:, kc, c * 128:(c + 1) * 128], h[:, kc, :], kc == 0, kc == 7, [Wc, h])
                if c < 6:
                    k.op('dve', lambda e: e.tensor_copy(out=cqf[:, c, :], in_=p[:, :]), rd=[p], wr=[cqf])
                else:
                    k.op('dve', lambda e: e.tensor_copy(out=ckvf[:, c - 6, :], in_=p[:, :]), rd=[p], wr=[ckvf])
            norm_block(k, cqf, 6, gcq, 0, o768, sq, msps, rstd, tmp, lambda c: (cqo, cqo[:, c, :]), epsb)
            k.dma('sp', CQv[:, :, sl], cqo[:, :, :], cqo, rd=[cqo])
            norm_block(k, ckvf, 2, gckv, 0, o256, sq, msps, rstd, tmp, lambda c: (ckvo, ckvo[:, c, :]), epsb)
            k.dma('sp', CKVv[:, :, sl], ckvo[:, :, :], ckvo, rd=[ckvo])
            for gi in range(24):
                p = pc[npc % 3]
                npc += 1
                gs = gst[(gi // 8) % 2]
                for kc in range(8):
                    mm(k, p, p[:, :], Wg[:, kc, gi * 128:(gi + 1) * 128], h[:, kc, :], kc == 0, kc == 7, [Wg, h])
                k.op('act', lambda e: e.activation(out=gs[:, gi % 8, :], in_=p[:, :], func=AF.Sigmoid,
                                                   bias=bg[:, gi:gi + 1], scale=1.0), rd=[p, bg], wr=[gs])
                if gi % 8 == 7:
                    k.dma('sp', Gv[:, gi - 7:gi + 1, sl], gs[:, :, :], gs, rd=[gs])
        k.phase_end()

    def ph_mla_up(l):
        k.phase_begin()
        Wq = k.sb("Wq", (128, 6, 768), BF16)
        Wqs = k.sb("Wqs", (128, 6, 768), BF16)
        Wkn = k.sb("Wkn", (128, 2, 512), BF16)
        Wvv = k.sb("Wvv", (128, 2, 512), BF16)
        wload(Wq, 0, w_uq[l], 768)
        ukv = w_ukv[l].rearrange("(kc p) (h t d) -> p kc h t d", p=128, t=2, d=64)
        for kc in range(2):
            k.dma('pool', Wkn[:, kc, :].rearrange("p (h d) -> p h d", d=64), ukv[:, kc, :, 0, :], Wkn, wr=[Wkn])
            k.dma('pool', Wvv[:, kc, :].rearrange("p (h d) -> p h d", d=64), ukv[:, kc, :, 1, :], Wvv, wr=[Wvv])
        for kc in range(6):
            src = Wq[:, kc, :].rearrange("p (h e) -> p h e", e=96)
            dst = Wqs[:, kc, :].rearrange("p (h e) -> p h e", e=96)
            eng = 'dve' if kc % 2 == 0 else 'pool'
            k.op(eng, lambda e: e.tensor_copy(out=dst[:, :, 0:64], in_=src[:, :, 0:64]), rd=[Wq], wr=[Wqs])
            k.op(eng, lambda e: e.tensor_copy(out=dst[:, :, 64:80], in_=src[:, :, 80:96]), rd=[Wq], wr=[Wqs])
            k.op(eng, lambda e: e.tensor_copy(out=dst[:, :, 80:96], in_=src[:, :, 64:80]), rd=[Wq], wr=[Wqs])
        cqb = [k.sb("cqb%d" % i, (128, 6, TB), BF16) for i in range(2)]
        ckb = [k.sb("ckb%d" % i, (128, 2, TB), BF16) for i in range(2)]
        ct = [k.sb("ct%d" % i, (96, TB), F32) for i in range(2)]
        st = [k.sb("st%d" % i, (96, TB), F32) for i in range(2)]
        psA = [k.ps("pa%d" % i) for i in range(3)]
        psB = [k.ps("pb%d" % i) for i in range(3)]
        t1 = [k.sb("t1_%d" % i, (96, TB), F32) for i in range(2)]
        t2 = [k.sb("t2_%d" % i, (96, TB), F32) for i in range(2)]
        og = [k.sb("og%d" % i, (128, TB), BF16) for i in range(4)]
        vst = k.sb("vst", (128, 4, 512), BF16)
        CQv = CQ.rearrange("c p s -> p c s")
        CKVv = CKV.rearrange("c p s -> p c s")

        def load_blk(b):
            sl = slice(b * TB, (b + 1) * TB)
            i = b % 2
            k.dma('sp', cqb[i][:, :, :], CQv[:, :, sl], cqb[i], wr=[cqb[i]])
            k.dma('sp', ckb[i][:, :, :], CKVv[:, :, sl], ckb[i], wr=[ckb[i]])
            k.dma('sp', ct[i][:, :], rcq[:, sl], ct[i], wr=[ct[i]])
            k.dma('sp', st[i][:, :], rsq[:, sl], st[i], wr=[st[i]])
        load_blk(0)
        n = 0
        no = 0
        for b in range(NTB):
            if b + 1 < NTB:
                load_blk(b + 1)
            i = b % 2
            sl = slice(b * TB, (b + 1) * TB)
            for h in range(8):
                pa, pb = psA[n % 3], psB[n % 3]
                for kc in range(6):
                    mm(k, pa, pa[0:96, :], Wq[:, kc, h * 96:(h + 1) * 96], cqb[i][:, kc, :], kc == 0, kc == 5, [Wq, cqb[i]])
                for kc in range(6):
                    mm(k, pb, pb[0:96, :], Wqs[:, kc, h * 96:(h + 1) * 96], cqb[i][:, kc, :], kc == 0, kc == 5, [Wqs, cqb[i]])
                a1, a2, o = t1[n % 2], t2[n % 2], og[no % 4]
                k.op('dve', lambda e: e.tensor_tensor(out=a1[:, :], in0=pa[0:96, :], in1=ct[i][:, :], op=ALU.mult),
                     rd=[pa, ct[i]], wr=[a1])
                k.op('dve', lambda e: e.tensor_tensor(out=a2[:, :], in0=pb[0:96, :], in1=st[i][:, :], op=ALU.mult),
                     rd=[pb, st[i]], wr=[a2])
                k.op('pool', lambda e: e.tensor_tensor(out=o[0:96, :], in0=a1[:, :], in1=a2[:, :], op=ALU.add),
                     rd=[a1, a2], wr=[o])
                k.dma('sp', QC[h][:, sl], o[0:96, :], o, rd=[o])
                n += 1
                no += 1
            for j in range(4):
                pa = psA[n % 3]
                n += 1
                o = og[no % 4]
                no += 1
                for kc in range(2):
                    mm(k, pa, pa[:, :], Wkn[:, kc, j * 128:(j + 1) * 128], ckb[i][:, kc, :], kc == 0, kc == 1, [Wkn, ckb[i]])
                k.op('act', lambda e: e.copy(out=o[:, :], in_=pa[:, :]), rd=[pa], wr=[o])
                k.dma('sp', KN[j][:, sl], o[:, :], o, rd=[o])
            for jt in range(4):
                pb = psB[n % 3]
                n += 1
                for kc in range(2):
                    mm(k, pb, pb[:, :], ckb[i][:, kc, jt * 128:(jt + 1) * 128], Wvv[:, kc, :], kc == 0, kc == 1, [Wvv, ckb[i]])
                k.op('act', lambda e: e.copy(out=vst[:, jt, :], in_=pb[:, :]), rd=[pb], wr=[vst])
            k.dma('sp', VC[sl, :].rearrange("(j p) c -> p j c", p=128), vst[:, :, :], vst, rd=[vst])
        k.phase_end()

    def ph_diff(l):
        k.phase_begin()
        onesf, epsb = consts()
        onesb = ones_tile("onesb", 1.0, BF16)
        o128 = ones_tile("o128", 1.0 / 128)
        gd = k.sb("gd", (128, 1), F32)
        k.dma('sp', gd[:, :], g_diff[l].rearrange("(p o) -> p o", o=1), gd, wr=[gd], slow=True)
        gdl = k.sb("gdl", (128, 1), F32)
        k.op('dve', lambda e: e.tensor_scalar(out=gdl[:, :], in0=gd[:, :], scalar1=1.0 - LAM_INIT[l], scalar2=None,
                                              op0=ALU.mult), rd=[gd], wr=[gdl])
        dl = k.sb("dl", (1, 256), F32)
        k.dma('sp', dl[:, :], diff_lambda[l].rearrange("(o a) d -> o (a d)", o=1), dl, wr=[dl])
        prod = k.sb("prod", (1, 128), F32)
        sums = k.sb("sums", (1, 2), F32)
        ex = k.sb("ex", (1, 2), F32)
        nl = k.sb("nl", (1, 2), F32)
        nlam = k.sb("nlam", (128, 2), F32)
        k.op('dve', lambda e: e.tensor_tensor(out=prod[0:1, 0:64], in0=dl[0:1, 0:64], in1=dl[0:1, 64:128], op=ALU.mult),
             rd=[dl], wr=[prod])
        k.op('dve', lambda e: e.tensor_tensor(out=prod[0:1, 64:128], in0=dl[0:1, 128:192], in1=dl[0:1, 192:256], op=ALU.mult),
             rd=[dl, prod], wr=[prod])
        k.op('dve', lambda e: e.tensor_reduce(out=sums[0:1, 0:2], in_=prod[0:1, :].rearrange("p (a d) -> p a d", a=2),
                                              axis=mybir.AxisListType.X, op=ALU.add), rd=[prod], wr=[sums])
        k.op('act', lambda e: e.activation(out=ex[0:1, :], in_=sums[0:1, :], func=AF.Exp), rd=[sums], wr=[ex])
        k.op('dve', lambda e: e.tensor_tensor(out=nl[0:1, 0:1], in0=ex[0:1, 1:2], in1=ex[0:1, 0:1], op=ALU.subtract),
             rd=[ex], wr=[nl])
        k.op('dve', lambda e: e.tensor_scalar(out=nl[0:1, 0:1], in0=nl[0:1, 0:1], scalar1=-LAM_INIT[l], scalar2=None,
                                              op0=ALU.add), rd=[nl], wr=[nl])
        k.op('dve', lambda e: e.tensor_copy(out=nl[0:1, 1:2], in_=nl[0:1, 0:1]), rd=[nl], wr=[nl])
        SA = k.ps("SA", (128, 1024))
        SB_ = k.ps("SB", (128, 1024))
        SC = k.ps("SC", (128, 1024))
        Sq = [SA, SB_]
        acc1, acc2 = k.ps("acc1"), k.ps("acc2")
        mm(k, SC, SC[:, 0:2], onesf[0:1, :], nl[0:1, 0:2], True, True, [onesf, nl])
        k.op('dve', lambda e: e.tensor_copy(out=nlam[:, :], in_=SC[:, 0:2]), rd=[SC], wr=[nlam])

        KT = [k.sb("KT%d" % i, (128, S), BF16) for i in range(2)]
        QT = [k.sb("QT%d" % i, (128, S), BF16) for i in range(2)]
        V = [k.sb("V%d" % i, (128, 64, 128), BF16) for i in range(2)]
        NP = 12
        Pb = [k.sb("P%d" % i, (128, 1024), BF16) for i in range(NP)]
        ps1 = [k.sb("ps1_%d" % i, (128, TB), F32) for i in range(2)]
        ps2 = [k.sb("ps2_%d" % i, (128, TB), F32) for i in range(2)]
        a1s = k.sb("a1s", (128, TB), F32)
        a2s = k.sb("a2s", (128, TB), F32)
        tl = k.sb("tl", (128, 1024), F32)
        rr12 = k.sb("rr12", (128, 1024), F32)
        o1 = k.sb("o1", (128, TB), F32)
        o2 = k.sb("o2", (128, TB), F32)
        od = k.sb("od", (128, TB), F32)
        sqd = k.sb("sqd", (128, TB), F32)
        tmp = k.sb("tmp", (128, TB), F32)
        rstd = k.sb("rstd", (128, TB), F32)
        ost = [k.sb("ost%d" % i, (128, TB), BF16) for i in range(2)]

        def load_head(h):
            i = h % 2
            k.dma('sp', KT[i][:, :], RP[4 + h], KT[i], wr=[KT[i]])
            k.dma('sp', QT[i][:, :], RP[h], QT[i], wr=[QT[i]])
            k.dma('sp', V[i][:, :, :], VA[:, h * 128:(h + 1) * 128].rearrange("(kc p) d -> p kc d", p=128), V[i], wr=[V[i]])
        load_head(0)
        sc = [0]
        pc = [0]
        pending = []

        def run_pending(kc):
            keep = []
            for (t, fn) in pending:
                if kc is None or t <= kc:
                    fn()
                else:
                    keep.append((t, fn))
            pending[:] = keep
        nq = 0
        for h in range(4):
            if h + 1 < 4:
                load_head(h + 1)
            i = h % 2
            for qb in range(NTB):
                qsl = slice(qb * TB, (qb + 1) * TB)
                pa1, pa2 = ps1[nq % 2], ps2[nq % 2]

                def qk(kc):
                    ksl = slice(kc * 128, (kc + 1) * 128)
                    s = Sq[sc[0] % 2]
                    sc[0] += 1
                    mm(k, s, s[:, 0:512], KT[i][0:64, ksl], QT[i][0:64, qsl], True, True, [KT[i], QT[i]])
                    mm(k, s, s[:, 512:1024], KT[i][64:128, ksl], QT[i][64:128, qsl], True, True, [KT[i], QT[i]])
                    return s

                def pv(kc, s):
                    p = Pb[pc[0] % NP]
                    pc[0] += 1
                    k.op('act', lambda e: e.activation(out=p[:, :], in_=s[:, :], func=AF.Exp, scale=0.125), rd=[s], wr=[p])
                    st_, sp_ = kc == 0, kc == 63
                    mm(k, acc1, acc1[:, :], V[i][:, kc, :], p[:, 0:512], st_, sp_, [V[i], p])
                    mm(k, acc2, acc2[:, :], V[i][:, kc, :], p[:, 512:1024], st_, sp_, [V[i], p])
                    if kc == 0:
                        k.op('dve', lambda e: e.tensor_copy(out=pa1[:, :], in_=p[:, 0:512]), rd=[p], wr=[pa1])
                        k.op('pool', lambda e: e.tensor_copy(out=pa2[:, :], in_=p[:, 512:1024]), rd=[p], wr=[pa2])
                    else:
                        k.op('dve', lambda e: e.tensor_tensor(out=pa1[:, :], in0=pa1[:, :], in1=p[:, 0:512], op=ALU.add),
                             rd=[p, pa1], wr=[pa1])
                        k.op('pool', lambda e: e.tensor_tensor(out=pa2[:, :], in0=pa2[:, :], in1=p[:, 512:1024], op=ALU.add),
                             rd=[p, pa2], wr=[pa2])
                cur = qk(0)
                for kc in range(64):
                    nxt = qk(kc + 1) if kc + 1 < 64 else None
                    pv(kc, cur)
                    cur = nxt
                    run_pending(kc)
                k.op('dve', lambda e: e.tensor_copy(out=a1s[:, :], in_=acc1[:, :]), rd=[acc1], wr=[a1s])
                k.op('dve', lambda e: e.tensor_copy(out=a2s[:, :], in_=acc2[:, :]), rd=[acc2], wr=[a2s])
                o = ost[nq % 2]
                nq += 1

                def ep1(pa1=pa1, pa2=pa2):
                    mm(k, SC, SC[:, 0:512], onesf[:, :], pa1[:, :], True, True, [onesf, pa1])
                    mm(k, SC, SC[:, 512:1024], onesf[:, :], pa2[:, :], True, True, [onesf, pa2])
                    k.op('act', lambda e: e.activation(out=tl[:, :], in_=SC[:, :], func=AF.Ln), rd=[SC], wr=[tl])
                    k.op('act', lambda e: e.activation(out=rr12[:, :], in_=tl[:, :], func=AF.Exp, scale=-1.0), rd=[tl], wr=[rr12])
                    k.op('dve', lambda e: e.tensor_tensor(out=o1[:, :], in0=a1s[:, :], in1=rr12[:, 0:512], op=ALU.mult), rd=[a1s, rr12], wr=[o1])
                    k.op('dve', lambda e: e.tensor_tensor(out=o2[:, :], in0=a2s[:, :], in1=rr12[:, 512:1024], op=ALU.mult), rd=[a2s, rr12], wr=[o2])
                    k.op('dve', lambda e: e.scalar_tensor_tensor(out=od[:, :], in0=o2[:, :], scalar=nlam[:, 0:1], in1=o1[:, :],
                                                                 op0=ALU.mult, op1=ALU.add), rd=[o2, o1, nlam], wr=[od])
                    k.op('pool', lambda e: e.tensor_tensor(out=sqd[:, :], in0=od[:, :], in1=od[:, :], op=ALU.mult), rd=[od], wr=[sqd])

                def ep2(o=o, h=h, qsl=qsl):
                    mm(k, SC, SC[:, 0:512], o128[:, :], sqd[:, :], True, True, [o128, sqd])
                    k.op('act', lambda e: e.activation(out=tmp[:, :], in_=SC[:, 0:512], func=AF.Ln, bias=epsb[:, 0:1], scale=1.0),
                         rd=[SC, epsb], wr=[tmp])
                    k.op('act', lambda e: e.activation(out=rstd[:, :], in_=tmp[:, :], func=AF.Exp, scale=-0.5), rd=[tmp], wr=[rstd])
                    k.op('dve', lambda e: e.scalar_tensor_tensor(out=o[:, :], in0=od[:, :], scalar=gdl[:, 0:1], in1=rstd[:, :],
                                                                 op0=ALU.mult, op1=ALU.mult), rd=[od, gdl, rstd], wr=[o])
                    k.dma('sp', OA[h][:, qsl], o[:, :], o, rd=[o])
                pending.append((3, ep1))
                pending.append((30, ep2))
        run_pending(None)
        k.phase_end()

    def ph_mla(l):
        k.phase_begin()
        KT = [k.sb("KT%d" % i, (128, S), BF16) for i in range(2)]
        QT = [k.sb("QT%d" % i, (128, S), BF16) for i in range(2)]
        V = [k.sb("V%d" % i, (128, 64, 128), BF16) for i in range(2)]
        for i in range(2):
            k.op('pool', lambda e: e.memset(V[i][:, :, 64:128], 1.0), wr=[V[i]])
        Pb = [k.sb("P%d" % i, (128, 1024), BF16) for i in range(4)]
        Sps = [k.ps("S%d" % i, (128, 1024)) for i in range(3)]
        acc = [k.ps("acc%d" % i) for i in range(2)]
        rr = k.sb("rr", (64, TB), F32)
        ost = [k.sb("ost%d" % i, (64, TB), BF16) for i in range(2)]
        sc_ = 96 ** -0.5

        def load_head(h):
            i = h % 2
            b0 = 64 * (h % 2)
            k.dma('sp', KT[i][0:64, :], KN[h // 2][b0:b0 + 64, :], KT[i], wr=[KT[i]])
            k.dma('sp', KT[i][64:96, :], KR[:, :], KT[i], wr=[KT[i]])
            k.dma('sp', QT[i][0:96, :], QC[h], QT[i], wr=[QT[i]])
            k.dma('sp', V[i][:, :, 0:64], VC[:, h * 64:(h + 1) * 64].rearrange("(kc p) d -> p kc d", p=128), V[i], wr=[V[i]])
        load_head(0)
        sc = [0]
        pc = [0]
        nq = 0
        for h in range(8):
            if h + 1 < 8:
                load_head(h + 1)
            i = h % 2
            b0 = 64 * (h % 2)
            for qb in range(NTB):
                qsl = slice(qb * TB, (qb + 1) * TB)
                a = acc[nq % 2]

                def qk(j):
                    s1 = Sps[sc[0] % 3]
                    sc[0] += 1
                    for t in range(2):
                        kc = 2 * j + t
                        mm(k, s1, s1[:, t * 512:(t + 1) * 512], KT[i][0:96, kc * 128:(kc + 1) * 128], QT[i][0:96, qsl],
                           True, True, [KT[i], QT[i]])
                    return s1

                def pv(j, s1):
                    p1 = Pb[pc[0] % 4]
                    pc[0] += 1
                    k.op('act', lambda e: e.activation(out=p1[:, :], in_=s1[:, :], func=AF.Exp, scale=sc_), rd=[s1], wr=[p1])
                    for t in range(2):
                        kc = 2 * j + t
                        mm(k, a, a[:, :], V[i][:, kc, :], p1[:, t * 512:(t + 1) * 512], kc == 0, kc == 63, [V[i], p1])
                ss = {0: qk(0), 1: qk(1)}
                for j in range(32):
                    if j + 2 < 32:
                        ss[j + 2] = qk(j + 2)
                    pv(j, ss.pop(j))
                o = ost[nq % 2]
                nq += 1
                k.op('dve', lambda e: e.reciprocal(out=rr[:, :], in_=a[64:128, :]), rd=[a], wr=[rr])
                k.op('dve', lambda e: e.tensor_tensor(out=o[:, :], in0=a[0:64, :], in1=rr[:, :], op=ALU.mult), rd=[a, rr], wr=[o])
                k.dma('sp', OC[h // 2][b0:b0 + 64, qsl], o[:, :], o, rd=[o])
        k.phase_end()

    def ph_dil(l):
        k.phase_begin()
        PADM = 1024
        onesb = ones_tile("onesb", 1.0, BF16)
        M0 = k.sb("M0", (128, 2, 128), BF16)
        M = k.sb("M", (128, 2, 128), BF16)
        Mf = k.sb("Mf", (128, 2, 128), BF16)
        Ml = k.sb("Ml", (128, 2, 128), BF16)
        for j in range(2):
            k.op('pool', lambda e: e.affine_select(out=M0[:, j, :], in_=onesb[:, :], pattern=[[-1, 128]], compare_op=ALU.is_ge,
                                                   fill=0.0, base=128 * j, channel_multiplier=1), rd=[onesb], wr=[M0])
            k.op('pool', lambda e: e.affine_select(out=M[:, j, :], in_=M0[:, j, :], pattern=[[1, 128]], compare_op=ALU.is_ge,
                                                   fill=0.0, base=128 - 128 * j, channel_multiplier=-1), rd=[M0], wr=[M])
        k.op('pool', lambda e: e.affine_select(out=Mf[:, 0, :], in_=M[:, 0, :], pattern=[[0, 128]], compare_op=ALU.is_ge,
                                               fill=0.0, base=-64, channel_multiplier=1), rd=[M], wr=[Mf])
        k.op('pool', lambda e: e.tensor_copy(out=Mf[:, 1, :], in_=M[:, 1, :]), rd=[M], wr=[Mf])
        k.op('pool', lambda e: e.affine_select(out=Ml[:, 1, :], in_=M[:, 1, :], pattern=[[0, 128]], compare_op=ALU.is_ge,
                                               fill=0.0, base=63, channel_multiplier=-1), rd=[M], wr=[Ml])
        k.op('pool', lambda e: e.tensor_copy(out=Ml[:, 0, :], in_=M[:, 0, :]), rd=[M], wr=[Ml])
        QT = [k.sb("QT%d" % i, (64, S), BF16) for i in range(2)]
        KT = [k.sb("KT%d" % i, (64, S + 2 * PADM), BF16) for i in range(2)]
        Vg = [k.sb("Vg%d" % i, (128, 80, 128), BF16) for i in range(2)]
        for i in range(2):
            k.op('pool', lambda e: e.memset(KT[i][:, 0:PADM], 0.0), wr=[KT[i]])
            k.op('pool', lambda e: e.memset(KT[i][:, PADM + S:PADM + S + PADM], 0.0), wr=[KT[i]])
            k.op('pool', lambda e: e.memset(Vg[i][:, :, 0:64], 0.0), wr=[Vg[i]])
            k.op('pool', lambda e: e.memset(Vg[i][:, :, 64:128], 1.0), wr=[Vg[i]])
        tot = k.sb("tot", (128, S), F32)
        rr = k.sb("rr", (64, 2048), F32)
        ost = [k.sb("ost%d" % i, (64, 2048), BF16) for i in range(2)]
        Pb = [k.sb("P%d" % i, (128, 1024), BF16) for i in range(3)]
        Pm = [k.sb("Pm%d" % i, (128, 1024), BF16) for i in range(3)]
        Sps = [k.ps("S%d" % i, (128, 1024)) for i in range(2)]
        acc = [k.ps("acc%d" % i) for i in range(3)]
        MB = {}
        for nm, parts in (('m', (M, M, M, M)), ('f', (Mf, M, M, M)), ('l', (M, M, M, Ml)), ('fl', (Mf, M, M, Ml))):
            t = k.sb("MB" + nm, (128, 4, 256), BF16)
            for uu, src in enumerate(parts):
                k.op('pool', lambda e: e.tensor_copy(out=t[:, uu, :], in_=src[:, :, :].rearrange("p a b -> p (a b)")), rd=[src], wr=[t])
            MB[nm] = t
        DIL = (1, 4, 16)
        jobs = [(h, g) for h in range(4) for g in range(3)]

        def load_job(ji):
            h, g = jobs[ji]
            i = ji % 2
            d = DIL[g]
            n = S // d
            nc_ = n // 128 + 1
            b0 = 64 * (h % 2)
            k.dma('sp', QT[i][:, :], RP[8 + 4 * g + h // 2][b0:b0 + 64, :], QT[i], wr=[QT[i]])
            k.dma('sp', KT[i][:, PADM:PADM + S], RP[8 + 4 * g + 2 + h // 2][b0:b0 + 64, :], KT[i], wr=[KT[i]])
            src = VB[g][:, h * 64:(h + 1) * 64].rearrange("(t c) e -> c t e", c=d)
            for c in range(d):
                base = c * nc_
                k.dma('sp', Vg[i][:, base + 1:base + n // 128, 0:64],
                      src[c][64:n - 64, :].rearrange("(m p) e -> p m e", p=128), Vg[i], wr=[Vg[i]])
                k.dma('sp', Vg[i][64:128, base, 0:64], src[c][0:64, :], Vg[i], wr=[Vg[i]])
                k.dma('sp', Vg[i][0:64, base + n // 128, 0:64], src[c][n - 64:n, :], Vg[i], wr=[Vg[i]])
        load_job(0)
        un = 0
        no = 0
        for ji, (h, g) in enumerate(jobs):
            if ji + 1 < len(jobs):
                load_job(ji + 1)
            i = ji % 2
            d = DIL[g]
            n = S // d
            nu = n // 128
            nc_ = nu + 1
            nb = nu // 4
            for c in range(d):
                for ub in range(nb):
                    sx = Sps[un % 2]
                    pb_, pm_, a = Pb[un % 3], Pm[un % 3], acc[un % 3]
                    for uu in range(4):
                        tq0 = 128 * (4 * ub + uu)
                        q0 = tq0 * d + c
                        qcols = slice(q0, q0 + 127 * d + 1, d)
                        for j in range(2):
                            ks = PADM + (tq0 - 64 + 128 * j) * d + c
                            kcols = slice(ks, ks + 127 * d + 1, d)
                            o_ = uu * 256 + j * 128
                            mm(k, sx, sx[:, o_:o_ + 128], KT[i][:, kcols], QT[i][:, qcols], True, True, [KT[i], QT[i]])
                    k.op('act', lambda e: e.activation(out=pb_[:, :], in_=sx[:, :], func=AF.Exp, scale=0.125), rd=[sx], wr=[pb_])
                    mk = MB['fl' if nb == 1 else ('f' if ub == 0 else ('l' if ub == nb - 1 else 'm'))]
                    eng = 'dve' if un % 2 == 0 else 'pool'
                    k.op(eng, lambda e: e.tensor_tensor(out=pm_[:, :], in0=pb_[:, :], in1=mk[:, :, :].rearrange("p a b -> p (a b)"),
                                                        op=ALU.mult), rd=[pb_, mk], wr=[pm_])
                    for uu in range(4):
                        u = 4 * ub + uu
                        for j in range(2):
                            o_ = uu * 256 + j * 128
                            mm(k, a, a[:, uu * 128:(uu + 1) * 128], Vg[i][:, c * nc_ + u + j, :], pm_[:, o_:o_ + 128],
                               j == 0, j == 1, [Vg[i], pm_])
                    q0 = 512 * ub * d + c
                    dst = tot[:, q0:q0 + 511 * d + 1:d]
                    if g == 0:
                        k.op('act', lambda e: e.copy(out=dst, in_=a[:, :]), rd=[a], wr=[tot])
                    else:
                        k.op('dve', lambda e: e.tensor_tensor(out=dst, in0=a[:, :], in1=dst, op=ALU.add), rd=[a, tot], wr=[tot])
                    un += 1
            if g == 2:
                b0 = 64 * (h % 2)
                for r in range(4):
                    rs = slice(r * 2048, (r + 1) * 2048)
                    o = ost[no % 2]
                    no += 1
                    k.op('dve', lambda e: e.reciprocal(out=rr[:, :], in_=tot[64:128, rs]), rd=[tot], wr=[rr])
                    k.op('dve', lambda e: e.tensor_tensor(out=o[:, :], in0=tot[0:64, rs], in1=rr[:, :], op=ALU.mult),
                         rd=[tot, rr], wr=[o])
                    k.dma('sp', OB[h // 2][b0:b0 + 64, rs], o[:, :], o, rd=[o])
        k.phase_end()

    def ph_out(l):
        k.phase_begin()
        onesf, epsb = consts()
        onesD = ones_tile("onesD", 1.0 / D)
        Woa = k.sb("Woa", (128, 4, D), BF16)
        Wob = k.sb("Wob", (128, 2, D), BF16)
        Woc = k.sb("Woc", (128, 4, D), BF16)
        Wo = k.sb("Wo", (128, 8, D), BF16)
        wload(Woa, 0, w_o_diff[l], D)
        wload(Wob, 0, w_o_dil[l], D)
        wload(Woc, 0, w_o_mla[l], D)
        wload(Wo, 0, w_out[l], D)
        gm = gvec("gm", g_mlp[l], 8)
        oab = [k.sb("oab%d" % i, (128, 4, TB), BF16) for i in range(2)]
        obb = [k.sb("obb%d" % i, (128, 2, TB), BF16) for i in range(2)]
        ocb = [k.sb("ocb%d" % i, (128, 4, TB), BF16) for i in range(2)]
        gb = [k.sb("gb%d" % i, (128, 24, TB), BF16) for i in range(2)]
        xb = [k.sb("xb%d" % i, (128, 8, TB), F32) for i in range(2)]
        mg = k.sb("mg", (128, 8, TB), BF16)
        m1 = [k.sb("m1_%d" % i, (128, TB), F32) for i in range(2)]
        m2 = [k.sb("m2_%d" % i, (128, TB), F32) for i in range(2)]
        m3 = [k.sb("m3_%d" % i, (128, TB), F32) for i in range(2)]
        sq = k.sb("sq", (128, 8, TB), F32)
        rstd = k.sb("rstd", (128, TB), F32)
        tmp = k.sb("tmp", (128, TB), F32)
        h2b = [k.sb("h2b%d" % i, (128, 8, TB), BF16) for i in range(2)]
        pa = [k.ps("pa%d" % i) for i in range(2)]
        pb = [k.ps("pb%d" % i) for i in range(2)]
        pc = [k.ps("pc%d" % i) for i in range(2)]
        msps = k.ps("msps")
        OAv = OA.rearrange("c p s -> p c s")
        OBv = OB.rearrange("c p s -> p c s")
        OCv = OC.rearrange("c p s -> p c s")
        Gv = G.rearrange("c p s -> p c s")
        H2v = H2.rearrange("c p s -> p c s")

        def load_blk(b):
            sl = slice(b * TB, (b + 1) * TB)
            i = b % 2
            k.dma('sp', oab[i][:, :, :], OAv[:, :, sl], oab[i], wr=[oab[i]])
            k.dma('sp', obb[i][:, :, :], OBv[:, :, sl], obb[i], wr=[obb[i]])
            k.dma('sp', ocb[i][:, :, :], OCv[:, :, sl], ocb[i], wr=[ocb[i]])
            k.dma('sp', gb[i][:, :, :], Gv[:, :, sl], gb[i], wr=[gb[i]])
            k.dma('sp', xb[i][:, :, :], xTv[:, :, sl], xb[i], wr=[xb[i]])
        load_blk(0)
        n = 0
        for b in range(NTB):
            if b + 1 < NTB:
                load_blk(b + 1)
            i = b % 2
            sl = slice(b * TB, (b + 1) * TB)
            for oc in range(8):
                cs = slice(oc * 128, (oc + 1) * 128)
                a, bb, c = pa[n % 2], pb[n % 2], pc[n % 2]
                for kc in range(4):
                    mm(k, a, a[:, :], Woa[:, kc, cs], oab[i][:, kc, :], kc == 0, kc == 3, [Woa, oab[i]])
                for kc in range(2):
                    mm(k, bb, bb[:, :], Wob[:, kc, cs], obb[i][:, kc, :], kc == 0, kc == 1, [Wob, obb[i]])
                for kc in range(4):
                    mm(k, c, c[:, :], Woc[:, kc, cs], ocb[i][:, kc, :], kc == 0, kc == 3, [Woc, ocb[i]])
                x1, x2, x3 = m1[n % 2], m2[n % 2], m3[n % 2]
                k.op('dve', lambda e: e.tensor_tensor(out=x1[:, :], in0=a[:, :], in1=gb[i][:, oc, :], op=ALU.mult), rd=[a, gb[i]], wr=[x1])
                k.op('dve', lambda e: e.tensor_tensor(out=x2[:, :], in0=bb[:, :], in1=gb[i][:, 8 + oc, :], op=ALU.mult), rd=[bb, gb[i]], wr=[x2])
                k.op('dve', lambda e: e.tensor_tensor(out=x3[:, :], in0=c[:, :], in1=gb[i][:, 16 + oc, :], op=ALU.mult), rd=[c, gb[i]], wr=[x3])
                k.op('pool', lambda e: e.tensor_tensor(out=x1[:, :], in0=x1[:, :], in1=x2[:, :], op=ALU.add), rd=[x1, x2], wr=[x1])
                k.op('pool', lambda e: e.tensor_tensor(out=mg[:, oc, :], in0=x1[:, :], in1=x3[:, :], op=ALU.add), rd=[x1, x3], wr=[mg])
                n += 1
            for oc in range(8):
                cs = slice(oc * 128, (oc + 1) * 128)
                a = pa[n % 2]
                n += 1
                for kc in range(8):
                    mm(k, a, a[:, :], Wo[:, kc, cs], mg[:, kc, :], kc == 0, kc == 7, [Wo, mg])
                k.op('dve', lambda e: e.tensor_tensor(out=xb[i][:, oc, :], in0=a[:, :], in1=xb[i][:, oc, :], op=ALU.add),
                     rd=[a, xb[i]], wr=[xb[i]])
            ho = h2b[i]
            norm_block(k, xb[i], 8, gm, 0, onesD, sq, msps, rstd, tmp, lambda c: (ho, ho[:, c, :]), epsb)
            k.dma('sp', xTv[:, :, sl], xb[i][:, :, :], xb[i], rd=[xb[i]])
            k.dma('sp', H2v[:, :, sl], ho[:, :, :], ho, rd=[ho])
        k.phase_end()

    def ph_up(l):
        k.phase_begin()
        Wu = k.sb("Wu", (128, 8, 4 * D), BF16)
        wload(Wu, 0, w_up[l], 4 * D)
        hb = [k.sb("hb%d" % i, (128, 8, TB), BF16) for i in range(2)]
        ust = [k.sb("ust%d" % i, (128, 8, TB), BF16) for i in range(2)]
        sqv = [k.sb("sqv%d" % i, (128, TB), F32) for i in range(2)]
        ps = [k.ps("ps%d" % i) for i in range(4)]
        H2v = H2.rearrange("c p s -> p c s")
        Uv = U.rearrange("c p s -> p c s")

        def load_blk(b):
            sl = slice(b * TB, (b + 1) * TB)
            k.dma('sp', hb[b % 2][:, :, :], H2v[:, :, sl], hb[b % 2], wr=[hb[b % 2]])
        load_blk(0)
        n = 0
        for b in range(NTB):
            if b + 1 < NTB:
                load_blk(b + 1)
            h = hb[b % 2]
            sl = slice(b * TB, (b + 1) * TB)
            for uc in range(32):
                p = ps[n % 4]
                sv = sqv[n % 2]
                n += 1
                us = ust[(uc // 8) % 2]
                for kc in range(8):
                    mm(k, p, p[:, :], Wu[:, kc, uc * 128:(uc + 1) * 128], h[:, kc, :], kc == 0, kc == 7, [Wu, h])
                k.op('act', lambda e: e.activation(out=sv[:, :], in_=p[:, :], func=AF.Square), rd=[p], wr=[sv])
                k.op('dve', lambda e: e.scalar_tensor_tensor(out=us[:, uc % 8, :], in0=p[:, :], scalar=0.0, in1=sv[:, :],
                                                             op0=ALU.is_gt, op1=ALU.mult), rd=[p, sv], wr=[us])
                if uc % 8 == 7:
                    k.dma('sp', Uv[:, uc - 7:uc + 1, sl], us[:, :, :], us, rd=[us])
        k.phase_end()

    def ph_down(l):
        last = (l == DEPTH - 1)
        k.phase_begin()
        onesf, epsb = consts()
        onesD = ones_tile("onesD", 1.0 / D)
        Wd = k.sb("Wd", (128, 32, D), BF16)
        wload(Wd, 0, w_down[l], D)
        gn = gvec("gn", g_final if last else g_mix[l + 1], 8)
        ub = [k.sb("ub%d" % i, (128, 32, TB), BF16) for i in range(2)]
        xb = [k.sb("xb%d" % i, (128, 8, TB), F32) for i in range(2)]
        sq = k.sb("sq", (128, 8, TB), F32)
        rstd = k.sb("rstd", (128, TB), F32)
        tmp = k.sb("tmp", (128, TB), F32)
        ps = [k.ps("ps%d" % i) for i in range(3)]
        msps = k.ps("msps")
        Uv = U.rearrange("c p s -> p c s")
        if last:
            ident = k.sb("ident", (128, 128), F32)
            k.op('pool', lambda e: e.affine_select(out=ident[:, :], in_=onesf[:, :], pattern=[[-1, 128]],
                                                   compare_op=ALU.is_equal, fill=0.0, base=0, channel_multiplier=1),
                 rd=[onesf], wr=[ident])
            yb = k.sb("yb", (128, 8, TB), F32)
            yo = [k.sb("yo%d" % i, (128, D), F32) for i in range(2)]
            tp = [k.ps("tp%d" % i) for i in range(2)]
            outv = out.rearrange("(b j p) d -> b j p d", p=128, j=4)
        else:
            hTo = [k.sb("hTo%d" % i, (128, 8, TB), BF16) for i in range(2)]

        def load_blk(b):
            sl = slice(b * TB, (b + 1) * TB)
            i = b % 2
            k.dma('sp', ub[i][:, :, :], Uv[:, :, sl], ub[i], wr=[ub[i]])
            k.dma('sp', xb[i][:, :, :], xTv[:, :, sl], xb[i], wr=[xb[i]])
        load_blk(0)
        n = 0
        ny = 0
        for b in range(NTB):
            if b + 1 < NTB:
                load_blk(b + 1)
            i = b % 2
            sl = slice(b * TB, (b + 1) * TB)
            for oc in range(8):
                p = ps[n % 3]
                n += 1
                for kc in range(32):
                    mm(k, p, p[:, :], Wd[:, kc, oc * 128:(oc + 1) * 128], ub[i][:, kc, :], kc == 0, kc == 31, [Wd, ub[i]])
                k.op('dve', lambda e: e.tensor_tensor(out=xb[i][:, oc, :], in0=p[:, :], in1=xb[i][:, oc, :], op=ALU.add),
                     rd=[p, xb[i]], wr=[xb[i]])
            if not last:
                ho = hTo[i]
                norm_block(k, xb[i], 8, gn, 0, onesD, sq, msps, rstd, tmp, lambda c: (ho, ho[:, c, :]), epsb)
                k.dma('sp', xTv[:, :, sl], xb[i][:, :, :], xb[i], rd=[xb[i]])
                k.dma('sp', hTv[:, :, sl], ho[:, :, :], ho, rd=[ho])
            else:
                norm_block(k, xb[i], 8, gn, 0, onesD, sq, msps, rstd, tmp, lambda c: (yb, yb[:, c, :]), epsb)
                for j in range(4):
                    y = yo[ny % 2]
                    ny += 1
                    for half in range(2):
                        t = tp[half]
                        for q in range(4):
                            fc = half * 4 + q
                            k.op('pe', lambda e: e.transpose(t[:, q * 128:(q + 1) * 128], yb[:, fc, j * 128:(j + 1) * 128],
                                                             ident[:, :]), rd=[yb, ident], wr=[t], sig=(q == 3))
                        if half == 0:
                            k.op('act', lambda e: e.copy(out=y[:, 0:512], in_=t[:, :]), rd=[t], wr=[y])
                        else:
                            k.op('dve', lambda e: e.tensor_copy(out=y[:, 512:1024], in_=t[:, :]), rd=[t], wr=[y])
                    k.dma('sp', outv[b][j], y[:, :], y, rd=[y])
        k.phase_end()

    def run_all():
        ph_first()
        if done('first'):
            return
        for l in layers:
            ph_rope(l)
            if done('rope%d' % l):
                return
            ph_vcg(l)
            if done('vcg%d' % l):
                return
            ph_mla_up(l)
            if done('mlaup%d' % l):
                return
            ph_diff(l)
            if done('diff%d' % l):
                return
            ph_mla(l)
            if done('mla%d' % l):
                return
            ph_dil(l)
            if done('dil%d' % l):
                return
            ph_out(l)
            if done('out%d' % l):
                return
            ph_up(l)
            if done('up%d' % l):
                return
            ph_down(l)
            if done('down%d' % l):
                return
    run_all()
    es.close()
    return nc


def host_tables():
    pos = np.arange(S, dtype=np.float32)

    def tab(half):
        inv = (np.float32(10000.0) ** (-np.arange(half, dtype=np.float32) / np.float32(half))).astype(np.float32)
        ang = (pos[None, :] * inv[:, None]).astype(np.float32)
        return np.cos(ang).astype(np.float32), np.sin(ang).astype(np.float32)
    c32, s32 = tab(32)
    c16, s16 = tab(16)
    rc64 = np.concatenate([c32, c32, c32, c32], 0)
    rs64 = np.concatenate([-s32, s32, -s32, s32], 0)
    rcq = np.concatenate([np.ones((64, S), np.float32), c16, c16], 0)
    rsq = np.concatenate([np.zeros((64, S), np.float32), -s16, s16], 0)
    return dict(rc64=np.ascontiguousarray(rc64), rs64=np.ascontiguousarray(rs64),
                rcq=np.ascontiguousarray(rcq), rsq=np.ascontiguousarray(rsq))


def kernel(**inputs):
    nc = bass.Bass("TRN2", target_bir_lowering=False)
    build(nc)
    shared = {k: np.ascontiguousarray(np.asarray(v, dtype=np.float32)) for k, v in inputs.items() if k != 'x'}
    shared.update(host_tables())
    x = np.asarray(inputs['x'], dtype=np.float32)
    in_maps = [dict(shared, x=np.ascontiguousarray(x[c])) for c in range(8)]
    res = run_bass_kernel_spmd(nc, in_maps, core_ids=list(range(8)))
    return np.stack([np.asarray(r['out'], dtype=np.float32) for r in res.results], 0)
```
